# Optimizing a Trainium2 kernel written in Bass

```python
import math
import jax, jax.numpy as jnp
from jax import lax
import numpy as np

D_MODEL = 2048
BATCH = 32
SEQ = 256
DEPTH = 1
DEC_BATCH = 8
DEC_SEQ = 1024
PAST_LEN = 512

GRID_W = 64
D_MIX = D_MODEL
D_CHUNK = D_MIX // 2
D_SSM = D_MIX - D_CHUNK
CHUNK = 128
CHUNK_HEADS = 8
CHUNK_HEAD_DIM = D_CHUNK // CHUNK_HEADS
SSM_GROUP = 16
N_SSM_GROUPS = D_SSM // SSM_GROUP
SSM_STATE = 64
D_IN = 2 * D_CHUNK + D_SSM
D_FF = 4 * D_MODEL
N_MOD = 6
EPS = 1e-6
DT_MIN = 1e-3
DT_MAX = 1e-1

kernel_name = "hybrid_chunkmlp_s5_prefix_diffusion_step"


def rmsnorm(x, g):
    xf = x.astype(jnp.float32)
    y = xf * lax.rsqrt(jnp.mean(xf * xf, axis=-1, keepdims=True) + EPS)
    return (y * g.astype(jnp.float32)).astype(x.dtype)


def grid_pos_embed(n_tok, dtype):
    rows = n_tok // GRID_W
    quarter = D_MODEL // 4
    freqs = 1.0 / (10000.0 ** (jnp.arange(quarter, dtype=jnp.float32) / quarter))
    r_ang = jnp.arange(rows, dtype=jnp.float32)[:, None] * freqs[None, :]
    c_ang = jnp.arange(GRID_W, dtype=jnp.float32)[:, None] * freqs[None, :]
    r_emb = jnp.concatenate([jnp.sin(r_ang), jnp.cos(r_ang)], axis=-1)
    c_emb = jnp.concatenate([jnp.sin(c_ang), jnp.cos(c_ang)], axis=-1)
    pos = jnp.concatenate([
        jnp.broadcast_to(r_emb[:, None, :], (rows, GRID_W, D_MODEL // 2)),
        jnp.broadcast_to(c_emb[None, :, :], (rows, GRID_W, D_MODEL // 2))], axis=-1)
    return pos.reshape(rows * GRID_W, D_MODEL).astype(dtype)


def modulation(cvec, w_ada, b_ada):
    return jax.nn.silu(cvec) @ w_ada + b_ada


def chunk_mlp_mixer(u, v, w_s, b_s, g_v):
    bsz, seq_len, _ = u.shape
    n_chunks = seq_len // CHUNK
    vh = rmsnorm(v, g_v).reshape(bsz, n_chunks, CHUNK, CHUNK_HEADS, CHUNK_HEAD_DIM)
    z = jnp.einsum("hpq,bnqhd->bnphd", w_s, vh) + b_s.T[None, None, :, :, None]
    return u * z.reshape(bsz, seq_len, D_CHUNK)


def ssm_discretize(lam_re, lam_im, log_dt, b_re, b_im):
    lam = lax.complex(lam_re.astype(jnp.float32), lam_im.astype(jnp.float32))
    dt = jnp.exp(log_dt.astype(jnp.float32))[:, None]
    lam_bar = jnp.exp(lam * dt)
    b = lax.complex(b_re.astype(jnp.float32), b_im.astype(jnp.float32))
    b_bar = ((lam_bar - 1.0) / lam)[..., None] * b
    return lam_bar, b_bar


def _scan_combine(e1, e2):
    a1, b1 = e1
    a2, b2 = e2
    return a1 * a2, a2 * b1 + b2


def ssm_scan(u, lam_bar, b_bar, h0, reverse):
    bu = jnp.einsum("gph,blgh->blgp", b_bar, u)
    first = -1 if reverse else 0
    bu = bu.at[:, first].add(lam_bar * h0)
    a = jnp.broadcast_to(lam_bar, bu.shape)
    _, h = lax.associative_scan(_scan_combine, (a, bu), axis=1, reverse=reverse)
    return h


def s5_mixer(x, lam_re, lam_im, log_dt, b_re, b_im, c_re, c_im, d_skip, w_glu, b_glu, h0_f, h0_b):
    bsz, seq_len, _ = x.shape
    xf = x.astype(jnp.float32)
    u = xf.reshape(bsz, seq_len, N_SSM_GROUPS, SSM_GROUP).astype(jnp.complex64)
    lam_f, bb_f = ssm_discretize(lam_re[0], lam_im[0], log_dt[0], b_re[0], b_im[0])
    lam_b, bb_b = ssm_discretize(lam_re[1], lam_im[1], log_dt[1], b_re[1], b_im[1])
    h_f = ssm_scan(u, lam_f, bb_f, h0_f, False)
    h_b = ssm_scan(u, lam_b, bb_b, h0_b, True)
    c_f = lax.complex(c_re[0].astype(jnp.float32), c_im[0].astype(jnp.float32))
    c_b = lax.complex(c_re[1].astype(jnp.float32), c_im[1].astype(jnp.float32))
    y = (jnp.einsum("ghp,blgp->blgh", c_f, h_f) + jnp.einsum("ghp,blgp->blgh", c_b, h_b)).real
    y = y.reshape(bsz, seq_len, D_SSM) + d_skip.astype(jnp.float32) * xf
    y = jax.nn.gelu(y)
    y = y * jax.nn.sigmoid(y @ w_glu.astype(jnp.float32) + b_glu.astype(jnp.float32))
    return y.astype(x.dtype), h_f[:, -1], h_b[:, 0]


def trunk_layer(x, mod, h0_f, h0_b, p):
    shift_m, scale_m, gate_m, shift_f, scale_f, gate_f = jnp.split(mod, N_MOD, axis=-1)
    h = rmsnorm(x, p["g_pre_mix"]) * (1.0 + scale_m) + shift_m
    proj = h @ p["w_in"]
    u_a, v_a, u_b = jnp.split(proj, [D_CHUNK, 2 * D_CHUNK], axis=-1)
    y_a = chunk_mlp_mixer(u_a, v_a, p["chunk_w_s"], p["chunk_b_s"], p["chunk_g_v"])
    y_b, hf, hb = s5_mixer(u_b, p["ssm_lam_re"], p["ssm_lam_im"], p["ssm_log_dt"],
                           p["ssm_b_re"], p["ssm_b_im"], p["ssm_c_re"], p["ssm_c_im"],
                           p["ssm_d"], p["w_glu"], p["b_glu"], h0_f, h0_b)
    mix = jnp.concatenate([rmsnorm(y_a, p["g_out_a"]), rmsnorm(y_b, p["g_out_b"])], axis=-1) @ p["w_out"]
    x = x + gate_m * rmsnorm(mix, p["g_post_mix"])
    h = rmsnorm(x, p["g_pre_ffn"]) * (1.0 + scale_f) + shift_f
    f = jnp.square(jax.nn.relu(h @ p["w_ff1"])) @ p["w_ff2"]
    x = x + gate_f * rmsnorm(f, p["g_post_ffn"])
    return x, hf, hb


def setup_inputs(seed: int = 0) -> dict:
    key = jax.random.key(seed)
    ks = jax.random.split(key, 32)
    f32 = jnp.float32
    nrm = lambda k, shape, s: jax.random.normal(k, shape, f32) * s
    gain = lambda k, shape: 1.0 + 0.02 * jax.random.normal(k, shape, f32)
    lam_im_base = math.pi * jnp.arange(SSM_STATE, dtype=f32)
    return {
        "x_prompt": nrm(ks[0], (BATCH, SEQ, D_MODEL), 1.0),
        "x_sample": nrm(ks[1], (DEC_BATCH, DEC_SEQ, D_MODEL), 1.0),
        "state_ssm": nrm(ks[2], (DEC_BATCH, DEPTH, 2, 2, N_SSM_GROUPS, SSM_STATE), 0.3),
        "c": nrm(ks[3], (DEC_BATCH, D_MODEL), 1.0),
        "c_ctx": nrm(ks[4], (D_MODEL,), 1.0),
        "w_ada": nrm(ks[5], (DEPTH, D_MODEL, N_MOD * D_MODEL), 0.5 * D_MODEL ** -0.5),
        "b_ada": nrm(ks[6], (DEPTH, N_MOD * D_MODEL), 0.02),
        "g_pre_mix": gain(ks[7], (DEPTH, D_MODEL)),
        "w_in": nrm(ks[8], (DEPTH, D_MODEL, D_IN), D_MODEL ** -0.5),
        "chunk_w_s": nrm(ks[9], (DEPTH, CHUNK_HEADS, CHUNK, CHUNK), CHUNK ** -0.5),
        "chunk_b_s": gain(ks[10], (DEPTH, CHUNK_HEADS, CHUNK)),
        "chunk_g_v": gain(ks[11], (DEPTH, D_CHUNK)),
        "ssm_lam_re": -0.5 + 0.01 * jax.random.normal(ks[12], (DEPTH, 2, N_SSM_GROUPS, SSM_STATE), f32),
        "ssm_lam_im": lam_im_base + 0.01 * jax.random.normal(ks[13], (DEPTH, 2, N_SSM_GROUPS, SSM_STATE), f32),
        "ssm_log_dt": jax.random.uniform(ks[14], (DEPTH, 2, N_SSM_GROUPS), f32, math.log(DT_MIN), math.log(DT_MAX)),
        "ssm_b_re": nrm(ks[15], (DEPTH, 2, N_SSM_GROUPS, SSM_STATE, SSM_GROUP), (2 * SSM_GROUP) ** -0.5),
        "ssm_b_im": nrm(ks[16], (DEPTH, 2, N_SSM_GROUPS, SSM_STATE, SSM_GROUP), (2 * SSM_GROUP) ** -0.5),
        "ssm_c_re": nrm(ks[17], (DEPTH, 2, N_SSM_GROUPS, SSM_GROUP, SSM_STATE), 2.0 ** -0.5),
        "ssm_c_im": nrm(ks[18], (DEPTH, 2, N_SSM_GROUPS, SSM_GROUP, SSM_STATE), 2.0 ** -0.5),
        "ssm_d": nrm(ks[19], (DEPTH, D_SSM), 0.5),
        "w_glu": nrm(ks[20], (DEPTH, D_SSM, D_SSM), D_SSM ** -0.5),
        "b_glu": nrm(ks[21], (DEPTH, D_SSM), 0.02),
        "g_out_a": gain(ks[22], (DEPTH, D_CHUNK)),
        "g_out_b": gain(ks[23], (DEPTH, D_SSM)),
        "w_out": nrm(ks[24], (DEPTH, D_MIX, D_MODEL), D_MIX ** -0.5),
        "g_post_mix": gain(ks[25], (DEPTH, D_MODEL)),
        "g_pre_ffn": gain(ks[26], (DEPTH, D_MODEL)),
        "w_ff1": nrm(ks[27], (DEPTH, D_MODEL, D_FF), D_MODEL ** -0.5),
        "w_ff2": nrm(ks[28], (DEPTH, D_FF, D_MODEL), D_FF ** -0.5),
        "g_post_ffn": gain(ks[29], (DEPTH, D_MODEL)),
    }


def reference(x_prompt, x_sample, state_ssm, c, c_ctx, w_ada, b_ada, g_pre_mix, w_in,
              chunk_w_s, chunk_b_s, chunk_g_v, ssm_lam_re, ssm_lam_im, ssm_log_dt,
              ssm_b_re, ssm_b_im, ssm_c_re, ssm_c_im, ssm_d, w_glu, b_glu, g_out_a, g_out_b,
              w_out, g_post_mix, g_pre_ffn, w_ff1, w_ff2, g_post_ffn):
    def layer_params(l):
        return {
            "g_pre_mix": g_pre_mix[l], "w_in": w_in[l],
            "chunk_w_s": chunk_w_s[l], "chunk_b_s": chunk_b_s[l], "chunk_g_v": chunk_g_v[l],
            "ssm_lam_re": ssm_lam_re[l], "ssm_lam_im": ssm_lam_im[l], "ssm_log_dt": ssm_log_dt[l],
            "ssm_b_re": ssm_b_re[l], "ssm_b_im": ssm_b_im[l], "ssm_c_re": ssm_c_re[l], "ssm_c_im": ssm_c_im[l],
            "ssm_d": ssm_d[l], "w_glu": w_glu[l], "b_glu": b_glu[l],
            "g_out_a": g_out_a[l], "g_out_b": g_out_b[l], "w_out": w_out[l],
            "g_post_mix": g_post_mix[l], "g_pre_ffn": g_pre_ffn[l],
            "w_ff1": w_ff1[l], "w_ff2": w_ff2[l], "g_post_ffn": g_post_ffn[l],
        }

    n_ctx_req = x_prompt.shape[0]
    h_zero = jnp.zeros((n_ctx_req, N_SSM_GROUPS, SSM_STATE), jnp.complex64)
    x = x_prompt
    ctx_states = []
    for l in range(DEPTH):
        mod_ctx = modulation(c_ctx, w_ada[l], b_ada[l])[None, None, :]
        x, hf, hb = trunk_layer(x, mod_ctx, h_zero, h_zero, layer_params(l))
        st_f = jnp.stack([hf.real, hf.imag], axis=1)
        st_b = jnp.stack([hb.real, hb.imag], axis=1)
        ctx_states.append(jnp.stack([st_f, st_b], axis=1))
    y_prompt = x
    new_state_ssm = jnp.stack(ctx_states, axis=1)

    st = state_ssm.astype(jnp.float32)
    x = x_sample + grid_pos_embed(x_sample.shape[1], x_sample.dtype)
    for l in range(DEPTH):
        h0_f = lax.complex(st[:, l, 0, 0], st[:, l, 0, 1])
        h0_b = lax.complex(st[:, l, 1, 0], st[:, l, 1, 1])
        mod_lat = modulation(c, w_ada[l], b_ada[l])[:, None, :]
        x, _, _ = trunk_layer(x, mod_lat, h0_f, h0_b, layer_params(l))
    y_sample = x

    return (y_prompt, y_sample, new_state_ssm)
```

```python
import contextlib
import math
import os
_SKIP = os.environ.get('KSKIP', '').split(',')
import numpy as np
import concourse.bass as bass
import concourse.mybir as mybir
from concourse.bass_utils import run_bass_kernel_spmd

F32 = mybir.dt.float32
BF16 = mybir.dt.bfloat16
I32 = mybir.dt.int32
AF = mybir.ActivationFunctionType
ALU = mybir.AluOpType
AX = mybir.AxisListType

EPS = 1e-6
NCORES = 8
TC = 8
NCH = 128
PADL = 64
TWO_PI = 2.0 * math.pi


class Prog:
    ENG = ("pe", "act", "dve", "pool", "sp")

    def __init__(self, nc):
        self.nc = nc
        self.ops = []
        self.last_w = {}
        self.readers = {}
        self.slot_last = {}
        self.last_on_eng = {}
        self.pending_barrier = {}

    def op(self, eng, fn, reads=(), writes=(), dma=False, slot=None, extra_deps=()):
        i = len(self.ops)
        deps = set(extra_deps)
        for k in reads:
            j = self.last_w.get(k)
            if j is not None:
                deps.add(j)
        for k in writes:
            j = self.last_w.get(k)
            if j is not None:
                deps.add(j)
            for r in self.readers.get(k, ()):
                deps.add(r)
        if dma:
            assert slot is not None
            j = self.slot_last.get(slot)
            if j is not None:
                deps.add(j)
            self.slot_last[slot] = i
        if eng in self.pending_barrier:
            deps |= self.pending_barrier.pop(eng)
        deps.discard(i)
        if eng == "pe":
            deps = {j for j in deps if not (self.ops[j]["eng"] == "pe" and not self.ops[j]["dma"])}
        best = {}
        keep = set()
        for j in deps:
            oj = self.ops[j]
            if oj["dma"]:
                keep.add(j)
            else:
                best[oj["eng"]] = max(best.get(oj["eng"], -1), j)
        deps = keep | set(best.values())
        self.ops.append(dict(eng=eng, fn=fn, deps=deps, dma=dma, slot=slot))
        for k in writes:
            self.last_w[k] = i
            self.readers[k] = []
        for k in reads:
            if k not in writes:
                lst = self.readers.setdefault(k, [])
                if not dma:
                    lst[:] = [r for r in lst if self.ops[r]["dma"] or self.ops[r]["eng"] != eng]
                lst.append(i)
        self.last_on_eng[eng] = i
        return i

    def barrier(self):
        deps = set(self.last_on_eng.values()) | set(self.slot_last.values())
        for e in self.ENG:
            self.pending_barrier[e] = set(deps) | self.pending_barrier.get(e, set())

    def emit(self, final_wait_eng="sp"):
        nc = self.nc
        ops = self.ops
        n = len(ops)
        needed = [False] * n
        for o in ops:
            for j in o["deps"]:
                needed[j] = True
        for i, o in enumerate(ops):
            if o["dma"]:
                needed[i] = True
        slots = sorted({o["slot"] for o in ops if o["dma"]}, key=str)
        LIMIT = 1000
        with contextlib.ExitStack() as st:
            tok = [None] * n
            ecount = {e: 0 for e in self.ENG}
            scount = {s: 0 for s in slots}
            for i, o in enumerate(ops):
                if o["dma"]:
                    scount[o["slot"]] += 16
                    tok[i] = (("d", o["slot"]), scount[o["slot"]])
                elif needed[i]:
                    c = ecount[o["eng"]]
                    ecount[o["eng"]] += 1
                    tok[i] = (("e", o["eng"]), (c // LIMIT) * 1000000 + (c % LIMIT) + 1)
            self.signal_counts = dict(ecount)
            esem = {}
            for e in self.ENG:
                for ep in range(ecount[e] // LIMIT + 1):
                    esem[(e, ep)] = st.enter_context(nc.semaphore("s_%s_%d" % (e, ep)))
            ssem = {s: st.enter_context(nc.semaphore("d_%d" % k)) for k, s in enumerate(slots)}

            def sem_and_val(key, val):
                if key[0] == "e":
                    return esem[(key[1], val // 1000000)], val % 1000000
                return ssem[key[1]], val

            block = st.enter_context(nc.Block())
            handles = {"pe": "tensor", "act": "scalar", "dve": "vector", "pool": "gpsimd", "sp": "sync"}

            def run_engine(ename, eng):
                waited = {}
                for i, o in enumerate(ops):
                    if o["eng"] != ename:
                        continue
                    for j in sorted(o["deps"]):
                        if tok[j] is None:
                            continue
                        key, val = tok[j]
                        if key == ("e", ename) and ename == "pe":
                            continue
                        if waited.get(key, 0) < val:
                            sm, vv = sem_and_val(key, val)
                            eng.wait_ge(sm, vv)
                            waited[key] = val
                    inst = o["fn"](eng)
                    if tok[i] is not None:
                        key, val = tok[i]
                        assert inst is not None
                        sm, vv = sem_and_val(key, val)
                        inst.then_inc(sm, 16 if o["dma"] else 1)
                if ename == final_wait_eng:
                    for s_ in slots:
                        if waited.get(("d", s_), 0) < scount[s_]:
                            eng.wait_ge(ssem[s_], scount[s_])

            for ename in self.ENG:
                dec = getattr(block, handles[ename])

                def _mk(en):
                    def body(eng):
                        run_engine(en, eng)
                    return body
                dec(_mk(ename))
        return nc


def _ap(t, offset, dims):
    return bass.AP(tensor=t, offset=offset, ap=[list(d) for d in dims])


def build(debug=None):
    nc = bass.Bass("TRN2", target_bir_lowering=False)

    def din(name, shape, dt=F32):
        return nc.dram_tensor(name, list(shape), dt, kind="ExternalInput").ap()

    def dout(name, shape, dt=F32):
        return nc.dram_tensor(name, list(shape), dt, kind="ExternalOutput").ap()

    x_in = [din("xs", [1024, 2048]), din("xp", [1024, 2048])]
    st0 = din("st0", [2, 2, 64, 64])
    c2 = din("c2", [2, 2048])
    w_ada = din("w_ada", [2048, 12288])
    b_ada = din("b_ada", [12288])
    g_pre_mix = din("g_pre_mix", [2048])
    w_in = din("w_in", [2048, 3072])
    chunk_w_s = din("chunk_w_s", [8, 128, 128])
    chunk_b_s = din("chunk_b_s", [8, 128])
    chunk_g_v = din("chunk_g_v", [1024])
    lam_re = din("ssm_lam_re", [2, 64, 64])
    lam_im = din("ssm_lam_im", [2, 64, 64])
    log_dt = din("ssm_log_dt", [2, 64])
    b_re = din("ssm_b_re", [2, 64, 64, 16])
    b_im = din("ssm_b_im", [2, 64, 64, 16])
    c_re = din("ssm_c_re", [2, 64, 16, 64])
    c_im = din("ssm_c_im", [2, 64, 16, 64])
    ssm_d = din("ssm_d", [1024])
    w_glu = din("w_glu", [1024, 1024])
    b_glu = din("b_glu", [1024])
    g_out_a = din("g_out_a", [1024])
    g_out_b = din("g_out_b", [1024])
    w_out = din("w_out", [2048, 2048])
    g_post_mix = din("g_post_mix", [2048])
    g_pre_ffn = din("g_pre_ffn", [2048])
    w_ff1 = din("w_ff1", [2048, 8192])
    w_ff2 = din("w_ff2", [8192, 2048])
    g_post_ffn = din("g_post_ffn", [2048])
    ident_d = din("k_ident", [128, 128])
    pm_d = din("k_pm", [128, 2])
    pairm_d = din("k_pairm", [128, 4])
    sel_d = din("k_sel", [64, 8, 128])

    y_out = [dout("ys", [1024, 2048]), dout("yp", [1024, 2048])]
    ns_out = dout("ns", [4, 2, 2, 64, 64])

    NSW = 8 * 2 * 2 * 128 + 9 * 2 * 2 * 128 + 15 * 128
    OFF_B, OFF_C, OFF_K = 0, 4096, 4096 + 4608
    swb_d = nc.dram_tensor("swb", [8, 128, NSW], BF16, kind="Internal").ap()
    NMU = 4 * 2 * 7 * 2
    swm_d = nc.dram_tensor("swm", [8, 128, NMU], F32, kind="Internal").ap()

    dbg_outs = {}

    with contextlib.ExitStack() as st:
        def sb(name, shape, dt=F32):
            return st.enter_context(nc.sbuf_tensor(name, list(shape), dt))

        BIG = sb("BIG", [128, 16384], F32)
        HT = sb("HT", [128, 16, 1024], BF16)
        VHR = sb("VHR", [128, 8192], BF16)
        UT = sb("UT", [128, 8, TC, NCH], BF16)
        WB = [sb("WB%d" % i, [128, 8192], BF16) for i in range(3)]
        XT = [sb("XT%d" % i, [128, 2048], F32) for i in range(2)]
        IDF = sb("IDF", [128, 128], F32)
        IDB = sb("IDB", [128, 128], BF16)
        ONESF = sb("ONESF", [128, 128], F32)
        PMK = sb("PMK", [128, 2], F32)
        PAIRM = sb("PAIRM", [128, 4], F32)
        VT1 = sb("VT1", [128, 96], F32)
        VT2 = sb("VT2", [128, 96], F32)
        MODT = sb("MODT", [128, 96, 2], F32)
        SCM = sb("SCM", [128, 16, 2], F32)
        SCF = sb("SCF", [128, 16, 2], F32)
        SCT = sb("SCT", [128, 16, 2], BF16)
        WST = sb("WST", [128, 8, 128], BF16)
        BSP = sb("BSP", [128, 8], F32)
        EPSC = sb("EPSC", [128, 1], F32)
        SMALL = sb("SMALL", [128, 256], F32)
        BNS = sb("BNS", [128, 8, 4, 6], F32)
        PS = [st.enter_context(nc.psum_tensor("PS%d" % i, [128, 512], F32)) for i in range(8)]

        P = Prog(nc)
        cnt = {"big": 0, "hi": 0, "ev": 0}

        def big_bank():
            b = cnt["big"] % 4
            cnt["big"] += 1
            return b

        def hi_bank():
            b = 4 + cnt["hi"] % 4
            cnt["hi"] += 1
            return b

        def ev_eng():
            cnt["ev"] += 1
            return "act" if cnt["ev"] % 2 else "dve"

        def kps(b):
            return [("ps", b)]

        def smcol(i, n=1):
            return SMALL[:, i:i + n]

        wblocks = []

        def add_wblock(src, kind):
            wblocks.append((src, kind))
            return len(wblocks) - 1

        def wview(b, kind):
            t = WB[b]
            if kind == "k16":
                return t[:, :].rearrange("p (k n) -> p k n", k=16)
            if kind == "k8":
                return t[:, 0:4096].rearrange("p (k n) -> p k n", k=8)
            if kind == "c4":
                return t[:, :].rearrange("p (c n) -> p c n", c=4)
            raise ValueError(kind)

        wstate = {"issued": 0}

        def w_issue_upto(i):
            while wstate["issued"] <= min(i, len(wblocks) - 1):
                j = wstate["issued"]
                src, kind = wblocks[j]
                b = j % 3
                P.op("pool", lambda e, b=b, kind=kind, src=src: e.dma_start(out=wview(b, kind), in_=src),
                     writes=[("wb", b)], dma=True, slot=("wb", b))
                wstate["issued"] += 1

        def w_acquire(i):
            w_issue_upto(i + 2)
            return i % 3

        def ada_block(j):
            return add_wblock(w_ada[:, j * 512:(j + 1) * 512].rearrange("(k p) n -> p k n", p=128), "k16")

        wb_ada = [ada_block(j) for j in range(8)]
        wb_unit = []
        for u in range(2):
            d = {}
            d["win"] = [add_wblock(w_in[:, j * 512:(j + 1) * 512].rearrange("(k p) n -> p k n", p=128), "k16")
                        for j in range(6)]
            if u == 0:
                wb_ada += [ada_block(j) for j in range(8, 24)]
            d["glu"] = [add_wblock(w_glu[:, j * 512:(j + 1) * 512].rearrange("(k p) n -> p k n", p=128), "k8")
                        for j in range(2)]
            d["wout"] = [add_wblock(w_out[:, j * 512:(j + 1) * 512].rearrange("(k p) n -> p k n", p=128), "k16")
                         for j in range(4)]
            d["ffn"] = []
            for g in range(16):
                i1 = add_wblock(w_ff1[:, g * 512:(g + 1) * 512].rearrange("(k p) n -> p k n", p=128), "k16")
                i2 = add_wblock(w_ff2[g * 512:(g + 1) * 512, :].rearrange("(c p) n -> p c n", p=128), "c4")
                d["ffn"].append((i1, i2))
            wb_unit.append(d)

        def rstd_from_bn(slots_ap, nslots, out_col, scratch_col):
            mv = smcol(scratch_col, 2)
            P.op("dve", lambda e: e.bn_aggr(out=mv, in_=slots_ap), reads=[("bns",)], writes=[("sm", scratch_col)])
            P.op("dve", lambda e: e.scalar_tensor_tensor(out=smcol(scratch_col + 2), in0=mv[:, 0:1], scalar=mv[:, 0:1],
                                                         in1=mv[:, 1:2], op0=ALU.mult, op1=ALU.add),
                 reads=[("sm", scratch_col)], writes=[("sm", scratch_col + 2)])
            P.op("act", lambda e: e.activation(out=smcol(scratch_col + 3), in_=smcol(scratch_col + 2), func=AF.Sqrt,
                                               bias=EPSC[:, 0:1], scale=1.0),
                 reads=[("sm", scratch_col + 2)], writes=[("sm", scratch_col + 3)])
            P.op("dve", lambda e: e.reciprocal(out=smcol(out_col), in_=smcol(scratch_col + 3)),
                 reads=[("sm", scratch_col + 3)], writes=[("sm", out_col)])

        P.op("sp", lambda e: e.dma_start(out=IDF[:, :], in_=ident_d), writes=[("idf",)], dma=True, slot="c0")
        P.op("sp", lambda e: e.dma_start(out=PMK[:, :], in_=pm_d), writes=[("pmk",)], dma=True, slot="c1")
        P.op("sp", lambda e: e.dma_start(out=PAIRM[:, :], in_=pairm_d), writes=[("pairm",)], dma=True, slot="c2")
        P.op("dve", lambda e: e.tensor_copy(out=IDB[:, :], in_=IDF[:, :]), reads=[("idf",)], writes=[("idb",)])
        P.op("dve", lambda e: e.memset(ONESF[:, :], 1.0), writes=[("onesf",)])
        P.op("dve", lambda e: e.memset(EPSC[:, :], EPS), writes=[("epsc",)])

        VN1 = XT[0][0:96, 0:128]
        VN2 = XT[0][0:96, 128:256]
        vecs = [(c2.rearrange("v (k q) -> (v k) q", q=128), 0, 32),
                (g_pre_mix.rearrange("(k q) -> k q", q=128), 32, 16),
                (g_pre_ffn.rearrange("(k q) -> k q", q=128), 48, 16),
                (g_out_a.rearrange("(k q) -> k q", q=128), 64, 8),
                (g_out_b.rearrange("(k q) -> k q", q=128), 72, 8),
                (b_glu.rearrange("(k q) -> k q", q=128), 80, 8),
                (ssm_d.rearrange("(k q) -> k q", q=128), 88, 8)]
        for vi, (src, r0, nr) in enumerate(vecs):
            P.op("sp", lambda e, src=src, r0=r0, nr=nr: e.dma_start(out=XT[0][r0:r0 + nr, 0:128], in_=src),
                 writes=[("vn1", vi)], dma=True, slot="c%d" % (vi % 3))
        P.op("sp", lambda e: e.dma_start(out=VN2, in_=b_ada.rearrange("(r q) -> r q", q=128)),
             writes=[("vn2",)], dma=True, slot="c0")
        P.op("pe", lambda e: e.transpose(out=PS[7][:, 0:96], in_=VN1, identity=IDF[0:96, 0:96]),
             reads=[("vn1", i) for i in range(7)] + [("idf",)], writes=kps(7))
        P.op("dve", lambda e: e.tensor_copy(out=VT1[:, :], in_=PS[7][:, 0:96]), reads=kps(7), writes=[("vt1",)])
        P.op("pe", lambda e: e.transpose(out=PS[6][:, 0:96], in_=VN2, identity=IDF[0:96, 0:96]),
             reads=[("vn2",), ("idf",)], writes=kps(6))
        P.op("dve", lambda e: e.tensor_copy(out=VT2[:, :], in_=PS[6][:, 0:96]), reads=kps(6), writes=[("vt2",)])
        for v in range(2):
            P.op("act", lambda e, v=v: e.activation(out=SCT[:, :, v], in_=VT1[:, v * 16:(v + 1) * 16], func=AF.Silu),
                 reads=[("vt1",)], writes=[("sct", v)])
        GPM, GPF, GOA, GOB, BGL, DSK = (VT1[:, 32:48], VT1[:, 48:64], VT1[:, 64:72], VT1[:, 72:80],
                                        VT1[:, 80:88], VT1[:, 88:96])
        WSN = XT[1][:, 0:1024].rearrange("p (h q) -> p h q", h=8)
        P.op("sp", lambda e: e.dma_start(out=WSN, in_=chunk_w_s.rearrange("h p q -> p h q")),
             writes=[("wsn",)], dma=True, slot="c1")
        P.op("sp", lambda e: e.dma_start(out=XT[1][0:8, 1024:1152], in_=chunk_b_s),
             writes=[("bsn",)], dma=True, slot="c2")
        for h in range(8):
            b = hi_bank()
            P.op("pe", lambda e, h=h, b=b: e.transpose(out=PS[b][:, 0:128], in_=WSN[:, h, :], identity=IDF[:, :]),
                 reads=[("wsn",), ("idf",)], writes=kps(b))
            P.op("dve", lambda e, h=h, b=b: e.tensor_copy(out=WST[:, h, :], in_=PS[b][:, 0:128]),
                 reads=kps(b), writes=[("wst", h)])
        b = hi_bank()
        P.op("pe", lambda e, b=b: e.transpose(out=PS[b][:, 0:8], in_=XT[1][0:8, 1024:1152], identity=IDF[0:8, 0:8]),
             reads=[("bsn",), ("idf",)], writes=kps(b))
        P.op("dve", lambda e, b=b: e.tensor_copy(out=BSP[:, :], in_=PS[b][:, 0:8]), reads=kps(b), writes=[("bsp",)])

        H0T = sb("H0T", [128, 128], F32)
        BIGH = HT[:, :, :].rearrange("p k n -> p (k n)").bitcast(F32)
        _go = {"o": 0}

        _regions = [BIG[:, :], BIGH, UT[:, :, :, :].rearrange("p g j c -> p (g j c)").bitcast(F32),
                    VHR[:, :].bitcast(F32)]
        _regions += [XT[0][:, 512:2048], XT[1][:, 1152:2048]]
        _rsz = [16384, 8192, 4096, 4096, 1536, 896]

        _roff = [0] * 6

        def galloc(n):
            for r in range(len(_regions)):
                if _roff[r] + n <= _rsz[r]:
                    o = _roff[r]
                    _roff[r] += n
                    return _regions[r][:, o:o + n]
            raise AssertionError("gen scratch exhausted")

        _pw = [galloc(9 * 256) for _ in range(2)]
        G_TBC = galloc(9 * 256)
        G_CMS = [galloc(2304).bitcast(BF16) for _ in range(2)]
        G_CST = galloc(2304).bitcast(BF16)
        G_TBB = galloc(8 * 256)
        G_BST = galloc(2048).bitcast(BF16)
        G_BN = galloc(2048)
        G_CR = galloc(2048)
        G_TMPC = galloc(9 * 128)
        G_SEL = galloc(1024)
        G_TMPB = galloc(8 * 128)
        G_KT = galloc(960).bitcast(BF16)
        G_KS = galloc(512)
        G_MM = galloc(512)
        G_LN = galloc(258)
        G_BXS = [galloc(256).bitcast(BF16) for _ in range(2)]
        GS = []
        for _sx in range(2):
            GS.append(dict(PR=galloc(258), DT=galloc(2), T=[galloc(128) for _ in range(16)], BRP=galloc(256),
                           BB=galloc(256), PW=_pw[_sx]))
        G_K0 = galloc(128)
        G_MU = galloc(112)
        G_MT = galloc(64)

        def v3(ap, d0, d1):
            return ap.rearrange("q (a b) -> q a b", a=d0, b=d1)

        def ssm_param_loads():
            P.op("sp", lambda e: e.dma_start(out=G_SEL[0:64, :], in_=sel_d.rearrange("g t q -> g (t q)")),
                 writes=[("g_sel",)], dma=True, slot="pl_sel")
            for d in range(2):
                P.op("sp", lambda e, d=d: e.dma_start(out=G_LN[0:64, d * 128:d * 128 + 64], in_=lam_re[d]),
                     writes=[("g_ln", d, 0)], dma=True, slot=("pl_ln0", d))
                P.op("sp", lambda e, d=d: e.dma_start(out=G_LN[0:64, d * 128 + 64:d * 128 + 128], in_=lam_im[d]),
                     writes=[("g_ln", d, 1)], dma=True, slot=("pl_ln1", d))
                P.op("sp", lambda e, d=d: e.dma_start(out=G_LN[0:64, 256 + d:257 + d],
                                                     in_=log_dt[d].rearrange("(g o) -> g o", o=1)),
                     writes=[("g_ln", d, 2)], dma=True, slot=("pl_ln2", d))
                for ri, src in enumerate((b_re, b_im)):
                    dst = G_BN[64 * d:64 * d + 64, ri * 1024:(ri + 1) * 1024].rearrange("p (g h) -> p g h", g=64)
                    P.op("sp", lambda e, dst=dst, src=src, d=d: e.dma_start(out=dst, in_=src[d].rearrange("g p h -> p g h")),
                         writes=[("g_bn", d, ri)], dma=True, slot=("pl", d, ri, "1"))
                for ri, src in enumerate((c_re, c_im)):
                    dst = G_CR[:, (d * 2 + ri) * 512:(d * 2 + ri + 1) * 512].rearrange("q (t p) -> q t p", t=8)
                    P.op("sp", lambda e, dst=dst, src=src, d=d: e.dma_start(
                        out=dst, in_=src[d].rearrange("(t g) h p -> (g h) t p", t=8)),
                        writes=[("g_cr", d, ri)], dma=True, slot=("pl", d, ri, "2"))
            for d in range(2):
                sl = G_CR[:, (d * 2 + 1) * 512:(d * 2 + 2) * 512]
                P.op("dve", lambda e, sl=sl: e.tensor_scalar(out=sl, in0=sl, scalar1=-1.0, scalar2=None, op0=ALU.mult),
                     reads=[("g_cr", d, 1)], writes=[("g_cr", d, 1)])
            P.op("sp", lambda e: e.dma_start(out=XT[0][:, 256:384],
                                             in_=st0.rearrange("d r (w s) p -> (d r w) (s p)", s=2)),
                 writes=[("h0n",)], dma=True, slot="c0")
            P.op("pe", lambda e: e.transpose(out=PS[5][:, 0:128], in_=XT[0][:, 256:384], identity=IDF[:, :]),
                 reads=[("h0n",), ("idf",)], writes=kps(5))
            P.op("dve", lambda e: e.tensor_copy(out=H0T[:, :], in_=PS[5][:, 0:128]), reads=kps(5), writes=[("h0t",)])

        gparam_keys = [("g_sel",)] + [("g_ln", d, i) for d in range(2) for i in range(3)]
        gbn_keys = [("g_bn", d, r) for d in range(2) for r in range(2)]
        gcr_keys = [("g_cr", d, r) for d in range(2) for r in range(2)]

        def cmul(eng, o_r, o_i, a_r, a_i, b_r, b_i, t, rk, wk, neg_im=False):
            rk = list(rk) + list(wk)
            wk = list(wk)

            def tt(out, a, b, op):
                P.op(eng, lambda e: e.tensor_tensor(out=out, in0=a, in1=b, op=op), reads=rk, writes=wk)
            if neg_im:
                tt(t, a_i, b_i, ALU.mult)
                tt(o_r, a_r, b_r, ALU.mult)
                tt(o_r, o_r, t, ALU.add)
                tt(t, a_i, b_r, ALU.mult)
                tt(o_i, a_r, b_i, ALU.mult)
                tt(o_i, o_i, t, ALU.subtract)
                return
            tt(t, a_i, b_i, ALU.mult)
            tt(o_r, a_r, b_r, ALU.mult)
            tt(o_r, o_r, t, ALU.subtract)
            tt(t, a_i, b_r, ALU.mult)
            tt(o_i, a_r, b_i, ALU.mult)
            tt(o_i, o_i, t, ALU.add)

        def genA(gt):
            E = "dve"
            sx = gt % 2
            S = GS[sx]
            KG = ("gen", sx)
            G_PR, G_DT, G_BRP, G_BB, G_PW = S["PR"], S["DT"], S["BRP"], S["BB"], S["PW"]
            pb = 4 + (gt % 2) * 2

            G_BX = G_BXS[sx]
            G_CM = G_CMS[sx]
            kcm = ("g_cm", sx)
            PW = G_PW.rearrange("q (n d r p) -> q n d r p", n=9, d=2, r=2)
            PSB = [PS[b_][:, :].bitcast(BF16) for b_ in range(8)]
            def tt(eng, out, a, b, op):
                P.op(eng, lambda e: e.tensor_tensor(out=out, in0=a, in1=b, op=op), reads=[KG], writes=[KG])
            P.op("pe", lambda e: e.matmul(PS[pb][:, 0:258], lhsT=G_SEL[0:64, gt * 128:(gt + 1) * 128], rhs=G_LN[0:64, :],
                                          start=True, stop=True), reads=gparam_keys, writes=kps(pb))
            P.op("dve", lambda e: e.tensor_copy(out=G_PR, in_=PS[pb][:, 0:258]), reads=kps(pb), writes=[KG])
            PRv = G_PR[:, 0:256].rearrange("q (d r p) -> q d r p", d=2, r=2)
            LR, LI = PRv[:, :, 0, :], PRv[:, :, 1, :]
            P.op("act", lambda e: e.activation(out=G_DT, in_=G_PR[:, 256:258], func=AF.Exp), reads=[KG], writes=[KG])
            T = [v3(x, 2, 64) for x in S["T"]]
            A, Bq, MAG, R, SINB, COSB, LBR, LBI, DEN, INV, NR, CRr, CIi, TM, TM2, TM3 = T
            dtb = G_DT.unsqueeze(2).to_broadcast([128, 2, 64])
            tt(E, A, LR, dtb, ALU.mult)
            tt(E, Bq, LI, dtb, ALU.mult)
            P.op("act", lambda e: e.activation(out=MAG, in_=A, func=AF.Exp), reads=[KG], writes=[KG])
            TMi = TM.bitcast(I32)
            P.op(E, lambda e: e.tensor_scalar(out=TMi, in0=Bq, scalar1=1.0 / TWO_PI, scalar2=None, op0=ALU.mult),
                 reads=[KG], writes=[KG])
            P.op(E, lambda e: e.tensor_copy(out=TM2, in_=TMi), reads=[KG], writes=[KG])
            P.op(E, lambda e: e.scalar_tensor_tensor(out=R, in0=TM2, scalar=-TWO_PI, in1=Bq, op0=ALU.mult, op1=ALU.add),
                 reads=[KG], writes=[KG])
            P.op(E, lambda e: e.tensor_scalar(out=TM2, in0=R, scalar1=math.pi, scalar2=-TWO_PI, op0=ALU.is_gt,
                                              op1=ALU.mult), reads=[KG], writes=[KG])
            tt(E, R, R, TM2, ALU.add)
            P.op(E, lambda e: e.tensor_scalar(out=TM2, in0=R, scalar1=-math.pi, scalar2=TWO_PI, op0=ALU.is_lt,
                                              op1=ALU.mult), reads=[KG], writes=[KG])
            tt(E, R, R, TM2, ALU.add)
            P.op("act", lambda e: e.activation(out=SINB, in_=R, func=AF.Sin), reads=[KG], writes=[KG])
            P.op(E, lambda e: e.tensor_scalar(out=TM3, in0=R, scalar1=math.pi / 2, scalar2=None, op0=ALU.add),
                 reads=[KG], writes=[KG])
            P.op(E, lambda e: e.tensor_scalar(out=TM2, in0=TM3, scalar1=math.pi, scalar2=-TWO_PI, op0=ALU.is_gt,
                                              op1=ALU.mult), reads=[KG], writes=[KG])
            tt(E, TM3, TM3, TM2, ALU.add)
            P.op("act", lambda e: e.activation(out=COSB, in_=TM3, func=AF.Sin), reads=[KG], writes=[KG])
            tt(E, LBR, MAG, COSB, ALU.mult)
            tt(E, LBI, MAG, SINB, ALU.mult)
            tt(E, DEN, LR, LR, ALU.mult)
            tt(E, TM, LI, LI, ALU.mult)
            tt(E, DEN, DEN, TM, ALU.add)
            P.op(E, lambda e: e.reciprocal(out=INV, in_=DEN), reads=[KG], writes=[KG])
            P.op(E, lambda e: e.tensor_scalar(out=NR, in0=LBR, scalar1=-1.0, scalar2=None, op0=ALU.add),
                 reads=[KG], writes=[KG])
            tt(E, CRr, NR, LR, ALU.mult)
            tt(E, TM, LBI, LI, ALU.mult)
            tt(E, CRr, CRr, TM, ALU.add)
            tt(E, CRr, CRr, INV, ALU.mult)
            tt(E, CIi, LBI, LR, ALU.mult)
            tt(E, TM, NR, LI, ALU.mult)
            tt(E, CIi, CIi, TM, ALU.subtract)
            tt(E, CIi, CIi, INV, ALU.mult)
            for d in range(2):
                for ri in range(2):
                    src = G_BN[64 * d:64 * d + 64, ri * 1024 + gt * 128:ri * 1024 + (gt + 1) * 128]
                    bnk = pb + 1 - d
                    P.op("pe", lambda e, src=src, d=d, ri=ri, bnk=bnk: e.transpose(
                        out=PS[bnk][:, ri * 64:(ri + 1) * 64], in_=src,
                        identity=IDF[64 * d:64 * d + 64, 64 * d:64 * d + 64]),
                        reads=gbn_keys + [("idf",)], writes=kps(bnk))
            for d in range(2):
                P.op("dve", lambda e, d=d: e.tensor_copy(out=G_BRP[:, d * 128:(d + 1) * 128], in_=PS[pb + 1 - d][:, 0:128]),
                     reads=kps(pb + 1 - d), writes=[KG])
            BRv = G_BRP.rearrange("q (d r p) -> q d r p", d=2, r=2)
            BBv = G_BB.rearrange("q (d r p) -> q d r p", d=2, r=2)
            cmul(E, BBv[:, :, 0, :], BBv[:, :, 1, :], CRr, CIi, BRv[:, :, 0, :], BRv[:, :, 1, :], TM, [KG], [KG])
            PW = G_PW.rearrange("q (n d r p) -> q n d r p", n=9, d=2, r=2)
            P.op("pool", lambda e: e.memset(PW[:, 0, :, 0, :], 1.0), reads=[KG], writes=[KG])
            P.op("pool", lambda e: e.memset(PW[:, 0, :, 1, :], 0.0), reads=[KG], writes=[KG])
            P.op(E, lambda e: e.tensor_copy(out=PW[:, 1, :, 0, :], in_=LBR), reads=[KG], writes=[KG])
            P.op(E, lambda e: e.tensor_copy(out=PW[:, 1, :, 1, :], in_=LBI), reads=[KG], writes=[KG])
            TMPb = G_TMPB.rearrange("q (n d p) -> q n d p", n=8, d=2)
            for (lo, cnt_, m) in ((1, 1, 1), (1, 2, 2), (1, 4, 4)):
                br = PW[:, m, :, 0, :].unsqueeze(1).to_broadcast([128, cnt_, 2, 64])
                bi = PW[:, m, :, 1, :].unsqueeze(1).to_broadcast([128, cnt_, 2, 64])
                cmul(E, PW[:, m + lo:m + lo + cnt_, :, 0, :], PW[:, m + lo:m + lo + cnt_, :, 1, :],
                     PW[:, lo:lo + cnt_, :, 0, :], PW[:, lo:lo + cnt_, :, 1, :], br, bi, TMPb[:, 0:cnt_, :, :],
                     [KG, ("g_tbb",)], [KG, ("g_tbb",)])
            TBb = G_TBB.rearrange("q (n d r p) -> q n d r p", n=8, d=2, r=2)
            bbr = BBv[:, :, 0, :].unsqueeze(1).to_broadcast([128, 8, 2, 64])
            bbi = BBv[:, :, 1, :].unsqueeze(1).to_broadcast([128, 8, 2, 64])
            cmul("dve", TBb[:, :, :, 0, :], TBb[:, :, :, 1, :], PW[:, 0:8, :, 0, :], PW[:, 0:8, :, 1, :], bbr, bbi,
                 TMPb, [KG], [("g_tbb",)])
            BST = G_BST.rearrange("q (m s p) -> q m s p", m=32, s=2)
            for s2 in range(2):
                P.op("act", lambda e, s2=s2: e.activation(
                    out=BST[:, :, s2, :], in_=G_TBB.rearrange("q (m p) -> q m p", m=32), func=AF.Copy,
                    scale=PMK[:, s2:s2 + 1]), reads=[("g_tbb",), ("pmk",)], writes=[("g_bst",)])
            P.op("sp", lambda e: e.dma_start(out=swb_d[gt, :, OFF_B:OFF_B + 4096], in_=G_BST),
                 reads=[("g_bst",)], writes=[("swb", gt)], dma=True, slot="gen0")
            PSB = [PS[b][:, :].bitcast(BF16) for b in range(8)]
            BST6 = G_BST.rearrange("q (n d r m) -> q n d r m", n=8, d=2, r=2)
            for d in range(2):
                for ri in range(2):
                    P.op("pe", lambda e, d=d, ri=ri: e.transpose(
                        out=PSB[pb + 1][:, (d * 2 + ri) * 128:(d * 2 + ri + 1) * 128], in_=BST6[:, 0, d, ri, :],
                        identity=IDB[:, :]), reads=[("g_bst",), ("idb",)], writes=kps(pb + 1))
            P.op("act", lambda e: e.activation(out=G_BX, in_=PSB[pb + 1][:, 0:512], func=AF.Copy), reads=kps(pb + 1),
                 writes=[("g_bx",)])
        def genB(gt):
            E = "dve"
            sx = gt % 2
            S = GS[sx]
            KG = ("gen", sx)
            G_PR, G_DT, G_BRP, G_BB, G_PW = S["PR"], S["DT"], S["BRP"], S["BB"], S["PW"]
            pb = 4 + (gt % 2) * 2

            G_BX = G_BXS[sx]
            G_CM = G_CMS[sx]
            kcm = ("g_cm", sx)
            PW = G_PW.rearrange("q (n d r p) -> q n d r p", n=9, d=2, r=2)
            PSB = [PS[b_][:, :].bitcast(BF16) for b_ in range(8)]
            TBc = G_TBC.rearrange("q (n d r p) -> q n d r p", n=9, d=2, r=2)
            TMPc = G_TMPC.rearrange("q (n d p) -> q n d p", n=9, d=2)
            CRv = G_CR.rearrange("q (d r t p) -> q d r t p", d=2, r=2, t=8)
            cr = CRv[:, :, 0, gt, :].unsqueeze(1).to_broadcast([128, 9, 2, 64])
            ci = CRv[:, :, 1, gt, :].unsqueeze(1).to_broadcast([128, 9, 2, 64])
            cmul("dve", TBc[:, :, :, 0, :], TBc[:, :, :, 1, :], PW[:, :, :, 0, :], PW[:, :, :, 1, :], cr, ci, TMPc,
                 [KG] + gcr_keys, [("g_tbc",)], neg_im=True)
            CM = G_CM.rearrange("q (m s p) -> q m s p", m=36, s=2)
            for s2 in range(2):
                P.op("act", lambda e, s2=s2: e.activation(
                    out=CM[:, :, s2, :], in_=G_TBC.rearrange("q (m p) -> q m p", m=36), func=AF.Copy,
                    scale=PMK[:, s2:s2 + 1]), reads=[("g_tbc",), ("pmk",)], writes=[kcm])
        def genC(gt):
            E = "dve"
            sx = gt % 2
            S = GS[sx]
            KG = ("gen", sx)
            G_PR, G_DT, G_BRP, G_BB, G_PW = S["PR"], S["DT"], S["BRP"], S["BB"], S["PW"]
            pb = 4 + (gt % 2) * 2

            G_BX = G_BXS[sx]
            G_CM = G_CMS[sx]
            kcm = ("g_cm", sx)
            PW = G_PW.rearrange("q (n d r p) -> q n d r p", n=9, d=2, r=2)
            PSB = [PS[b_][:, :].bitcast(BF16) for b_ in range(8)]
            CM3 = G_CM.rearrange("q (m c) -> q m c", m=36)
            CST3 = G_CST.rearrange("q (m c) -> q m c", m=36)
            for grp in range(5):
                m0 = grp * 8
                mcnt = min(8, 36 - m0)
                bb = pb + (grp % 2)
                for i in range(mcnt):
                    P.op("pe", lambda e, m=m0 + i, i=i, bb=bb: e.transpose(
                        out=PSB[bb][:, i * 128:(i + 1) * 128], in_=CM3[:, m, :], identity=IDB[:, :]),
                        reads=[kcm, ("idb",)], writes=kps(bb))
                P.op("act" if grp % 2 else "dve",
                     (lambda e, m0=m0, mcnt=mcnt, bb=bb: e.activation(
                         out=CST3[:, m0:m0 + mcnt, :], in_=PSB[bb][:, 0:mcnt * 128].rearrange("q (m c) -> q m c", m=mcnt),
                         func=AF.Copy)) if grp % 2 else
                     (lambda e, m0=m0, mcnt=mcnt, bb=bb: e.tensor_copy(
                         out=CST3[:, m0:m0 + mcnt, :], in_=PSB[bb][:, 0:mcnt * 128].rearrange("q (m c) -> q m c", m=mcnt))),
                     reads=kps(bb), writes=[("g_cst", grp)])
            cst_keys = [("g_cst", g) for g in range(5)]
            P.op("sp", lambda e: e.dma_start(out=swb_d[gt, :, OFF_C:OFF_C + 4608], in_=G_CST),
                 reads=cst_keys, writes=[("swb", gt)], dma=True, slot="gen1")
            CST5 = G_CST.rearrange("q (n d r c) -> q n d r c", n=9, d=2, r=2)
            BX4 = G_BX.rearrange("q (d r c) -> q d r c", d=2, r=2)
            for d in range(2):
                for w in range(4):
                    for ri in range(2):
                        P.op("pe", lambda e, d=d, w=w, ri=ri: e.matmul(
                            PS[pb][32 * w:32 * w + 32, d * 256:(d + 1) * 256],
                            lhsT=BX4[:, d, ri, 32 * w:32 * w + 32], rhs=CST5[:, 0:8, d, ri, 32 * w:32 * w + 32],
                            start=(ri == 0), stop=(ri == 1), tile_position=(0, 32 * w)),
                            reads=cst_keys + [("g_bx",)], writes=kps(pb))
            P.op("dve", lambda e: e.tensor_copy(out=G_KS, in_=PS[pb][:, 0:512]), reads=kps(pb), writes=[("g_ks",)])
            KS = G_KS.rearrange("q (d n c) -> q d n c", d=2, n=8)
            KT = G_KT.rearrange("q (l w c) -> q l w c", l=15, w=4)
            pmb7 = PAIRM[:, :].unsqueeze(1).unsqueeze(3).to_broadcast([128, 7, 4, 32])
            for d in range(2):
                P.op("dve", lambda e, d=d: e.tensor_tensor(
                    out=KT[:, 1 + 7 * d:8 + 7 * d, :, :], in0=KS[:, d, 1:8, :].unsqueeze(2).to_broadcast([128, 7, 4, 32]),
                    in1=pmb7, op=ALU.mult), reads=[("g_ks",), ("pairm",)], writes=[("g_kt",)])
            K0 = G_K0.rearrange("q (w c) -> q w c", w=4)
            P.op("dve", lambda e: e.tensor_tensor(out=G_MT[:, 0:32], in0=KS[:, 0, 0, :], in1=KS[:, 1, 0, :], op=ALU.add),
                 reads=[("g_ks",)], writes=[("g_k0",)])
            P.op("dve", lambda e: e.tensor_tensor(
                out=K0, in0=G_MT[:, 0:32].unsqueeze(1).to_broadcast([128, 4, 32]),
                in1=PAIRM[:, :].unsqueeze(2).to_broadcast([128, 4, 32]), op=ALU.mult),
                reads=[("g_k0",), ("pairm",)], writes=[("g_k0",)])
            P.op("dve", lambda e: e.scalar_tensor_tensor(
                out=G_KT[:, 0:128], in0=IDF[:, :], scalar=DSK[:, gt:gt + 1], in1=G_K0, op0=ALU.mult, op1=ALU.add),
                reads=[("g_k0",), ("idf",), ("vt1",)], writes=[("g_kt",)])
            P.op("sp", lambda e: e.dma_start(out=swb_d[gt, :, OFF_K:OFF_K + 1920], in_=G_KT),
                 reads=[("g_kt",)], writes=[("swb", gt)], dma=True, slot="gen2")
            MM = G_MM.rearrange("q (m s p) -> q m s p", m=4, s=2)
            for s2 in range(2):
                P.op("act", lambda e, s2=s2: e.activation(
                    out=MM[:, :, s2, :], in_=G_PW[:, 8 * 256:9 * 256].rearrange("q (m p) -> q m p", m=4), func=AF.Copy,
                    scale=PMK[:, s2:s2 + 1]), reads=[KG, ("pmk",)], writes=[("g_mm",)])
            for m in range(4):
                P.op("pe", lambda e, m=m: e.transpose(out=PS[pb + 1][:, m * 128:(m + 1) * 128],
                                                      in_=G_MM[:, m * 128:(m + 1) * 128], identity=IDF[:, :]),
                     reads=[("g_mm",), ("idf",)], writes=kps(pb + 1))
            MU = G_MU.rearrange("q (w d k r) -> q w d k r", w=4, d=2, k=7)
            psm = PS[pb + 1][:, :].rearrange("q (m w s h) -> q m w s h", m=4, w=4, s=2)[:, :, :, :, 0]
            P.op("dve", lambda e: e.tensor_reduce(out=MU[:, :, :, 0, :].rearrange("q w d r -> q d r w"),
                                                  in_=psm.rearrange("q (d r) w s -> q d r w s", d=2),
                                                  axis=AX.X, op=ALU.add), reads=kps(pb + 1), writes=[("g_mu",)])
            MT = G_MT[:, 32:56].rearrange("q (i w d) -> q i w d", i=3, w=4)
            ME = "pool"
            for k in range(6):
                a = MU[:, :, :, k, 0]
                bq = MU[:, :, :, k, 1]
                P.op(ME, lambda e, a=a: e.tensor_tensor(out=MT[:, 0], in0=a, in1=a, op=ALU.mult),
                     reads=[("g_mu",)], writes=[("g_mt",)])
                P.op(ME, lambda e, bq=bq: e.tensor_tensor(out=MT[:, 1], in0=bq, in1=bq, op=ALU.mult),
                     reads=[("g_mu",), ("g_mt",)], writes=[("g_mt",)])
                P.op(ME, lambda e, k=k: e.tensor_tensor(out=MU[:, :, :, k + 1, 0], in0=MT[:, 0], in1=MT[:, 1],
                                                        op=ALU.subtract), reads=[("g_mt",)], writes=[("g_mu",)])
                P.op(ME, lambda e, a=a, bq=bq: e.tensor_tensor(out=MT[:, 2], in0=a, in1=bq, op=ALU.mult),
                     reads=[("g_mu",), ("g_mt",)], writes=[("g_mt",)])
                P.op(ME, lambda e, k=k: e.tensor_tensor(out=MU[:, :, :, k + 1, 1], in0=MT[:, 2], in1=MT[:, 2],
                                                        op=ALU.add), reads=[("g_mt",)], writes=[("g_mu",)])
            P.op("sp", lambda e: e.dma_start(out=swm_d[gt], in_=G_MU), reads=[("g_mu",)], writes=[("swm", gt)],
                 dma=True, slot="gen3")

        ssm_param_loads()

        def mod_block(j):
            wb = w_acquire(wb_ada[j])
            wv = wview(wb, "k16")
            b = big_bank() if j < 8 else 6 + (j % 2)
            for cc in range(4):
                for k in range(16):
                    P.op("pe", lambda e, wv=wv, cc=cc, k=k, b=b: e.matmul(
                        PS[b][:, cc * 2:cc * 2 + 2], lhsT=wv[:, k, cc * 128:(cc + 1) * 128], rhs=SCT[:, k, :],
                        start=(k == 0), stop=(k == 15)),
                        reads=[("wb", wb), ("sct", 0), ("sct", 1)], writes=kps(b))
            if j < 8:
                P.op("dve", lambda e, j=j, b=b: e.tensor_tensor(
                    out=MODT[:, j * 4:(j + 1) * 4, :], in0=PS[b][:, 0:8].rearrange("p (c v) -> p c v", v=2),
                    in1=VT2[:, j * 4:(j + 1) * 4].unsqueeze(2).to_broadcast([128, 4, 2]), op=ALU.add),
                    reads=kps(b) + [("vt2",)], writes=[("modt", j)])
            else:
                for cc in range(4):
                    P.op("act", lambda e, j=j, b=b, cc=cc: e.activation(
                        out=MODT[:, j * 4 + cc, :], in_=PS[b][:, cc * 2:cc * 2 + 2], func=AF.Identity,
                        bias=VT2[:, j * 4 + cc:j * 4 + cc + 1], scale=1.0),
                        reads=kps(b) + [("vt2",)], writes=[("modt", j)])

        def modk(j0, n=16):
            return [("modt", j) for j in range(j0 // 4, (j0 + n + 3) // 4)]

        def mod_finish_f():
            P.op("dve", lambda e: e.scalar_tensor_tensor(
                out=SCF[:, :, :], in0=MODT[:, 64:80, :], scalar=1.0, in1=GPF.unsqueeze(2).to_broadcast([128, 16, 2]),
                op0=ALU.add, op1=ALU.mult), reads=modk(64) + [("vt1",)], writes=[("scf",)])

        gen_on = not (debug or "").startswith(("mod", "ph1", "ph2", "ph3"))
        if gen_on:
            genA(0)
            genB(0)
        for j in range(8):
            mod_block(j)
            if gen_on:
                if j < 7:
                    genA(j + 1)
                    genB(j + 1)
                genC(j)
        P.op("dve", lambda e: e.scalar_tensor_tensor(
            out=SCM[:, :, :], in0=MODT[:, 16:32, :], scalar=1.0, in1=GPM.unsqueeze(2).to_broadcast([128, 16, 2]),
            op0=ALU.add, op1=ALU.mult), reads=modk(16) + [("vt1",)], writes=[("scm",)])
        modall = modk(0, 96)

        if debug == "gen":
            P.barrier()
            dbg_outs["swb"] = dout("dbg_swb", [8, 128, NSW], BF16)
            dbg_outs["swm"] = dout("dbg_swm", [8, 128, NMU])
            P.op("sp", lambda e: e.dma_start(out=dbg_outs["swb"], in_=swb_d), dma=True, slot="dbg")
            P.op("sp", lambda e: e.dma_start(out=dbg_outs["swm"], in_=swm_d), dma=True, slot="dbg2")
            P.emit()
            return nc
        if debug == "mod":
            dbg_outs["modt"] = dout("dbg_modt", [128, 192])
            P.op("sp", lambda e: e.dma_start(out=dbg_outs["modt"], in_=MODT[:, :, :].rearrange("p j v -> p (j v)")),
                 reads=modall, dma=True, slot="dbg")
            P.emit()
            return nc


        UA = BIG[:, 0:8192].rearrange("p (t n) -> p t n", t=8)
        VT = BIG[:, 8192:16384].rearrange("p (t n) -> p t n", t=8)
        VH = VHR[:, :].rearrange("p (t n) -> p t n", t=8)
        GV = XT[1][:, 0:1024]
        UTF = UT[:, :, :, :].rearrange("p g j c -> p (g j c)").bitcast(F32)
        GMROW = UTF[:, 0:2048]
        GFROW = UTF[:, 2048:4096]

        FREQ = sb("FREQ", [128, 512], F32)
        PIDX = sb("PIDX", [128, 4], F32)
        PIDI = sb("PIDI", [128, 4], I32)

        def build_freq():
            FI = XT[1][:, 0:512].bitcast(I32)
            P.op("pool", lambda e: e.iota(FI, pattern=[[1, 512]], base=0, channel_multiplier=0), writes=[("xt", 1)])
            P.op("dve", lambda e: e.tensor_copy(out=FREQ[:, :], in_=FI), reads=[("xt", 1)], writes=[("freq",)])
            P.op("act", lambda e: e.activation(out=FREQ[:, :], in_=FREQ[:, :], func=AF.Exp,
                                               scale=-math.log(10000.0) / 512.0), reads=[("freq",)], writes=[("freq",)])
            P.op("pool", lambda e: e.iota(PIDI[:, 0:1], pattern=[[0, 1]], base=0, channel_multiplier=1),
                 writes=[("pidi",)])
            P.op("dve", lambda e: e.tensor_single_scalar(out=PIDI[:, 1:2], in_=PIDI[:, 0:1], scalar=63,
                                                         op=ALU.bitwise_and), reads=[("pidi",)], writes=[("pidi1",)])
            P.op("dve", lambda e: e.tensor_single_scalar(out=PIDI[:, 2:3], in_=PIDI[:, 0:1], scalar=6,
                                                         op=ALU.arith_shift_right), reads=[("pidi",)], writes=[("pidi2",)])
            P.op("dve", lambda e: e.tensor_copy(out=PIDX[:, 0:2], in_=PIDI[:, 1:3]),
                 reads=[("pidi1",), ("pidi2",)], writes=[("pidx",)])

        def sincos_table(dst, dst_key, idx_col, add_const, tmpA, tmpB, tmp_key):
            TI = tmpB.bitcast(I32)
            P.op("dve", lambda e: e.tensor_scalar(out=tmpA, in0=FREQ[:, :], scalar1=idx_col, scalar2=None, op0=ALU.mult),
                 reads=[("freq",), ("pidx",)], writes=[tmp_key])
            if add_const != 0:
                P.op("dve", lambda e: e.scalar_tensor_tensor(out=tmpA, in0=FREQ[:, :], scalar=float(add_const), in1=tmpA,
                                                             op0=ALU.mult, op1=ALU.add),
                     reads=[("freq",), tmp_key], writes=[tmp_key])
            P.op("dve", lambda e: e.tensor_scalar(out=TI, in0=tmpA, scalar1=1.0 / TWO_PI, scalar2=None, op0=ALU.mult),
                 reads=[tmp_key], writes=[tmp_key])
            P.op("dve", lambda e: e.tensor_copy(out=tmpB, in_=TI), reads=[tmp_key], writes=[tmp_key])
            P.op("dve", lambda e: e.scalar_tensor_tensor(out=tmpA, in0=tmpB, scalar=-TWO_PI, in1=tmpA, op0=ALU.mult,
                                                         op1=ALU.add), reads=[tmp_key], writes=[tmp_key])
            P.op("dve", lambda e: e.tensor_scalar(out=tmpB, in0=tmpA, scalar1=math.pi, scalar2=-TWO_PI, op0=ALU.is_gt,
                                                  op1=ALU.mult), reads=[tmp_key], writes=[tmp_key])
            P.op("dve", lambda e: e.tensor_tensor(out=tmpA, in0=tmpA, in1=tmpB, op=ALU.add), reads=[tmp_key],
                 writes=[tmp_key])
            P.op("act", lambda e: e.activation(out=dst[:, 0:512], in_=tmpA, func=AF.Sin), reads=[tmp_key],
                 writes=[dst_key])
            P.op("dve", lambda e: e.tensor_scalar(out=tmpA, in0=tmpA, scalar1=math.pi / 2, scalar2=None, op0=ALU.add),
                 reads=[tmp_key, dst_key], writes=[tmp_key])
            P.op("dve", lambda e: e.tensor_scalar(out=tmpB, in0=tmpA, scalar1=math.pi, scalar2=-TWO_PI, op0=ALU.is_gt,
                                                  op1=ALU.mult), reads=[tmp_key], writes=[tmp_key])
            P.op("dve", lambda e: e.tensor_tensor(out=tmpA, in0=tmpA, in1=tmpB, op=ALU.add), reads=[tmp_key],
                 writes=[tmp_key])
            P.op("act", lambda e: e.activation(out=dst[:, 512:1024], in_=tmpA, func=AF.Sin), reads=[tmp_key],
                 writes=[dst_key])

        VHF = VHR[:, :].bitcast(F32)
        POSC = VHF[:, 0:1024]
        ROWT = VHF[:, 1024:2048]
        PTMPA = VHF[:, 2048:2560]
        PTMPB = VHF[:, 2560:3072]

        def load_x_tile(u, t, xb_i, src):
            xb = XT[xb_i]
            P.op("sp", lambda e: e.dma_start(out=xb[:, :], in_=src[t * 128:(t + 1) * 128, :]),
                 writes=[("xt", xb_i)], dma=True, slot=("xt", xb_i))
            if u == 0:
                sincos_table(ROWT, ("rowt",), PIDX[:, 1:2], 2 * t, PTMPA, PTMPB, ("ptmp",))
                P.op("dve", lambda e: e.tensor_tensor(out=xb[:, 0:1024], in0=xb[:, 0:1024], in1=ROWT, op=ALU.add),
                     reads=[("rowt",), ("xt", xb_i)], writes=[("xt", xb_i)])
                P.op("dve", lambda e: e.tensor_tensor(out=xb[:, 1024:2048], in0=xb[:, 1024:2048], in1=POSC, op=ALU.add),
                     reads=[("posc",), ("xt", xb_i)], writes=[("xt", xb_i)])

        def norm_and_transpose(src_ap, src_keys, t, sc_ap, sh_ap, v, bank_base, mkeys=(), stage=0):
            mkeys = list(mkeys)
            if stage in (0, 1):
                _norm_part(src_ap, src_keys, t)
            if stage in (0, 2):
                _tr_part(src_ap, src_keys, t, sc_ap, sh_ap, v, bank_base, mkeys)

        def _norm_part(src_ap, src_keys, t):
            for c4 in range(4):
                P.op("dve", lambda e, c4=c4: e.bn_stats(out=BNS[:, t, c4, :], in_=src_ap[:, c4 * 512:(c4 + 1) * 512]),
                     reads=src_keys, writes=[("bns", t, c4)])
            col = 16 + t * 8
            mv = smcol(col, 2)
            P.op("dve", lambda e: e.bn_aggr(out=mv, in_=BNS[:, t, :, :].rearrange("p a b -> p (a b)")),
                 reads=[("bns", t, c4) for c4 in range(4)], writes=[("sm", col)])
            P.op("dve", lambda e: e.scalar_tensor_tensor(out=smcol(col + 2), in0=mv[:, 0:1], scalar=mv[:, 0:1],
                                                         in1=mv[:, 1:2], op0=ALU.mult, op1=ALU.add),
                 reads=[("sm", col)], writes=[("sm", col + 2)])
            P.op("act", lambda e: e.activation(out=smcol(col + 3), in_=smcol(col + 2), func=AF.Sqrt,
                                               bias=EPSC[:, 0:1], scale=1.0),
                 reads=[("sm", col + 2), ("epsc",)], writes=[("sm", col + 3)])
            P.op("dve", lambda e: e.reciprocal(out=smcol(col + 4), in_=smcol(col + 3)),
                 reads=[("sm", col + 3)], writes=[("sm", col + 4)])
            P.op("act", lambda e: e.activation(out=src_ap, in_=src_ap, func=AF.Copy, scale=smcol(col + 4)),
                 reads=list(src_keys) + [("sm", col + 4)], writes=src_keys)

        def _tr_part(src_ap, src_keys, t, sc_ap, sh_ap, v, bank_base, mkeys):
            for kg in range(4):
                b = bank_base + kg
                for j in range(4):
                    k = kg * 4 + j
                    P.op("pe", lambda e, k=k, b=b, j=j: e.transpose(out=PS[b][:, j * 128:(j + 1) * 128],
                                                                    in_=src_ap[:, k * 128:(k + 1) * 128], identity=IDF[:, :]),
                         reads=list(src_keys) + [("idf",)], writes=[("ps", b)])
                for j in range(4):
                    k = kg * 4 + j
                    eng = ev_eng()
                    if eng == "act":
                        P.op("act", lambda e, k=k, b=b, j=j: e.activation(
                            out=HT[:, k, t * 128:(t + 1) * 128], in_=PS[b][:, j * 128:(j + 1) * 128], func=AF.Identity,
                            scale=sc_ap[:, k, v:v + 1], bias=sh_ap[:, k, v:v + 1]),
                            reads=[("ps", b), ("scm",), ("scf",)] + mkeys, writes=[("ht", t)])
                    else:
                        P.op("dve", lambda e, k=k, b=b, j=j: e.tensor_scalar(
                            out=HT[:, k, t * 128:(t + 1) * 128], in0=PS[b][:, j * 128:(j + 1) * 128],
                            scalar1=sc_ap[:, k, v:v + 1], scalar2=sh_ap[:, k, v:v + 1], op0=ALU.mult, op1=ALU.add),
                            reads=[("ps", b), ("scm",), ("scf",)] + mkeys, writes=[("ht", t)])

        P.barrier()
        build_freq()

        def unit(u):
            v = u
            W = wb_unit[u]
            xsrc = x_in[u]
            if u == 0:
                sincos_table(POSC, ("posc",), PIDX[:, 0:1], 0, PTMPA, PTMPB, ("ptmp",))
            def ph1(t, stage):
                if stage == 1:
                    load_x_tile(u, t, t % 2, xsrc)
                norm_and_transpose(XT[t % 2][:, :], [("xt", t % 2)], t, SCM, MODT[:, 0:16, :], v, (t % 2) * 4, modk(0),
                                   stage=stage)
            for t in range(8):
                ph1(t, 1)
                ph1(t, 2)
            P.barrier()
            if debug == "ph1" and u == int(debug_unit):
                return "stop"
            if "gv" not in _SKIP:
                P.op("sp", lambda e: e.dma_start(out=GV, in_=chunk_g_v.partition_broadcast(128)),
                     writes=[("gv",), ("xt", 1)], dma=True, slot="c0")
            for j in range(6):
                if "ub" in _SKIP and j >= 4:
                    continue
                wb = w_acquire(W["win"][j])
                wv = wview(wb, "k16")
                if j < 4:
                    for t in range(8):
                        b = big_bank()
                        for k in range(16):
                            P.op("pe", lambda e, k=k, t=t, b=b, wv=wv: e.matmul(
                                PS[b][:, :], lhsT=HT[:, k, t * 128:(t + 1) * 128], rhs=wv[:, k, :],
                                start=(k == 0), stop=(k == 15)), reads=[("wb", wb), ("ht", t)], writes=kps(b))
                        if j < 2:
                            P.op("act", lambda e, t=t, b=b, j=j: e.activation(
                                out=UA[:, t, j * 512:(j + 1) * 512], in_=PS[b][:, :], func=AF.Copy),
                                reads=kps(b), writes=[("ua", t, j)])
                        else:
                            jj = j - 2
                            P.op("act", lambda e, t=t, b=b, jj=jj: e.activation(
                                out=VT[:, t, jj * 512:(jj + 1) * 512], in_=PS[b][:, :], func=AF.Copy),
                                reads=kps(b), writes=[("vt", t, jj)])
                            P.op("dve", lambda e, t=t, jj=jj: e.bn_stats(out=BNS[:, t, jj, :],
                                                                         in_=VT[:, t, jj * 512:(jj + 1) * 512]),
                                 reads=[("vt", t, jj)], writes=[("bns", t, jj)])
                else:
                    jj = j - 4
                    for cc in range(4):
                        gt = jj * 4 + cc
                        for th in range(2):
                            b = big_bank()
                            for k in range(16):
                                P.op("pe", lambda e, k=k, th=th, b=b, wv=wv, cc=cc: e.matmul(
                                    PS[b][:, :], lhsT=wv[:, k, cc * 128:(cc + 1) * 128],
                                    rhs=HT[:, k, th * 512:(th + 1) * 512], start=(k == 0), stop=(k == 15)),
                                    reads=[("wb", wb)] + [("ht", th * 4 + i) for i in range(4)], writes=kps(b))
                            eng = ev_eng()
                            outap = UT[:, gt, :, th * 64:(th + 1) * 64].rearrange("p j c -> p c j")
                            inap = PS[b][:, :].rearrange("p (c j) -> p c j", j=TC)
                            if eng == "act":
                                P.op("act", lambda e, outap=outap, inap=inap: e.activation(out=outap, in_=inap, func=AF.Copy),
                                     reads=kps(b), writes=[("ut", gt, th)])
                            else:
                                P.op("dve", lambda e, outap=outap, inap=inap: e.tensor_copy(out=outap, in_=inap),
                                     reads=kps(b), writes=[("ut", gt, th)])
            if debug == "ph2" and u == int(debug_unit):
                return "stop"
            def cmix(t, stage):
                col = 16 + t * 8
                c0 = 128 + t * 8
                if stage == 1:
                    col = 16 + t * 8
                    mv = smcol(col, 2)
                    P.op("dve", lambda e, t=t, mv=mv: e.bn_aggr(out=mv, in_=BNS[:, t, 0:2, :].rearrange("p a b -> p (a b)")),
                         reads=[("bns", t, 0), ("bns", t, 1)], writes=[("sm", col)])
                    P.op("dve", lambda e, mv=mv, col=col: e.scalar_tensor_tensor(
                        out=smcol(col + 2), in0=mv[:, 0:1], scalar=mv[:, 0:1], in1=mv[:, 1:2], op0=ALU.mult, op1=ALU.add),
                        reads=[("sm", col)], writes=[("sm", col + 2)])
                    P.op("act", lambda e, col=col: e.activation(out=smcol(col + 3), in_=smcol(col + 2), func=AF.Sqrt,
                                                                bias=EPSC[:, 0:1], scale=1.0),
                         reads=[("sm", col + 2), ("epsc",)], writes=[("sm", col + 3)])
                    P.op("dve", lambda e, col=col: e.reciprocal(out=smcol(col + 4), in_=smcol(col + 3)),
                         reads=[("sm", col + 3)], writes=[("sm", col + 4)])
                    P.op("dve", lambda e, t=t, col=col: e.scalar_tensor_tensor(
                        out=VH[:, t, :], in0=VT[:, t, :], scalar=smcol(col + 4), in1=GV, op0=ALU.mult, op1=ALU.mult),
                        reads=[("vt", t, 0), ("vt", t, 1), ("sm", col + 4), ("gv",)], writes=[("vh", t)])
                    zb = 4 + (t % 2) * 2
                    for h in range(8):
                        b = zb + h // 4
                        P.op("pe", lambda e, t=t, h=h, b=b: e.matmul(
                            PS[b][:, (h % 4) * 128:(h % 4 + 1) * 128], lhsT=WST[:, h, :], rhs=VH[:, t, h * 128:(h + 1) * 128],
                            start=True, stop=True), reads=[("vh", t), ("wst", h)], writes=[("ps", b)])
                    for h in range(8):
                        b = zb + h // 4
                        P.op("dve", lambda e, t=t, h=h, b=b: e.scalar_tensor_tensor(
                            out=UA[:, t, h * 128:(h + 1) * 128], in0=PS[b][:, (h % 4) * 128:(h % 4 + 1) * 128],
                            scalar=BSP[:, h:h + 1], in1=UA[:, t, h * 128:(h + 1) * 128], op0=ALU.add, op1=ALU.mult),
                            reads=[("ps", b), ("bsp",), ("ua", t, h // 4)], writes=[("ua", t, h // 4)])
                    for c2i in range(2):
                        P.op("dve", lambda e, t=t, c2i=c2i: e.bn_stats(out=BNS[:, t, 2 + c2i, :],
                                                                       in_=UA[:, t, c2i * 512:(c2i + 1) * 512]),
                             reads=[("ua", t, c2i)], writes=[("bns", t, 2 + c2i)])
                    col2 = col + 5
                    P.op("dve", lambda e, t=t, col2=col2: e.bn_aggr(out=SMALL[:, 128 + t * 8:130 + t * 8],
                                                                   in_=BNS[:, t, 2:4, :].rearrange("p a b -> p (a b)")),
                         reads=[("bns", t, 2), ("bns", t, 3)], writes=[("sm", 128 + t * 8)])
                    c0 = 128 + t * 8
                    P.op("dve", lambda e, c0=c0: e.scalar_tensor_tensor(
                        out=smcol(c0 + 2), in0=smcol(c0), scalar=smcol(c0), in1=smcol(c0 + 1), op0=ALU.mult, op1=ALU.add),
                        reads=[("sm", c0)], writes=[("sm", c0 + 2)])
                    P.op("act", lambda e, c0=c0: e.activation(out=smcol(c0 + 3), in_=smcol(c0 + 2), func=AF.Sqrt,
                                                              bias=EPSC[:, 0:1], scale=1.0),
                         reads=[("sm", c0 + 2), ("epsc",)], writes=[("sm", c0 + 3)])
                    P.op("dve", lambda e, c0=c0: e.reciprocal(out=smcol(c0 + 4), in_=smcol(c0 + 3)),
                         reads=[("sm", c0 + 3)], writes=[("sm", c0 + 4)])
                    P.op("act", lambda e, t=t, c0=c0: e.activation(out=UA[:, t, :], in_=UA[:, t, :], func=AF.Copy,
                                                                   scale=smcol(c0 + 4)),
                         reads=[("ua", t, 0), ("ua", t, 1), ("sm", c0 + 4)], writes=[("ua", t, 0), ("ua", t, 1)])
                if stage == 2:
                    tb = (t % 2) * 2
                    for kg in range(2):
                        b = tb + kg
                        for j in range(4):
                            k = kg * 4 + j
                            P.op("pe", lambda e, t=t, k=k, b=b, j=j: e.transpose(
                                out=PS[b][:, j * 128:(j + 1) * 128], in_=UA[:, t, k * 128:(k + 1) * 128], identity=IDF[:, :]),
                                reads=[("ua", t, k // 4), ("idf",)], writes=[("ps", b)])
                        for j in range(4):
                            k = kg * 4 + j
                            eng = ev_eng()
                            if eng == "act":
                                P.op("act", lambda e, t=t, k=k, b=b, j=j: e.activation(
                                    out=HT[:, k, t * 128:(t + 1) * 128], in_=PS[b][:, j * 128:(j + 1) * 128], func=AF.Copy,
                                    scale=GOA[:, k:k + 1]), reads=[("ps", b), ("vt1",)], writes=[("ht", t)])
                            else:
                                P.op("dve", lambda e, t=t, k=k, b=b, j=j: e.tensor_scalar(
                                    out=HT[:, k, t * 128:(t + 1) * 128], in0=PS[b][:, j * 128:(j + 1) * 128],
                                    scalar1=GOA[:, k:k + 1], scalar2=None, op0=ALU.mult),
                                    reads=[("ps", b), ("vt1",)], writes=[("ht", t)])
            cmix(0, 1)
            for t in range(8):
                if t < 7:
                    cmix(t + 1, 1)
                cmix(t, 2)
            P.barrier()
            if debug == "ph3" and u == int(debug_unit):
                return "stop"
            SWBt = BIG[:, 0:2048].bitcast(BF16)
            SWCt = BIG[:, 2048:5312].bitcast(BF16)
            Bst = SWBt.rearrange("q (n d r m) -> q n d r m", n=8, d=2, r=2)
            Cst = SWCt[:, 0:4608].rearrange("q (n d r c) -> q n d r c", n=9, d=2, r=2)
            KTw = SWCt[:, 4608:6528].rearrange("q (l c) -> q l c", l=15)
            SBF = BIG[:, 5312:6368].bitcast(BF16).rearrange("q (d i c) -> q d i c", d=2, i=8)
            SMUS = [BIG[:, 6368 + i * 168:6480 + i * 168].rearrange("q (w d k r) -> q w d k r", w=4, d=2, k=7) for i in range(2)]
            NMUS = [BIG[:, 6480 + i * 168:6536 + i * 168].rearrange("q (w d k) -> q w d k", w=4, d=2) for i in range(2)]
            SMUF = [BIG[:, 6368 + i * 168:6480 + i * 168] for i in range(2)]
            HM = BIG[:, 6704:6728]
            T1 = BIG[:, 6728:7240]
            FSt = BIG[:, 7240:7752]
            FS = FSt.rearrange("q (s d r w) -> q s d r w", s=4, d=2, r=2)
            YG = BIG[:, 8192:16384].rearrange("q (g n) -> q g n", g=8)
            YGB = VHR[:, :].rearrange("q (g n) -> q g n", g=8)
            HT2 = HT[:, 8:16, :].rearrange("p k n -> p (k n)").bitcast(F32)
            SBUFS = [[XT[0][:, 0:1536], XT[1][:, 0:1536]], [HT2[:, 0:1536], HT2[:, 1536:3072]]]
            PTMP = [XT[0][:, 1536 + i * 128:1664 + i * 128] for i in range(4)] + \
                   [XT[1][:, 1536 + i * 128:1664 + i * 128] for i in range(4)] + \
                   [HT2[:, 3072 + i * 128:3200 + i * 128] for i in range(8)]
            nlev = 7 if u == 0 else 5

            def dat(buf, i, d, shift=0):
                if u == 0:
                    o = 64 if d == 0 else 0
                    return buf.rearrange("q (i c) -> q i c", i=8)[:, i, o + shift:o + 128 + shift]
                o = 16 if d == 0 else 0
                return buf.rearrange("q (i s c) -> q i s c", i=8, s=4)[:, i, :, o + shift:o + 32 + shift]

            def dat2(buf, w, d, shift=0):
                if u == 0:
                    o = 64 if d == 0 else 0
                    return buf.rearrange("q (i c) -> q i c", i=8)[:, 2 * w:2 * w + 2, o + shift:o + 128 + shift]
                o = 16 if d == 0 else 0
                return buf.rearrange("q (i c) -> q i c", i=32)[:, 8 * w:8 * w + 8, o + shift:o + 32 + shift]

            def dat_rows(buf, i0, i1, d):
                if u == 0:
                    o = 64 if d == 0 else 0
                    return buf.rearrange("q (i c) -> q i c", i=8)[:, i0:i1, o:o + 128]
                o = 16 if d == 0 else 0
                return buf.rearrange("q (i s c) -> q i s c", i=8, s=4)[:, i0:i1, :, o:o + 32]

            def ptv(i):
                return PTMP[i] if u == 0 else PTMP[i].rearrange("q (s c) -> q s c", s=4)

            def skey(d, si, w):
                return ("scan", d, si, w)

            P.op("pool", lambda e: e.memset(XT[0][:, :], 0.0), writes=[skey(0, 0, w) for w in range(4)] + [("xt", 0)])
            P.op("pool", lambda e: e.memset(XT[1][:, :], 0.0), writes=[skey(0, 1, w) for w in range(4)] + [("xt", 1)])
            P.op("pool", lambda e: e.memset(HT2[:, :], 0.0), writes=[skey(1, si_, w) for w in range(4) for si_ in range(2)])
            P.op("pool", lambda e: e.memset(BIG[:, 5312:6368], 0.0), writes=[("sbf", 0), ("sbf", 1)])

            def emit_V(gt):
                P.op("sp", lambda e: e.dma_start(out=SWBt[:, :], in_=swb_d[gt, :, OFF_B:OFF_B + 4096]), reads=[("swb", gt)],
                     writes=[("swB",)], dma=True, slot="swB")
                for d in range(2):
                    for w in range(4):
                        for ri in range(2):
                            col = (d * 2 + ri) * 128
                            for j in range(8):
                                n = 7 - j if d == 0 else j
                                P.op("pe", lambda e, w=w, ri=ri, j=j, n=n, d=d, col=col: e.matmul(
                                    PS[w][:, col:col + 128], lhsT=Bst[32 * w:32 * w + 32, n, d, ri, :],
                                    rhs=UT[32 * w:32 * w + 32, gt, j, :], start=(j == 0), stop=(j == 7),
                                    tile_position=(32 * w, 0)),
                                    reads=[("swB",), ("ut", gt, 0), ("ut", gt, 1)], writes=kps(w))

            def load_mu(gt):
                i = gt % 2
                P.op("sp", lambda e: e.dma_start(out=SMUF[i], in_=swm_d[gt]), reads=[("swm", gt)],
                     writes=[("smu", i)], dma=True, slot=("smu", i))
                P.op("act", lambda e: e.activation(out=NMUS[i], in_=SMUS[i][:, :, :, :, 1], func=AF.Copy, scale=-1.0),
                     reads=[("smu", i)], writes=[("nmu", i)])

            def evac_V(gt):
                for d in range(2):
                    for w in range(4):
                        if u == 0:
                            inap = PS[w][:, d * 256:(d + 1) * 256].rearrange("q (i c) -> q i c", i=2)
                        else:
                            inap = PS[w][:, d * 256:(d + 1) * 256].rearrange("q (i s c) -> q i s c", i=2, s=4)
                        outap = dat_rows(SBUFS[d][0], 2 * w, 2 * w + 2, d)
                        P.op("act", lambda e, outap=outap, inap=inap: e.activation(out=outap, in_=inap, func=AF.Copy),
                             reads=kps(w), writes=[skey(d, 0, w)])

            def scan(gt):
                SMU, NMU = SMUS[gt % 2], NMUS[gt % 2]
                mk = [("smu", gt % 2), ("nmu", gt % 2)]
                if u == 0:
                    for d in range(2):
                        h0r = H0T[:, (d * 2 + 0) * 32 + gt * 4:(d * 2 + 0) * 32 + gt * 4 + 4]
                        h0i = H0T[:, (d * 2 + 1) * 32 + gt * 4:(d * 2 + 1) * 32 + gt * 4 + 4]
                        mur, mui = SMU[:, :, d, 0, 0], SMU[:, :, d, 0, 1]
                        HMr, HMi, HMt = HM[:, d * 12:d * 12 + 4], HM[:, d * 12 + 4:d * 12 + 8], HM[:, d * 12 + 8:d * 12 + 12]
                        hk = [("hm", d)]
                        rk = [("hm", d), ("h0t",)] + mk
                        P.op("dve", lambda e, mui=mui, h0i=h0i, HMt=HMt: e.tensor_tensor(out=HMt, in0=mui, in1=h0i, op=ALU.mult), reads=rk, writes=hk)
                        P.op("dve", lambda e, mur=mur, h0r=h0r, HMr=HMr: e.tensor_tensor(out=HMr, in0=mur, in1=h0r, op=ALU.mult), reads=rk, writes=hk)
                        P.op("dve", lambda e, HMr=HMr, HMt=HMt: e.tensor_tensor(out=HMr, in0=HMr, in1=HMt, op=ALU.subtract), reads=rk, writes=hk)
                        P.op("dve", lambda e, mui=mui, h0r=h0r, HMt=HMt: e.tensor_tensor(out=HMt, in0=mui, in1=h0r, op=ALU.mult), reads=rk, writes=hk)
                        P.op("dve", lambda e, mur=mur, h0i=h0i, HMi=HMi: e.tensor_tensor(out=HMi, in0=mur, in1=h0i, op=ALU.mult), reads=rk, writes=hk)
                        P.op("dve", lambda e, HMi=HMi, HMt=HMt: e.tensor_tensor(out=HMi, in0=HMi, in1=HMt, op=ALU.add), reads=rk, writes=hk)
                        pos = 64 if d == 0 else 127
                        SA5 = SBUFS[d][0].rearrange("q (w r c) -> q w r c", w=4, r=2)
                        allsa = [skey(d, 0, w) for w in range(4)]
                        P.op("dve", lambda e, pos=pos, SA5=SA5, HMr=HMr: e.tensor_tensor(
                            out=SA5[:, :, 0, pos], in0=SA5[:, :, 0, pos], in1=HMr, op=ALU.add), reads=allsa + hk, writes=allsa)
                        P.op("dve", lambda e, pos=pos, SA5=SA5, HMi=HMi: e.tensor_tensor(
                            out=SA5[:, :, 1, pos], in0=SA5[:, :, 1, pos], in1=HMi, op=ALU.add), reads=allsa + hk, writes=allsa)
                si = 0
                for k in range(nlev):
                    for term in (0, 1, 3):
                        for d in range(2):
                            sft = (1 << k) * (-1 if d == 0 else 1)
                            src, dst = SBUFS[d][si], SBUFS[d][1 - si]
                            for w in range(4):
                                mur = SMU[:, w, d, k, 0:1]
                                mui = SMU[:, w, d, k, 1:2]
                                nmui = NMU[:, w, d, k:k + 1]
                                re, im = 2 * w, 2 * w + 1
                                rk = [skey(d, si, w), skey(d, 1 - si, w)] + mk
                                wk = [skey(d, 1 - si, w)]
                                if term == 0:
                                    o_, i0_, sc_, i1_ = dat2(dst, w, d), dat2(src, w, d, sft), mur, dat2(src, w, d)
                                elif term == 1:
                                    o_, i0_, sc_, i1_ = dat(dst, re, d), dat(src, im, d, sft), nmui, dat(dst, re, d)
                                else:
                                    o_, i0_, sc_, i1_ = dat(dst, im, d), dat(src, re, d, sft), mui, dat(dst, im, d)
                                P.op("dve", lambda e, o_=o_, i0_=i0_, sc_=sc_, i1_=i1_: e.scalar_tensor_tensor(
                                    out=o_, in0=i0_, scalar=sc_, in1=i1_, op0=ALU.mult, op1=ALU.add), reads=rk, writes=wk)
                    si = 1 - si
                for d in range(2):
                    fin = SBUFS[d][si]
                    fkeys = [skey(d, si, w) for w in range(4)]
                    c0 = 1 if d == 0 else 0
                    if u == 0:
                        P.op("act", lambda e, fin=fin, d=d, c0=c0: e.activation(
                            out=SBF[:, d, :, c0:c0 + 128], in_=dat_rows(fin, 0, 8, d), func=AF.Copy),
                            reads=fkeys, writes=[("sbf", d)])
                        hs = 0 if d == 0 else 128
                        P.op("act", lambda e, d=d, hs=hs, gt=gt: e.activation(
                            out=SBF[:, d, :, hs].rearrange("q (w r) -> q w r", w=4),
                            in_=H0T[:, d * 64:d * 64 + 64].rearrange("q (r w) -> q w r", r=2)[:, gt * 4:gt * 4 + 4, :],
                            func=AF.Copy), reads=[("h0t",)], writes=[("sbf", d)])
                    else:
                        P.op("act", lambda e, fin=fin, d=d, c0=c0: e.activation(
                            out=SBF[:, d, :, :].rearrange("q i (s c) -> q i s c", s=4)[:, :, :, c0:c0 + 32],
                            in_=dat_rows(fin, 0, 8, d), func=AF.Copy), reads=fkeys, writes=[("sbf", d)])
                        pos = 47 if d == 0 else 0
                        P.op("act", lambda e, fin=fin, d=d, pos=pos, gt=gt: e.activation(
                            out=FS[:, :, d, :, gt * 4:gt * 4 + 4].rearrange("q s r w -> q w r s"),
                            in_=fin.rearrange("q (w r s c) -> q w r s c", w=4, r=2, s=4)[:, :, :, :, pos], func=AF.Copy),
                            reads=fkeys, writes=[("fs",)])

            def firc(gt):
                yb = 4 + 2 * (gt % 2)
                P.op("sp", lambda e: e.dma_start(out=SWCt[:, :], in_=swb_d[gt, :, OFF_C:OFF_C + 6528]), reads=[("swb", gt)],
                     writes=[("swC",)], dma=True, slot="swC")
                for j in range(8):
                    bank = yb + j // 4
                    col = (j % 4) * 128
                    for i in range(8):
                        L = 0 if i == j else (j - i if i < j else 7 + (i - j))
                        P.op("pe", lambda e, bank=bank, col=col, L=L, i=i: e.matmul(
                            PS[bank][:, col:col + 128], lhsT=KTw[:, L, :], rhs=UT[:, gt, i, :], start=(i == 0), stop=False),
                            reads=[("swC",), ("ut", gt, 0), ("ut", gt, 1)], writes=kps(bank))
                    for w in range(4):
                        for d in range(2):
                            for ri in range(2):
                                n = j + 1 if d == 0 else 8 - j
                                idx = 2 * w + ri
                                c0 = 0 if d == 0 else 1
                                if u == 0:
                                    rhs = SBF[:, d, idx, c0:c0 + 128]
                                else:
                                    rhs = SBF[:, d, idx, :].rearrange("q (s c) -> q s c", s=4)[:, :, c0:c0 + 32]
                                last = (d == 1 and ri == 1)
                                P.op("pe", lambda e, bank=bank, col=col, w=w, d=d, ri=ri, n=n, rhs=rhs, last=last: e.matmul(
                                    PS[bank][32 * w:32 * w + 32, col:col + 128], lhsT=Cst[:, n, d, ri, 32 * w:32 * w + 32],
                                    rhs=rhs, start=False, stop=last, tile_position=(0, 32 * w)),
                                    reads=[("swC",), ("sbf", 0), ("sbf", 1)], writes=kps(bank))

            def yevac(gt):
                yb = 4 + 2 * (gt % 2)
                for jb in range(2):
                    bank = yb + jb
                    inap = PS[bank][:, :].rearrange("q (j c) -> q j c", j=4)
                    P.op("act", lambda e, jb=jb, inap=inap: e.activation(
                        out=YG[:, gt, :].rearrange("q (c j) -> q j c", j=TC)[:, jb * 4:jb * 4 + 4, :], in_=inap,
                        func=AF.Gelu_apprx_tanh), reads=kps(bank), writes=[("yg", gt)])
                    P.op("act", lambda e, jb=jb, inap=inap: e.activation(
                        out=YGB[:, gt, :].rearrange("q (c j) -> q j c", j=TC)[:, jb * 4:jb * 4 + 4, :], in_=inap,
                        func=AF.Gelu_apprx_tanh), reads=kps(bank), writes=[("ygb", gt)])

            emit_V(0)
            load_mu(0)
            evac_V(0)
            emit_V(1)
            for gt in range(8):
                scan(gt)
                if gt < 7:
                    load_mu(gt + 1)
                    evac_V(gt + 1)
                if u == 0:
                    for jb in (8 + 2 * gt, 9 + 2 * gt):
                        mod_block(jb)
                firc(gt)
                if gt < 6:
                    emit_V(gt + 2)
                yevac(gt)
            if u == 0:
                mod_finish_f()
            if u == 1:
                for sq in range(4):
                    b = hi_bank()
                    P.op("pe", lambda e, sq=sq, b=b: e.transpose(out=PS[b][:, 0:128], in_=FSt[:, sq * 128:(sq + 1) * 128],
                                                                 identity=IDF[:, :]), reads=[("fs",), ("idf",)], writes=kps(b))
                    P.op("dve", lambda e, sq=sq, b=b: e.tensor_copy(out=T1[:, sq * 128:(sq + 1) * 128], in_=PS[b][:, 0:128]),
                         reads=kps(b), writes=[("t1",)])
                    P.op("sp", lambda e, sq=sq: e.dma_start(
                        out=ns_out[sq].rearrange("d r (w t) p -> (d r w) (t p)", t=2), in_=T1[:, sq * 128:(sq + 1) * 128]),
                        reads=[("t1",)], dma=True, slot=("ns", sq))
            P.barrier()
            if debug == "ph4" and u == int(debug_unit):
                return "stop"
            SIGT = [XT[0][:, 0:512], XT[0][:, 512:1024]]
            SQT = [XT[0][:, 1024:1536], XT[0][:, 1536:2048], XT[1][:, 1024:1536], XT[1][:, 1536:2048]]
            pend = []

            def flush_pend(keep):
                while len(pend) > keep:
                    ti_, th_, o_ = pend.pop(0)
                    P.op("pe", lambda e, ti_=ti_, th_=th_, o_=o_: e.matmul(
                        PS[6 + th_][:, :], lhsT=ONESF[:, :], rhs=SQT[ti_], start=(o_ == 0), stop=(o_ == 7)),
                        reads=[("sqt", ti_), ("onesf",)], writes=kps(6 + th_))
            RB = XT[1][:, 0:1024]
            tcount = 0
            for j in range(2):
                wb = w_acquire(W["glu"][j])
                wv = wview(wb, "k8")
                for cc in range(4):
                    o = j * 4 + cc
                    for th in range(2):
                        b = big_bank()
                        for k in range(8):
                            P.op("pe", lambda e, k=k, b=b, wv=wv, cc=cc, th=th: e.matmul(
                                PS[b][:, :], lhsT=wv[:, k, cc * 128:(cc + 1) * 128], rhs=YGB[:, k, th * 512:(th + 1) * 512],
                                start=(k == 0), stop=(k == 7)), reads=[("wb", wb)] + [("ygb", g) for g in range(8)],
                                writes=kps(b))
                        ti = tcount % 2
                        qi = tcount % 4
                        tcount += 1
                        P.op("act", lambda e, b=b, ti=ti, o=o: e.activation(out=SIGT[ti], in_=PS[b][:, :], func=AF.Sigmoid,
                                                                          bias=BGL[:, o:o + 1], scale=1.0),
                             reads=kps(b) + [("vt1",)], writes=[("sigt", ti)])
                        ysl = YG[:, o, th * 512:(th + 1) * 512]
                        P.op("dve", lambda e, ysl=ysl, ti=ti: e.tensor_tensor(out=ysl, in0=ysl, in1=SIGT[ti], op=ALU.mult),
                             reads=[("sigt", ti), ("yg", o)], writes=[("yg", o)])
                        P.op("pool", lambda e, ysl=ysl, qi=qi: e.tensor_tensor(out=SQT[qi], in0=ysl, in1=ysl, op=ALU.mult),
                             reads=[("yg", o)], writes=[("sqt", qi)])
                        flush_pend(2)
                        pend.append((qi, th, o))
            flush_pend(0)
            for th in range(2):
                P.op("act", lambda e, th=th: e.activation(out=RB[:, th * 512:(th + 1) * 512], in_=PS[6 + th][:, :],
                                                          func=AF.Sqrt, bias=EPSC[:, 0:1], scale=1.0 / 1024.0),
                     reads=kps(6 + th) + [("epsc",)], writes=[("rb", th)])
                P.op("dve", lambda e, th=th: e.reciprocal(out=RB[:, th * 512:(th + 1) * 512], in_=RB[:, th * 512:(th + 1) * 512]),
                     reads=[("rb", th)], writes=[("rb", th)])
            for o in range(8):
                P.op("dve", lambda e, o=o: e.scalar_tensor_tensor(
                    out=HT[:, 8 + o, :], in0=YG[:, o, :], scalar=GOB[:, o:o + 1], in1=RB, op0=ALU.mult, op1=ALU.mult),
                    reads=[("yg", o), ("rb", 0), ("rb", 1), ("vt1",)], writes=[("ht", t) for t in range(8)])
            P.barrier()
            if debug == "ph5" and u == int(debug_unit):
                return "stop"
            MO = BIG[:, :].rearrange("p (t n) -> p t n", t=8)
            for j in range(4):
                wb = w_acquire(W["wout"][j])
                wv = wview(wb, "k16")
                for t in range(8):
                    b = big_bank()
                    for k in range(16):
                        P.op("pe", lambda e, k=k, t=t, b=b, wv=wv: e.matmul(
                            PS[b][:, :], lhsT=HT[:, k, t * 128:(t + 1) * 128], rhs=wv[:, k, :],
                            start=(k == 0), stop=(k == 15)), reads=[("wb", wb), ("ht", t)], writes=kps(b))
                    P.op("act", lambda e, t=t, b=b, j=j: e.activation(out=MO[:, t, j * 512:(j + 1) * 512], in_=PS[b][:, :],
                                                                      func=AF.Copy), reads=kps(b), writes=[("mo", t, j)])
                    P.op("dve", lambda e, t=t, j=j: e.bn_stats(out=BNS[:, t, j, :], in_=MO[:, t, j * 512:(j + 1) * 512]),
                         reads=[("mo", t, j)], writes=[("bns", t, j)])

            def gate_row(dst, gvec, j0, key):
                P.op("sp", lambda e: e.dma_start(out=dst, in_=gvec.partition_broadcast(128)), writes=[key], dma=True, slot="c0")
                GBT = [XT[1][:, 1024:1152], XT[1][:, 1152:1280]]
                for k in range(16):
                    gi = k % 2
                    b = 4 + (k // 4) % 4
                    P.op("dve", lambda e, gi=gi, k=k: e.tensor_copy(out=GBT[gi], in_=MODT[:, j0 + k, v:v + 1].to_broadcast([128, 128])),
                         reads=modk(j0), writes=[("gbt", gi), ("xt", 1)])
                    P.op("pe", lambda e, gi=gi, k=k, b=b: e.matmul(PS[b][:, (k % 4) * 128:(k % 4 + 1) * 128], lhsT=GBT[gi],
                                                                   rhs=IDF[:, :], start=True, stop=True),
                         reads=[("gbt", gi), ("idf",), ("xt", 1)], writes=kps(b))
                    if k % 4 == 3:
                        k4 = k // 4
                        P.op("dve", lambda e, b=b, k4=k4: e.tensor_tensor(
                            out=dst[:, k4 * 512:(k4 + 1) * 512], in0=PS[b][:, :], in1=dst[:, k4 * 512:(k4 + 1) * 512], op=ALU.mult),
                            reads=kps(b) + [key], writes=[key])

            gate_row(GMROW, g_post_mix, 32, ("gmrow",))
            if u == 0:
                sincos_table(POSC, ("posc",), PIDX[:, 0:1], 0, PTMPA, PTMPB, ("ptmp",))
            for t in range(8):
                mok = [("mo", t, j) for j in range(4)]
                P.op("dve", lambda e, t=t: e.tensor_tensor(out=MO[:, t, :], in0=MO[:, t, :], in1=GMROW, op=ALU.mult),
                     reads=mok + [("gmrow",)] + [("bns", t, c4) for c4 in range(4)], writes=mok)

            def res_s1(t):
                c0 = 80 + t * 8
                P.op("dve", lambda e: e.bn_aggr(out=SMALL[:, c0:c0 + 2], in_=BNS[:, t, :, :].rearrange("p a b -> p (a b)")),
                     reads=[("bns", t, c4) for c4 in range(4)], writes=[("sm", c0)])
                P.op("dve", lambda e: e.scalar_tensor_tensor(
                    out=smcol(c0 + 2), in0=smcol(c0), scalar=smcol(c0), in1=smcol(c0 + 1), op0=ALU.mult, op1=ALU.add),
                    reads=[("sm", c0)], writes=[("sm", c0 + 2)])
                P.op("act", lambda e: e.activation(out=smcol(c0 + 3), in_=smcol(c0 + 2), func=AF.Sqrt,
                                                   bias=EPSC[:, 0:1], scale=1.0),
                     reads=[("sm", c0 + 2), ("epsc",)], writes=[("sm", c0 + 3)])
                P.op("dve", lambda e: e.reciprocal(out=smcol(c0 + 4), in_=smcol(c0 + 3)),
                     reads=[("sm", c0 + 3)], writes=[("sm", c0 + 4)])
                xb_i = t % 2
                load_x_tile(u, t, xb_i, xsrc)
                mok = [("mo", t, j) for j in range(4)]
                P.op("dve", lambda e: e.scalar_tensor_tensor(
                    out=XT[xb_i][:, :], in0=MO[:, t, :], scalar=smcol(c0 + 4), in1=XT[xb_i][:, :], op0=ALU.mult, op1=ALU.add),
                    reads=mok + [("sm", c0 + 4), ("xt", xb_i)], writes=[("xt", xb_i)])
                P.op("sp", lambda e: e.dma_start(out=y_out[u][t * 128:(t + 1) * 128, :], in_=XT[xb_i][:, :]),
                     reads=[("xt", xb_i)], writes=[("ydram", t)], dma=True, slot=("yo", xb_i))
                for c4 in range(4):
                    P.op("dve", lambda e, c4=c4: e.bn_stats(out=BNS[:, t, c4, :], in_=XT[xb_i][:, c4 * 512:(c4 + 1) * 512]),
                         reads=[("xt", xb_i)], writes=[("bns", t, c4)])
                c1 = 144 + t * 8
                P.op("dve", lambda e: e.bn_aggr(out=SMALL[:, c1:c1 + 2], in_=BNS[:, t, :, :].rearrange("p a b -> p (a b)")),
                     reads=[("bns", t, c4) for c4 in range(4)], writes=[("sm", c1)])
                P.op("dve", lambda e: e.scalar_tensor_tensor(
                    out=smcol(c1 + 2), in0=smcol(c1), scalar=smcol(c1), in1=smcol(c1 + 1), op0=ALU.mult, op1=ALU.add),
                    reads=[("sm", c1)], writes=[("sm", c1 + 2)])
                P.op("act", lambda e: e.activation(out=smcol(c1 + 3), in_=smcol(c1 + 2), func=AF.Sqrt,
                                                   bias=EPSC[:, 0:1], scale=1.0),
                     reads=[("sm", c1 + 2), ("epsc",)], writes=[("sm", c1 + 3)])
                P.op("dve", lambda e: e.reciprocal(out=smcol(c1 + 4), in_=smcol(c1 + 3)),
                     reads=[("sm", c1 + 3)], writes=[("sm", c1 + 4)])
                P.op("act", lambda e: e.activation(out=MO[:, t, :], in_=XT[xb_i][:, :], func=AF.Copy, scale=smcol(c1 + 4)),
                     reads=[("xt", xb_i), ("sm", c1 + 4)], writes=mok)

            def res_s2(t):
                mok = [("mo", t, j) for j in range(4)]
                tb = (t % 2) * 2
                for kg in range(4):
                    b = tb + kg % 2
                    for jq in range(4):
                        k = kg * 4 + jq
                        P.op("pe", lambda e, k=k, b=b, jq=jq: e.transpose(
                            out=PS[b][:, jq * 128:(jq + 1) * 128], in_=MO[:, t, k * 128:(k + 1) * 128], identity=IDF[:, :]),
                            reads=mok + [("idf",)], writes=kps(b))
                    for jq in range(4):
                        k = kg * 4 + jq
                        eng = ev_eng()
                        if eng == "act":
                            P.op("act", lambda e, k=k, b=b, jq=jq: e.activation(
                                out=HT[:, k, t * 128:(t + 1) * 128], in_=PS[b][:, jq * 128:(jq + 1) * 128], func=AF.Identity,
                                scale=SCF[:, k, v:v + 1], bias=MODT[:, 48 + k, v:v + 1]),
                                reads=kps(b) + [("scf",)] + modk(48), writes=[("ht", t)])
                        else:
                            P.op("dve", lambda e, k=k, b=b, jq=jq: e.tensor_scalar(
                                out=HT[:, k, t * 128:(t + 1) * 128], in0=PS[b][:, jq * 128:(jq + 1) * 128],
                                scalar1=SCF[:, k, v:v + 1], scalar2=MODT[:, 48 + k, v:v + 1], op0=ALU.mult, op1=ALU.add),
                                reads=kps(b) + [("scf",)] + modk(48), writes=[("ht", t)])

            res_s1(0)
            for t in range(8):
                if t < 7:
                    res_s1(t + 1)
                res_s2(t)
            P.barrier()
            if debug == "ph6" and u == int(debug_unit):
                return "stop"
            Fv = BIG[:, :].rearrange("p (t n) -> p t n", t=8)
            AT = VHR[:, :].rearrange("p (a c n) -> p a c n", a=2, c=4)
            RT = [XT[0][:, 0:512], XT[0][:, 512:1024], XT[0][:, 1024:1536], XT[0][:, 1536:2048]]
            rcount = 0
            gate_row(GFROW, g_post_ffn, 80, ("gfrow",))
            for g in range(16):
                i1, i2 = W["ffn"][g]
                wb1 = w_acquire(i1)
                wv1 = wview(wb1, "k16")
                ab = g % 2
                for cc in range(4):
                    for th in range(2):
                        b = big_bank()
                        for k in range(16):
                            P.op("pe", lambda e, k=k, b=b, wv1=wv1, cc=cc, th=th: e.matmul(
                                PS[b][:, :], lhsT=wv1[:, k, cc * 128:(cc + 1) * 128], rhs=HT[:, k, th * 512:(th + 1) * 512],
                                start=(k == 0), stop=(k == 15)),
                                reads=[("wb", wb1)] + [("ht", th * 4 + i) for i in range(4)], writes=kps(b))
                        ri_ = rcount % 4
                        rcount += 1
                        P.op("act", lambda e, b=b, ri_=ri_: e.activation(out=RT[ri_], in_=PS[b][:, :], func=AF.Relu),
                             reads=kps(b), writes=[("rt", ri_)])
                        P.op("pool" if rcount % 2 else "dve", lambda e, ri_=ri_, ab=ab, cc=cc, th=th: e.tensor_tensor(
                            out=AT[:, ab, cc, th * 512:(th + 1) * 512], in0=RT[ri_], in1=RT[ri_], op=ALU.mult),
                            reads=[("rt", ri_)], writes=[("at", ab, cc, th)])
                wb2 = w_acquire(i2)
                wv2 = wview(wb2, "c4")
                for t in range(8):
                    for nh in range(4):
                        b = hi_bank()
                        for cc in range(4):
                            P.op("pe", lambda e, t=t, nh=nh, b=b, cc=cc, wv2=wv2, ab=ab: e.matmul(
                                PS[b][:, :], lhsT=AT[:, ab, cc, t * 128:(t + 1) * 128], rhs=wv2[:, cc, nh * 512:(nh + 1) * 512],
                                start=(cc == 0), stop=(cc == 3)),
                                reads=[("wb", wb2)] + [("at", ab, c_, t // 4) for c_ in range(4)], writes=kps(b))
                        fsl = Fv[:, t, nh * 512:(nh + 1) * 512]
                        if g == 0:
                            P.op("act", lambda e, b=b, fsl=fsl: e.activation(out=fsl, in_=PS[b][:, :], func=AF.Copy),
                                 reads=kps(b), writes=[("f", t, nh)])
                        else:
                            P.op("dve", lambda e, b=b, fsl=fsl: e.tensor_tensor(out=fsl, in0=PS[b][:, :], in1=fsl, op=ALU.add),
                                 reads=kps(b) + [("f", t, nh)], writes=[("f", t, nh)])
            P.barrier()
            for t in range(8):
                fk = [("f", t, nh) for nh in range(4)]
                for c4 in range(4):
                    P.op("dve", lambda e, t=t, c4=c4: e.bn_stats(out=BNS[:, t, c4, :], in_=Fv[:, t, c4 * 512:(c4 + 1) * 512]),
                         reads=fk, writes=[("bns", t, c4)])
                c2_ = 208 + t * 5
                P.op("dve", lambda e, t=t, c2_=c2_: e.bn_aggr(out=SMALL[:, c2_:c2_ + 2], in_=BNS[:, t, :, :].rearrange("p a b -> p (a b)")),
                     reads=[("bns", t, c4) for c4 in range(4)], writes=[("sm", c2_)])
                P.op("dve", lambda e, c2_=c2_: e.scalar_tensor_tensor(
                    out=smcol(c2_ + 2), in0=smcol(c2_), scalar=smcol(c2_), in1=smcol(c2_ + 1), op0=ALU.mult, op1=ALU.add),
                    reads=[("sm", c2_)], writes=[("sm", c2_ + 2)])
                P.op("act", lambda e, c2_=c2_: e.activation(out=smcol(c2_ + 3), in_=smcol(c2_ + 2), func=AF.Sqrt,
                                                            bias=EPSC[:, 0:1], scale=1.0),
                     reads=[("sm", c2_ + 2), ("epsc",)], writes=[("sm", c2_ + 3)])
                P.op("dve", lambda e, c2_=c2_: e.reciprocal(out=smcol(c2_ + 4), in_=smcol(c2_ + 3)),
                     reads=[("sm", c2_ + 3)], writes=[("sm", c2_ + 4)])
                xb_i = t % 2
                FT = VHF[:, xb_i * 2048:(xb_i + 1) * 2048]
                P.op("sp", lambda e, t=t, FT=FT: e.dma_start(out=FT, in_=y_out[u][t * 128:(t + 1) * 128, :]),
                     reads=[("ydram", t)], writes=[("ft", xb_i)], dma=True, slot=("ft", xb_i))
                P.op("dve", lambda e, t=t, c2_=c2_: e.scalar_tensor_tensor(
                    out=Fv[:, t, :], in0=Fv[:, t, :], scalar=smcol(c2_ + 4), in1=GFROW, op0=ALU.mult, op1=ALU.mult),
                    reads=fk + [("sm", c2_ + 4), ("gfrow",)], writes=fk)
                P.op("pool", lambda e, t=t, FT=FT: e.tensor_tensor(out=FT, in0=FT, in1=Fv[:, t, :], op=ALU.add),
                     reads=fk + [("ft", xb_i)], writes=[("ft", xb_i)])
                P.op("sp", lambda e, t=t, FT=FT: e.dma_start(out=y_out[u][t * 128:(t + 1) * 128, :], in_=FT),
                     reads=[("ft", xb_i)], writes=[("ydram", t)], dma=True, slot=("yo2", xb_i))
            if u == 1:
                P.barrier()
            return None

        debug_unit = 0
        if debug is not None and ":" in debug:
            debug, debug_unit = debug.split(":")
        stop = None
        for u in range(2):
            stop = unit(u)
            if stop:
                break
        if debug == "ph4":
            dbg_outs["big"] = dout("dbg_big", [128, 8192])
            P.op("sp", lambda e: e.dma_start(out=dbg_outs["big"], in_=BIG[:, 8192:16384]), dma=True, slot="dbg2")
            P.emit()
            return nc
        if debug in ("ph1", "ph2", "ph3"):
            dbg_outs["ht"] = dout("dbg_ht", [128, 16 * 1024], BF16)
            P.barrier()
            P.op("sp", lambda e: e.dma_start(out=dbg_outs["ht"], in_=HT[:, :, :].rearrange("p k n -> p (k n)")),
                 dma=True, slot="dbg")
            if debug != "ph1":
                dbg_outs["big"] = dout("dbg_big", [128, 16384])
                P.op("sp", lambda e: e.dma_start(out=dbg_outs["big"], in_=BIG[:, :]), dma=True, slot="dbg2")
                dbg_outs["ut"] = dout("dbg_ut", [128, 8192], BF16)
                P.op("sp", lambda e: e.dma_start(out=dbg_outs["ut"], in_=UT[:, :, :, :].rearrange("p g j c -> p (g j c)")),
                     dma=True, slot="dbg3")
            P.emit()
            return nc
        if debug in ("ph5", "ph6"):
            dbg_outs["ht"] = dout("dbg_ht", [128, 16 * 1024], BF16)
            P.op("sp", lambda e: e.dma_start(out=dbg_outs["ht"], in_=HT[:, :, :].rearrange("p k n -> p (k n)")),
                 dma=True, slot="dbg")
        P.emit()
    return nc


def _consts():
    ident = np.eye(128, dtype=np.float32)
    q = np.arange(128)
    pm = np.stack([((q // 16) % 2 == 0), ((q // 16) % 2 == 1)], axis=1).astype(np.float32)
    pairm = (q[:, None] // 32 == np.arange(4)[None, :]).astype(np.float32)
    sel = np.zeros((64, 8, 128), np.float32)
    for gt in range(8):
        for qq in range(128):
            sel[gt * 8 + qq // 16, gt, qq] = 1.0
    return dict(k_ident=ident, k_pm=pm, k_pairm=pairm, k_sel=sel)


def make_in_maps(inp):
    f = lambda a: np.ascontiguousarray(np.asarray(a, dtype=np.float32))
    shared = {}
    for k in ["w_ada", "b_ada", "g_pre_mix", "w_in", "chunk_w_s", "chunk_b_s", "chunk_g_v", "ssm_lam_re", "ssm_lam_im",
              "ssm_log_dt", "ssm_b_re", "ssm_b_im", "ssm_c_re", "ssm_c_im", "ssm_d", "w_glu", "b_glu", "g_out_a",
              "g_out_b", "w_out", "g_post_mix", "g_pre_ffn", "w_ff1", "w_ff2", "g_post_ffn"]:
        shared[k] = f(inp[k])[0]
    shared.update(_consts())
    xs = f(inp["x_sample"]); xp = f(inp["x_prompt"]); stt = f(inp["state_ssm"]); c = f(inp["c"]); cc = f(inp["c_ctx"])
    maps = []
    for i in range(NCORES):
        m = dict(shared)
        m["xs"] = xs[i]
        m["xp"] = np.ascontiguousarray(xp[4 * i:4 * i + 4].reshape(1024, 2048))
        m["st0"] = np.ascontiguousarray(stt[i, 0])
        m["c2"] = np.ascontiguousarray(np.stack([c[i], cc]))
        maps.append(m)
    return maps


_NC_CACHE = {}


def kernel(**inputs):
    if "nc" not in _NC_CACHE:
        _NC_CACHE["nc"] = build()
    nc = _NC_CACHE["nc"]
    maps = make_in_maps(inputs)
    res = run_bass_kernel_spmd(nc, maps, core_ids=list(range(NCORES)))
    r = res.results
    y_sample = np.stack([r[i]["ys"] for i in range(NCORES)]).astype(np.float32)
    y_prompt = np.concatenate([r[i]["yp"].reshape(4, 256, 2048) for i in range(NCORES)]).astype(np.float32)
    ns = np.concatenate([r[i]["ns"].reshape(4, 1, 2, 2, 64, 64) for i in range(NCORES)]).astype(np.float32)
    return (y_prompt, y_sample, ns)
```

```python
import contextlib
import math
import os
_SKIP = os.environ.get('KSKIP', '').split(',')
import numpy as np
import concourse.bass as bass
import concourse.mybir as mybir
from concourse.bass_utils import run_bass_kernel_spmd

F32 = mybir.dt.float32
BF16 = mybir.dt.bfloat16
I32 = mybir.dt.int32
AF = mybir.ActivationFunctionType
ALU = mybir.AluOpType
AX = mybir.AxisListType

EPS = 1e-6
NCORES = 8
TC = 8
NCH = 128
PADL = 64
TWO_PI = 2.0 * math.pi


class Prog:
    ENG = ("pe", "act", "dve", "pool", "sp")

    def __init__(self, nc):
        self.nc = nc
        self.ops = []
        self.last_w = {}
        self.readers = {}
        self.slot_last = {}
        self.last_on_eng = {}
        self.pending_barrier = {}

    def op(self, eng, fn, reads=(), writes=(), dma=False, slot=None, extra_deps=()):
        i = len(self.ops)
        deps = set(extra_deps)
        for k in reads:
            j = self.last_w.get(k)
            if j is not None:
                deps.add(j)
        for k in writes:
            j = self.last_w.get(k)
            if j is not None:
                deps.add(j)
            for r in self.readers.get(k, ()):
                deps.add(r)
        if dma:
            assert slot is not None
            j = self.slot_last.get(slot)
            if j is not None:
                deps.add(j)
            self.slot_last[slot] = i
        if eng in self.pending_barrier:
            deps |= self.pending_barrier.pop(eng)
        deps.discard(i)
        if eng == "pe":
            deps = {j for j in deps if not (self.ops[j]["eng"] == "pe" and not self.ops[j]["dma"])}
        best = {}
        keep = set()
        for j in deps:
            oj = self.ops[j]
            if oj["dma"]:
                keep.add(j)
            else:
                best[oj["eng"]] = max(best.get(oj["eng"], -1), j)
        deps = keep | set(best.values())
        self.ops.append(dict(eng=eng, fn=fn, deps=deps, dma=dma, slot=slot))
        for k in writes:
            self.last_w[k] = i
            self.readers[k] = []
        for k in reads:
            if k not in writes:
                lst = self.readers.setdefault(k, [])
                if not dma:
                    lst[:] = [r for r in lst if self.ops[r]["dma"] or self.ops[r]["eng"] != eng]
                lst.append(i)
        self.last_on_eng[eng] = i
        return i

    def barrier(self):
        deps = set(self.last_on_eng.values()) | set(self.slot_last.values())
        for e in self.ENG:
            self.pending_barrier[e] = set(deps) | self.pending_barrier.get(e, set())

    def emit(self, final_wait_eng="sp"):
        nc = self.nc
        ops = self.ops
        n = len(ops)
        needed = [False] * n
        for o in ops:
            for j in o["deps"]:
                needed[j] = True
        for i, o in enumerate(ops):
            if o["dma"]:
                needed[i] = True
        slots = sorted({o["slot"] for o in ops if o["dma"]}, key=str)
        LIMIT = 1000
        with contextlib.ExitStack() as st:
            tok = [None] * n
            ecount = {e: 0 for e in self.ENG}
            scount = {s: 0 for s in slots}
            for i, o in enumerate(ops):
                if o["dma"]:
                    scount[o["slot"]] += 16
                    tok[i] = (("d", o["slot"]), scount[o["slot"]])
                elif needed[i]:
                    c = ecount[o["eng"]]
                    ecount[o["eng"]] += 1
                    tok[i] = (("e", o["eng"]), (c // LIMIT) * 1000000 + (c % LIMIT) + 1)
            self.signal_counts = dict(ecount)
            esem = {}
            for e in self.ENG:
                for ep in range(ecount[e] // LIMIT + 1):
                    esem[(e, ep)] = st.enter_context(nc.semaphore("s_%s_%d" % (e, ep)))
            ssem = {s: st.enter_context(nc.semaphore("d_%d" % k)) for k, s in enumerate(slots)}

            def sem_and_val(key, val):
                if key[0] == "e":
                    return esem[(key[1], val // 1000000)], val % 1000000
                return ssem[key[1]], val

            block = st.enter_context(nc.Block())
            handles = {"pe": "tensor", "act": "scalar", "dve": "vector", "pool": "gpsimd", "sp": "sync"}

            def run_engine(ename, eng):
                waited = {}
                for i, o in enumerate(ops):
                    if o["eng"] != ename:
                        continue
                    for j in sorted(o["deps"]):
                        if tok[j] is None:
                            continue
                        key, val = tok[j]
                        if key == ("e", ename) and ename == "pe":
                            continue
                        if waited.get(key, 0) < val:
                            sm, vv = sem_and_val(key, val)
                            eng.wait_ge(sm, vv)
                            waited[key] = val
                    inst = o["fn"](eng)
                    if tok[i] is not None:
                        key, val = tok[i]
                        assert inst is not None
                        sm, vv = sem_and_val(key, val)
                        inst.then_inc(sm, 16 if o["dma"] else 1)
                if ename == final_wait_eng:
                    for s_ in slots:
                        if waited.get(("d", s_), 0) < scount[s_]:
                            eng.wait_ge(ssem[s_], scount[s_])

            for ename in self.ENG:
                dec = getattr(block, handles[ename])

                def _mk(en):
                    def body(eng):
                        run_engine(en, eng)
                    return body
                dec(_mk(ename))
        return nc


def _ap(t, offset, dims):
    return bass.AP(tensor=t, offset=offset, ap=[list(d) for d in dims])


def build(debug=None):
    nc = bass.Bass("TRN2", target_bir_lowering=False)

    def din(name, shape, dt=F32):
        return nc.dram_tensor(name, list(shape), dt, kind="ExternalInput").ap()

    def dout(name, shape, dt=F32):
        return nc.dram_tensor(name, list(shape), dt, kind="ExternalOutput").ap()

    x_in = [din("xs", [1024, 2048]), din("xp", [1024, 2048])]
    st0 = din("st0", [2, 2, 64, 64])
    c2 = din("c2", [2, 2048])
    w_ada = din("w_ada", [2048, 12288])
    b_ada = din("b_ada", [12288])
    g_pre_mix = din("g_pre_mix", [2048])
    w_in = din("w_in", [2048, 3072])
    chunk_w_s = din("chunk_w_s", [8, 128, 128])
    chunk_b_s = din("chunk_b_s", [8, 128])
    chunk_g_v = din("chunk_g_v", [1024])
    lam_re = din("ssm_lam_re", [2, 64, 64])
    lam_im = din("ssm_lam_im", [2, 64, 64])
    log_dt = din("ssm_log_dt", [2, 64])
    b_re = din("ssm_b_re", [2, 64, 64, 16])
    b_im = din("ssm_b_im", [2, 64, 64, 16])
    c_re = din("ssm_c_re", [2, 64, 16, 64])
    c_im = din("ssm_c_im", [2, 64, 16, 64])
    ssm_d = din("ssm_d", [1024])
    w_glu = din("w_glu", [1024, 1024])
    b_glu = din("b_glu", [1024])
    g_out_a = din("g_out_a", [1024])
    g_out_b = din("g_out_b", [1024])
    w_out = din("w_out", [2048, 2048])
    g_post_mix = din("g_post_mix", [2048])
    g_pre_ffn = din("g_pre_ffn", [2048])
    w_ff1 = din("w_ff1", [2048, 8192])
    w_ff2 = din("w_ff2", [8192, 2048])
    g_post_ffn = din("g_post_ffn", [2048])
    ident_d = din("k_ident", [128, 128])
    pm_d = din("k_pm", [128, 2])
    pairm_d = din("k_pairm", [128, 4])
    sel_d = din("k_sel", [64, 8, 128])

    y_out = [dout("ys", [1024, 2048]), dout("yp", [1024, 2048])]
    ns_out = dout("ns", [4, 2, 2, 64, 64])

    NSW = 8 * 2 * 2 * 128 + 9 * 2 * 2 * 128 + 15 * 128
    OFF_B, OFF_C, OFF_K = 0, 4096, 4096 + 4608
    swb_d = nc.dram_tensor("swb", [8, 128, NSW], BF16, kind="Internal").ap()
    NMU = 4 * 2 * 7 * 2
    swm_d = nc.dram_tensor("swm", [8, 128, NMU], F32, kind="Internal").ap()

    dbg_outs = {}

    with contextlib.ExitStack() as st:
        def sb(name, shape, dt=F32):
            return st.enter_context(nc.sbuf_tensor(name, list(shape), dt))

        BIG = sb("BIG", [128, 16384], F32)
        HT = sb("HT", [128, 16, 1024], BF16)
        VHR = sb("VHR", [128, 8192], BF16)
        UT = sb("UT", [128, 8, TC, NCH], BF16)
        WB = [sb("WB%d" % i, [128, 8192], BF16) for i in range(3)]
        XT = [sb("XT%d" % i, [128, 2048], F32) for i in range(2)]
        IDF = sb("IDF", [128, 128], F32)
        IDB = sb("IDB", [128, 128], BF16)
        ONESF = sb("ONESF", [128, 128], F32)
        PMK = sb("PMK", [128, 2], F32)
        PAIRM = sb("PAIRM", [128, 4], F32)
        VT1 = sb("VT1", [128, 96], F32)
        VT2 = sb("VT2", [128, 96], F32)
        MODT = sb("MODT", [128, 96, 2], F32)
        SCM = sb("SCM", [128, 16, 2], F32)
        SCF = sb("SCF", [128, 16, 2], F32)
        SCT = sb("SCT", [128, 16, 2], BF16)
        WST = sb("WST", [128, 8, 128], BF16)
        BSP = sb("BSP", [128, 8], F32)
        EPSC = sb("EPSC", [128, 1], F32)
        SMALL = sb("SMALL", [128, 256], F32)
        BNS = sb("BNS", [128, 8, 4, 6], F32)
        PS = [st.enter_context(nc.psum_tensor("PS%d" % i, [128, 512], F32)) for i in range(8)]

        P = Prog(nc)
        cnt = {"big": 0, "hi": 0, "ev": 0}

        def big_bank():
            b = cnt["big"] % 4
            cnt["big"] += 1
            return b

        def hi_bank():
            b = 4 + cnt["hi"] % 4
            cnt["hi"] += 1
            return b

        def ev_eng():
            cnt["ev"] += 1
            return "act" if cnt["ev"] % 2 else "dve"

        def kps(b):
            return [("ps", b)]

        def smcol(i, n=1):
            return SMALL[:, i:i + n]

        wblocks = []

        def add_wblock(src, kind):
            wblocks.append((src, kind))
            return len(wblocks) - 1

        def wview(b, kind):
            t = WB[b]
            if kind == "k16":
                return t[:, :].rearrange("p (k n) -> p k n", k=16)
            if kind == "k8":
                return t[:, 0:4096].rearrange("p (k n) -> p k n", k=8)
            if kind == "c4":
                return t[:, :].rearrange("p (c n) -> p c n", c=4)
            raise ValueError(kind)

        wstate = {"issued": 0}

        def w_issue_upto(i):
            while wstate["issued"] <= min(i, len(wblocks) - 1):
                j = wstate["issued"]
                src, kind = wblocks[j]
                b = j % 3
                P.op("pool", lambda e, b=b, kind=kind, src=src: e.dma_start(out=wview(b, kind), in_=src),
                     writes=[("wb", b)], dma=True, slot=("wb", b))
                wstate["issued"] += 1

        def w_acquire(i):
            w_issue_upto(i + 2)
            return i % 3

        def ada_block(j):
            return add_wblock(w_ada[:, j * 512:(j + 1) * 512].rearrange("(k p) n -> p k n", p=128), "k16")

        wb_ada = [ada_block(j) for j in range(8)]
        wb_unit = []
        for u in range(2):
            d = {}
            d["win"] = [add_wblock(w_in[:, j * 512:(j + 1) * 512].rearrange("(k p) n -> p k n", p=128), "k16")
                        for j in range(6)]
            if u == 0:
                wb_ada += [ada_block(j) for j in range(8, 24)]
            d["glu"] = [add_wblock(w_glu[:, j * 512:(j + 1) * 512].rearrange("(k p) n -> p k n", p=128), "k8")
                        for j in range(2)]
            d["wout"] = [add_wblock(w_out[:, j * 512:(j + 1) * 512].rearrange("(k p) n -> p k n", p=128), "k16")
                         for j in range(4)]
            d["ffn"] = []
            for g in range(16):
                i1 = add_wblock(w_ff1[:, g * 512:(g + 1) * 512].rearrange("(k p) n -> p k n", p=128), "k16")
                i2 = add_wblock(w_ff2[g * 512:(g + 1) * 512, :].rearrange("(c p) n -> p c n", p=128), "c4")
                d["ffn"].append((i1, i2))
            wb_unit.append(d)

        def rstd_from_bn(slots_ap, nslots, out_col, scratch_col):
            mv = smcol(scratch_col, 2)
            P.op("dve", lambda e: e.bn_aggr(out=mv, in_=slots_ap), reads=[("bns",)], writes=[("sm", scratch_col)])
            P.op("dve", lambda e: e.scalar_tensor_tensor(out=smcol(scratch_col + 2), in0=mv[:, 0:1], scalar=mv[:, 0:1],
                                                         in1=mv[:, 1:2], op0=ALU.mult, op1=ALU.add),
                 reads=[("sm", scratch_col)], writes=[("sm", scratch_col + 2)])
            P.op("act", lambda e: e.activation(out=smcol(scratch_col + 3), in_=smcol(scratch_col + 2), func=AF.Sqrt,
                                               bias=EPSC[:, 0:1], scale=1.0),
                 reads=[("sm", scratch_col + 2)], writes=[("sm", scratch_col + 3)])
            P.op("dve", lambda e: e.reciprocal(out=smcol(out_col), in_=smcol(scratch_col + 3)),
                 reads=[("sm", scratch_col + 3)], writes=[("sm", out_col)])

        P.op("sp", lambda e: e.dma_start(out=IDF[:, :], in_=ident_d), writes=[("idf",)], dma=True, slot="c0")
        P.op("sp", lambda e: e.dma_start(out=PMK[:, :], in_=pm_d), writes=[("pmk",)], dma=True, slot="c1")
        P.op("sp", lambda e: e.dma_start(out=PAIRM[:, :], in_=pairm_d), writes=[("pairm",)], dma=True, slot="c2")
        P.op("dve", lambda e: e.tensor_copy(out=IDB[:, :], in_=IDF[:, :]), reads=[("idf",)], writes=[("idb",)])
        P.op("dve", lambda e: e.memset(ONESF[:, :], 1.0), writes=[("onesf",)])
        P.op("dve", lambda e: e.memset(EPSC[:, :], EPS), writes=[("epsc",)])

        VN1 = XT[0][0:96, 0:128]
        VN2 = XT[0][0:96, 128:256]
        vecs = [(c2.rearrange("v (k q) -> (v k) q", q=128), 0, 32),
                (g_pre_mix.rearrange("(k q) -> k q", q=128), 32, 16),
                (g_pre_ffn.rearrange("(k q) -> k q", q=128), 48, 16),
                (g_out_a.rearrange("(k q) -> k q", q=128), 64, 8),
                (g_out_b.rearrange("(k q) -> k q", q=128), 72, 8),
                (b_glu.rearrange("(k q) -> k q", q=128), 80, 8),
                (ssm_d.rearrange("(k q) -> k q", q=128), 88, 8)]
        for vi, (src, r0, nr) in enumerate(vecs):
            P.op("sp", lambda e, src=src, r0=r0, nr=nr: e.dma_start(out=XT[0][r0:r0 + nr, 0:128], in_=src),
                 writes=[("vn1", vi)], dma=True, slot="c%d" % (vi % 3))
        P.op("sp", lambda e: e.dma_start(out=VN2, in_=b_ada.rearrange("(r q) -> r q", q=128)),
             writes=[("vn2",)], dma=True, slot="c0")
        P.op("pe", lambda e: e.transpose(out=PS[7][:, 0:96], in_=VN1, identity=IDF[0:96, 0:96]),
             reads=[("vn1", i) for i in range(7)] + [("idf",)], writes=kps(7))
        P.op("dve", lambda e: e.tensor_copy(out=VT1[:, :], in_=PS[7][:, 0:96]), reads=kps(7), writes=[("vt1",)])
        P.op("pe", lambda e: e.transpose(out=PS[6][:, 0:96], in_=VN2, identity=IDF[0:96, 0:96]),
             reads=[("vn2",), ("idf",)], writes=kps(6))
        P.op("dve", lambda e: e.tensor_copy(out=VT2[:, :], in_=PS[6][:, 0:96]), reads=kps(6), writes=[("vt2",)])
        for v in range(2):
            P.op("act", lambda e, v=v: e.activation(out=SCT[:, :, v], in_=VT1[:, v * 16:(v + 1) * 16], func=AF.Silu),
                 reads=[("vt1",)], writes=[("sct", v)])
        GPM, GPF, GOA, GOB, BGL, DSK = (VT1[:, 32:48], VT1[:, 48:64], VT1[:, 64:72], VT1[:, 72:80],
                                        VT1[:, 80:88], VT1[:, 88:96])
        WSN = XT[1][:, 0:1024].rearrange("p (h q) -> p h q", h=8)
        P.op("sp", lambda e: e.dma_start(out=WSN, in_=chunk_w_s.rearrange("h p q -> p h q")),
             writes=[("wsn",)], dma=True, slot="c1")
        P.op("sp", lambda e: e.dma_start(out=XT[1][0:8, 1024:1152], in_=chunk_b_s),
             writes=[("bsn",)], dma=True, slot="c2")
        for h in range(8):
            b = hi_bank()
            P.op("pe", lambda e, h=h, b=b: e.transpose(out=PS[b][:, 0:128], in_=WSN[:, h, :], identity=IDF[:, :]),
                 reads=[("wsn",), ("idf",)], writes=kps(b))
            P.op("dve", lambda e, h=h, b=b: e.tensor_copy(out=WST[:, h, :], in_=PS[b][:, 0:128]),
                 reads=kps(b), writes=[("wst", h)])
        b = hi_bank()
        P.op("pe", lambda e, b=b: e.transpose(out=PS[b][:, 0:8], in_=XT[1][0:8, 1024:1152], identity=IDF[0:8, 0:8]),
             reads=[("bsn",), ("idf",)], writes=kps(b))
        P.op("dve", lambda e, b=b: e.tensor_copy(out=BSP[:, :], in_=PS[b][:, 0:8]), reads=kps(b), writes=[("bsp",)])

        H0T = sb("H0T", [128, 128], F32)
        BIGH = HT[:, :, :].rearrange("p k n -> p (k n)").bitcast(F32)
        _go = {"o": 0}

        _regions = [BIG[:, :], BIGH, UT[:, :, :, :].rearrange("p g j c -> p (g j c)").bitcast(F32),
                    VHR[:, :].bitcast(F32)]
        _regions += [XT[0][:, 512:2048], XT[1][:, 1152:2048]]
        _rsz = [16384, 8192, 4096, 4096, 1536, 896]

        _roff = [0] * 6

        def galloc(n):
            for r in range(len(_regions)):
                if _roff[r] + n <= _rsz[r]:
                    o = _roff[r]
                    _roff[r] += n
                    return _regions[r][:, o:o + n]
            raise AssertionError("gen scratch exhausted")

        _pw = [galloc(9 * 256) for _ in range(2)]
        G_TBC = galloc(9 * 256)
        G_CMS = [galloc(2304).bitcast(BF16) for _ in range(2)]
        G_CST = galloc(2304).bitcast(BF16)
        G_TBB = galloc(8 * 256)
        G_BST = galloc(2048).bitcast(BF16)
        G_BN = galloc(2048)
        G_CR = galloc(2048)
        G_TMPC = galloc(9 * 128)
        G_SEL = galloc(1024)
        G_TMPB = galloc(8 * 128)
        G_KT = galloc(960).bitcast(BF16)
        G_KS = galloc(512)
        G_MM = galloc(512)
        G_LN = galloc(258)
        G_BXS = [galloc(256).bitcast(BF16) for _ in range(2)]
        GS = []
        for _sx in range(2):
            GS.append(dict(PR=galloc(258), DT=galloc(2), T=[galloc(128) for _ in range(16)], BRP=galloc(256),
                           BB=galloc(256), PW=_pw[_sx]))
        G_K0 = galloc(128)
        G_MU = galloc(112)
        G_MT = galloc(64)

        def v3(ap, d0, d1):
            return ap.rearrange("q (a b) -> q a b", a=d0, b=d1)

        def ssm_param_loads():
            P.op("sp", lambda e: e.dma_start(out=G_SEL[0:64, :], in_=sel_d.rearrange("g t q -> g (t q)")),
                 writes=[("g_sel",)], dma=True, slot="pl_sel")
            for d in range(2):
                P.op("sp", lambda e, d=d: e.dma_start(out=G_LN[0:64, d * 128:d * 128 + 64], in_=lam_re[d]),
                     writes=[("g_ln", d, 0)], dma=True, slot=("pl_ln0", d))
                P.op("sp", lambda e, d=d: e.dma_start(out=G_LN[0:64, d * 128 + 64:d * 128 + 128], in_=lam_im[d]),
                     writes=[("g_ln", d, 1)], dma=True, slot=("pl_ln1", d))
                P.op("sp", lambda e, d=d: e.dma_start(out=G_LN[0:64, 256 + d:257 + d],
                                                     in_=log_dt[d].rearrange("(g o) -> g o", o=1)),
                     writes=[("g_ln", d, 2)], dma=True, slot=("pl_ln2", d))
                for ri, src in enumerate((b_re, b_im)):
                    dst = G_BN[64 * d:64 * d + 64, ri * 1024:(ri + 1) * 1024].rearrange("p (g h) -> p g h", g=64)
                    P.op("sp", lambda e, dst=dst, src=src, d=d: e.dma_start(out=dst, in_=src[d].rearrange("g p h -> p g h")),
                         writes=[("g_bn", d, ri)], dma=True, slot=("pl", d, ri, "1"))
                for ri, src in enumerate((c_re, c_im)):
                    dst = G_CR[:, (d * 2 + ri) * 512:(d * 2 + ri + 1) * 512].rearrange("q (t p) -> q t p", t=8)
                    P.op("sp", lambda e, dst=dst, src=src, d=d: e.dma_start(
                        out=dst, in_=src[d].rearrange("(t g) h p -> (g h) t p", t=8)),
                        writes=[("g_cr", d, ri)], dma=True, slot=("pl", d, ri, "2"))
            for d in range(2):
                sl = G_CR[:, (d * 2 + 1) * 512:(d * 2 + 2) * 512]
                P.op("dve", lambda e, sl=sl: e.tensor_scalar(out=sl, in0=sl, scalar1=-1.0, scalar2=None, op0=ALU.mult),
                     reads=[("g_cr", d, 1)], writes=[("g_cr", d, 1)])
            P.op("sp", lambda e: e.dma_start(out=XT[0][:, 256:384],
                                             in_=st0.rearrange("d r (w s) p -> (d r w) (s p)", s=2)),
                 writes=[("h0n",)], dma=True, slot="c0")
            P.op("pe", lambda e: e.transpose(out=PS[5][:, 0:128], in_=XT[0][:, 256:384], identity=IDF[:, :]),
                 reads=[("h0n",), ("idf",)], writes=kps(5))
            P.op("dve", lambda e: e.tensor_copy(out=H0T[:, :], in_=PS[5][:, 0:128]), reads=kps(5), writes=[("h0t",)])

        gparam_keys = [("g_sel",)] + [("g_ln", d, i) for d in range(2) for i in range(3)]
        gbn_keys = [("g_bn", d, r) for d in range(2) for r in range(2)]
        gcr_keys = [("g_cr", d, r) for d in range(2) for r in range(2)]

        def cmul(eng, o_r, o_i, a_r, a_i, b_r, b_i, t, rk, wk, neg_im=False):
            rk = list(rk) + list(wk)
            wk = list(wk)

            def tt(out, a, b, op):
                P.op(eng, lambda e: e.tensor_tensor(out=out, in0=a, in1=b, op=op), reads=rk, writes=wk)
            if neg_im:
                tt(t, a_i, b_i, ALU.mult)
                tt(o_r, a_r, b_r, ALU.mult)
                tt(o_r, o_r, t, ALU.add)
                tt(t, a_i, b_r, ALU.mult)
                tt(o_i, a_r, b_i, ALU.mult)
                tt(o_i, o_i, t, ALU.subtract)
                return
            tt(t, a_i, b_i, ALU.mult)
            tt(o_r, a_r, b_r, ALU.mult)
            tt(o_r, o_r, t, ALU.subtract)
            tt(t, a_i, b_r, ALU.mult)
            tt(o_i, a_r, b_i, ALU.mult)
            tt(o_i, o_i, t, ALU.add)

        def genA(gt):
            E = "dve"
            sx = gt % 2
            S = GS[sx]
            KG = ("gen", sx)
            G_PR, G_DT, G_BRP, G_BB, G_PW = S["PR"], S["DT"], S["BRP"], S["BB"], S["PW"]
            pb = 4 + (gt % 2) * 2

            G_BX = G_BXS[sx]
            G_CM = G_CMS[sx]
            kcm = ("g_cm", sx)
            PW = G_PW.rearrange("q (n d r p) -> q n d r p", n=9, d=2, r=2)
            PSB = [PS[b_][:, :].bitcast(BF16) for b_ in range(8)]
            def tt(eng, out, a, b, op):
                P.op(eng, lambda e: e.tensor_tensor(out=out, in0=a, in1=b, op=op), reads=[KG], writes=[KG])
            P.op("pe", lambda e: e.matmul(PS[pb][:, 0:258], lhsT=G_SEL[0:64, gt * 128:(gt + 1) * 128], rhs=G_LN[0:64, :],
                                          start=True, stop=True), reads=gparam_keys, writes=kps(pb))
            P.op("dve", lambda e: e.tensor_copy(out=G_PR, in_=PS[pb][:, 0:258]), reads=kps(pb), writes=[KG])
            PRv = G_PR[:, 0:256].rearrange("q (d r p) -> q d r p", d=2, r=2)
            LR, LI = PRv[:, :, 0, :], PRv[:, :, 1, :]
            P.op("act", lambda e: e.activation(out=G_DT, in_=G_PR[:, 256:258], func=AF.Exp), reads=[KG], writes=[KG])
            T = [v3(x, 2, 64) for x in S["T"]]
            A, Bq, MAG, R, SINB, COSB, LBR, LBI, DEN, INV, NR, CRr, CIi, TM, TM2, TM3 = T
            dtb = G_DT.unsqueeze(2).to_broadcast([128, 2, 64])
            tt(E, A, LR, dtb, ALU.mult)
            tt(E, Bq, LI, dtb, ALU.mult)
            P.op("act", lambda e: e.activation(out=MAG, in_=A, func=AF.Exp), reads=[KG], writes=[KG])
            TMi = TM.bitcast(I32)
            P.op(E, lambda e: e.tensor_scalar(out=TMi, in0=Bq, scalar1=1.0 / TWO_PI, scalar2=None, op0=ALU.mult),
                 reads=[KG], writes=[KG])
            P.op(E, lambda e: e.tensor_copy(out=TM2, in_=TMi), reads=[KG], writes=[KG])
            P.op(E, lambda e: e.scalar_tensor_tensor(out=R, in0=TM2, scalar=-TWO_PI, in1=Bq, op0=ALU.mult, op1=ALU.add),
                 reads=[KG], writes=[KG])
            P.op(E, lambda e: e.tensor_scalar(out=TM2, in0=R, scalar1=math.pi, scalar2=-TWO_PI, op0=ALU.is_gt,
                                              op1=ALU.mult), reads=[KG], writes=[KG])
            tt(E, R, R, TM2, ALU.add)
            P.op(E, lambda e: e.tensor_scalar(out=TM2, in0=R, scalar1=-math.pi, scalar2=TWO_PI, op0=ALU.is_lt,
                                              op1=ALU.mult), reads=[KG], writes=[KG])
            tt(E, R, R, TM2, ALU.add)
            P.op("act", lambda e: e.activation(out=SINB, in_=R, func=AF.Sin), reads=[KG], writes=[KG])
            P.op(E, lambda e: e.tensor_scalar(out=TM3, in0=R, scalar1=math.pi / 2, scalar2=None, op0=ALU.add),
                 reads=[KG], writes=[KG])
            P.op(E, lambda e: e.tensor_scalar(out=TM2, in0=TM3, scalar1=math.pi, scalar2=-TWO_PI, op0=ALU.is_gt,
                                              op1=ALU.mult), reads=[KG], writes=[KG])
            tt(E, TM3, TM3, TM2, ALU.add)
            P.op("act", lambda e: e.activation(out=COSB, in_=TM3, func=AF.Sin), reads=[KG], writes=[KG])
            tt(E, LBR, MAG, COSB, ALU.mult)
            tt(E, LBI, MAG, SINB, ALU.mult)
            tt(E, DEN, LR, LR, ALU.mult)
            tt(E, TM, LI, LI, ALU.mult)
            tt(E, DEN, DEN, TM, ALU.add)
            P.op(E, lambda e: e.reciprocal(out=INV, in_=DEN), reads=[KG], writes=[KG])
            P.op(E, lambda e: e.tensor_scalar(out=NR, in0=LBR, scalar1=-1.0, scalar2=None, op0=ALU.add),
                 reads=[KG], writes=[KG])
            tt(E, CRr, NR, LR, ALU.mult)
            tt(E, TM, LBI, LI, ALU.mult)
            tt(E, CRr, CRr, TM, ALU.add)
            tt(E, CRr, CRr, INV, ALU.mult)
            tt(E, CIi, LBI, LR, ALU.mult)
            tt(E, TM, NR, LI, ALU.mult)
            tt(E, CIi, CIi, TM, ALU.subtract)
            tt(E, CIi, CIi, INV, ALU.mult)
            for d in range(2):
                for ri in range(2):
                    src = G_BN[64 * d:64 * d + 64, ri * 1024 + gt * 128:ri * 1024 + (gt + 1) * 128]
                    bnk = pb + 1 - d
                    P.op("pe", lambda e, src=src, d=d, ri=ri, bnk=bnk: e.transpose(
                        out=PS[bnk][:, ri * 64:(ri + 1) * 64], in_=src,
                        identity=IDF[64 * d:64 * d + 64, 64 * d:64 * d + 64]),
                        reads=gbn_keys + [("idf",)], writes=kps(bnk))
            for d in range(2):
                P.op("dve", lambda e, d=d: e.tensor_copy(out=G_BRP[:, d * 128:(d + 1) * 128], in_=PS[pb + 1 - d][:, 0:128]),
                     reads=kps(pb + 1 - d), writes=[KG])
            BRv = G_BRP.rearrange("q (d r p) -> q d r p", d=2, r=2)
            BBv = G_BB.rearrange("q (d r p) -> q d r p", d=2, r=2)
            cmul(E, BBv[:, :, 0, :], BBv[:, :, 1, :], CRr, CIi, BRv[:, :, 0, :], BRv[:, :, 1, :], TM, [KG], [KG])
            PW = G_PW.rearrange("q (n d r p) -> q n d r p", n=9, d=2, r=2)
            P.op("pool", lambda e: e.memset(PW[:, 0, :, 0, :], 1.0), reads=[KG], writes=[KG])
            P.op("pool", lambda e: e.memset(PW[:, 0, :, 1, :], 0.0), reads=[KG], writes=[KG])
            P.op(E, lambda e: e.tensor_copy(out=PW[:, 1, :, 0, :], in_=LBR), reads=[KG], writes=[KG])
            P.op(E, lambda e: e.tensor_copy(out=PW[:, 1, :, 1, :], in_=LBI), reads=[KG], writes=[KG])
            TMPb = G_TMPB.rearrange("q (n d p) -> q n d p", n=8, d=2)
            for (lo, cnt_, m) in ((1, 1, 1), (1, 2, 2), (1, 4, 4)):
                br = PW[:, m, :, 0, :].unsqueeze(1).to_broadcast([128, cnt_, 2, 64])
                bi = PW[:, m, :, 1, :].unsqueeze(1).to_broadcast([128, cnt_, 2, 64])
                cmul(E, PW[:, m + lo:m + lo + cnt_, :, 0, :], PW[:, m + lo:m + lo + cnt_, :, 1, :],
                     PW[:, lo:lo + cnt_, :, 0, :], PW[:, lo:lo + cnt_, :, 1, :], br, bi, TMPb[:, 0:cnt_, :, :],
                     [KG, ("g_tbb",)], [KG, ("g_tbb",)])
            TBb = G_TBB.rearrange("q (n d r p) -> q n d r p", n=8, d=2, r=2)
            bbr = BBv[:, :, 0, :].unsqueeze(1).to_broadcast([128, 8, 2, 64])
            bbi = BBv[:, :, 1, :].unsqueeze(1).to_broadcast([128, 8, 2, 64])
            cmul("dve", TBb[:, :, :, 0, :], TBb[:, :, :, 1, :], PW[:, 0:8, :, 0, :], PW[:, 0:8, :, 1, :], bbr, bbi,
                 TMPb, [KG], [("g_tbb",)])
            BST = G_BST.rearrange("q (m s p) -> q m s p", m=32, s=2)
            for s2 in range(2):
                P.op("act", lambda e, s2=s2: e.activation(
                    out=BST[:, :, s2, :], in_=G_TBB.rearrange("q (m p) -> q m p", m=32), func=AF.Copy,
                    scale=PMK[:, s2:s2 + 1]), reads=[("g_tbb",), ("pmk",)], writes=[("g_bst",)])
            P.op("sp", lambda e: e.dma_start(out=swb_d[gt, :, OFF_B:OFF_B + 4096], in_=G_BST),
                 reads=[("g_bst",)], writes=[("swb", gt)], dma=True, slot="gen0")
            PSB = [PS[b][:, :].bitcast(BF16) for b in range(8)]
            BST6 = G_BST.rearrange("q (n d r m) -> q n d r m", n=8, d=2, r=2)
            for d in range(2):
                for ri in range(2):
                    P.op("pe", lambda e, d=d, ri=ri: e.transpose(
                        out=PSB[pb + 1][:, (d * 2 + ri) * 128:(d * 2 + ri + 1) * 128], in_=BST6[:, 0, d, ri, :],
                        identity=IDB[:, :]), reads=[("g_bst",), ("idb",)], writes=kps(pb + 1))
            P.op("act", lambda e: e.activation(out=G_BX, in_=PSB[pb + 1][:, 0:512], func=AF.Copy), reads=kps(pb + 1),
                 writes=[("g_bx",)])
        def genB(gt):
            E = "dve"
            sx = gt % 2
            S = GS[sx]
            KG = ("gen", sx)
            G_PR, G_DT, G_BRP, G_BB, G_PW = S["PR"], S["DT"], S["BRP"], S["BB"], S["PW"]
            pb = 4 + (gt % 2) * 2

            G_BX = G_BXS[sx]
            G_CM = G_CMS[sx]
            kcm = ("g_cm", sx)
            PW = G_PW.rearrange("q (n d r p) -> q n d r p", n=9, d=2, r=2)
            PSB = [PS[b_][:, :].bitcast(BF16) for b_ in range(8)]
            TBc = G_TBC.rearrange("q (n d r p) -> q n d r p", n=9, d=2, r=2)
            TMPc = G_TMPC.rearrange("q (n d p) -> q n d p", n=9, d=2)
            CRv = G_CR.rearrange("q (d r t p) -> q d r t p", d=2, r=2, t=8)
            cr = CRv[:, :, 0, gt, :].unsqueeze(1).to_broadcast([128, 9, 2, 64])
            ci = CRv[:, :, 1, gt, :].unsqueeze(1).to_broadcast([128, 9, 2, 64])
            cmul("dve", TBc[:, :, :, 0, :], TBc[:, :, :, 1, :], PW[:, :, :, 0, :], PW[:, :, :, 1, :], cr, ci, TMPc,
                 [KG] + gcr_keys, [("g_tbc",)], neg_im=True)
            CM = G_CM.rearrange("q (m s p) -> q m s p", m=36, s=2)
            for s2 in range(2):
                P.op("act", lambda e, s2=s2: e.activation(
                    out=CM[:, :, s2, :], in_=G_TBC.rearrange("q (m p) -> q m p", m=36), func=AF.Copy,
                    scale=PMK[:, s2:s2 + 1]), reads=[("g_tbc",), ("pmk",)], writes=[kcm])
        def genC(gt):
            E = "dve"
            sx = gt % 2
            S = GS[sx]
            KG = ("gen", sx)
            G_PR, G_DT, G_BRP, G_BB, G_PW = S["PR"], S["DT"], S["BRP"], S["BB"], S["PW"]
            pb = 4 + (gt % 2) * 2

            G_BX = G_BXS[sx]
            G_CM = G_CMS[sx]
            kcm = ("g_cm", sx)
            PW = G_PW.rearrange("q (n d r p) -> q n d r p", n=9, d=2, r=2)
            PSB = [PS[b_][:, :].bitcast(BF16) for b_ in range(8)]
            CM3 = G_CM.rearrange("q (m c) -> q m c", m=36)
            CST3 = G_CST.rearrange("q (m c) -> q m c", m=36)
            for grp in range(5):
                m0 = grp * 8
                mcnt = min(8, 36 - m0)
                bb = pb + (grp % 2)
                for i in range(mcnt):
                    P.op("pe", lambda e, m=m0 + i, i=i, bb=bb: e.transpose(
                        out=PSB[bb][:, i * 128:(i + 1) * 128], in_=CM3[:, m, :], identity=IDB[:, :]),
                        reads=[kcm, ("idb",)], writes=kps(bb))
                P.op("act" if grp % 2 else "dve",
                     (lambda e, m0=m0, mcnt=mcnt, bb=bb: e.activation(
                         out=CST3[:, m0:m0 + mcnt, :], in_=PSB[bb][:, 0:mcnt * 128].rearrange("q (m c) -> q m c", m=mcnt),
                         func=AF.Copy)) if grp % 2 else
                     (lambda e, m0=m0, mcnt=mcnt, bb=bb: e.tensor_copy(
                         out=CST3[:, m0:m0 + mcnt, :], in_=PSB[bb][:, 0:mcnt * 128].rearrange("q (m c) -> q m c", m=mcnt))),
                     reads=kps(bb), writes=[("g_cst", grp)])
            cst_keys = [("g_cst", g) for g in range(5)]
            P.op("sp", lambda e: e.dma_start(out=swb_d[gt, :, OFF_C:OFF_C + 4608], in_=G_CST),
                 reads=cst_keys, writes=[("swb", gt)], dma=True, slot="gen1")
            CST5 = G_CST.rearrange("q (n d r c) -> q n d r c", n=9, d=2, r=2)
            BX4 = G_BX.rearrange("q (d r c) -> q d r c", d=2, r=2)
            for d in range(2):
                for w in range(4):
                    for ri in range(2):
                        P.op("pe", lambda e, d=d, w=w, ri=ri: e.matmul(
                            PS[pb][32 * w:32 * w + 32, d * 256:(d + 1) * 256],
                            lhsT=BX4[:, d, ri, 32 * w:32 * w + 32], rhs=CST5[:, 0:8, d, ri, 32 * w:32 * w + 32],
                            start=(ri == 0), stop=(ri == 1), tile_position=(0, 32 * w)),
                            reads=cst_keys + [("g_bx",)], writes=kps(pb))
            P.op("dve", lambda e: e.tensor_copy(out=G_KS, in_=PS[pb][:, 0:512]), reads=kps(pb), writes=[("g_ks",)])
            KS = G_KS.rearrange("q (d n c) -> q d n c", d=2, n=8)
            KT = G_KT.rearrange("q (l w c) -> q l w c", l=15, w=4)
            pmb7 = PAIRM[:, :].unsqueeze(1).unsqueeze(3).to_broadcast([128, 7, 4, 32])
            for d in range(2):
                P.op("dve", lambda e, d=d: e.tensor_tensor(
                    out=KT[:, 1 + 7 * d:8 + 7 * d, :, :], in0=KS[:, d, 1:8, :].unsqueeze(2).to_broadcast([128, 7, 4, 32]),
                    in1=pmb7, op=ALU.mult), reads=[("g_ks",), ("pairm",)], writes=[("g_kt",)])
            K0 = G_K0.rearrange("q (w c) -> q w c", w=4)
            P.op("dve", lambda e: e.tensor_tensor(out=G_MT[:, 0:32], in0=KS[:, 0, 0, :], in1=KS[:, 1, 0, :], op=ALU.add),
                 reads=[("g_ks",)], writes=[("g_k0",)])
            P.op("dve", lambda e: e.tensor_tensor(
                out=K0, in0=G_MT[:, 0:32].unsqueeze(1).to_broadcast([128, 4, 32]),
                in1=PAIRM[:, :].unsqueeze(2).to_broadcast([128, 4, 32]), op=ALU.mult),
                reads=[("g_k0",), ("pairm",)], writes=[("g_k0",)])
            P.op("dve", lambda e: e.scalar_tensor_tensor(
                out=G_KT[:, 0:128], in0=IDF[:, :], scalar=DSK[:, gt:gt + 1], in1=G_K0, op0=ALU.mult, op1=ALU.add),
                reads=[("g_k0",), ("idf",), ("vt1",)], writes=[("g_kt",)])
            P.op("sp", lambda e: e.dma_start(out=swb_d[gt, :, OFF_K:OFF_K + 1920], in_=G_KT),
                 reads=[("g_kt",)], writes=[("swb", gt)], dma=True, slot="gen2")
            MM = G_MM.rearrange("q (m s p) -> q m s p", m=4, s=2)
            for s2 in range(2):
                P.op("act", lambda e, s2=s2: e.activation(
                    out=MM[:, :, s2, :], in_=G_PW[:, 8 * 256:9 * 256].rearrange("q (m p) -> q m p", m=4), func=AF.Copy,
                    scale=PMK[:, s2:s2 + 1]), reads=[KG, ("pmk",)], writes=[("g_mm",)])
            for m in range(4):
                P.op("pe", lambda e, m=m: e.transpose(out=PS[pb + 1][:, m * 128:(m + 1) * 128],
                                                      in_=G_MM[:, m * 128:(m + 1) * 128], identity=IDF[:, :]),
                     reads=[("g_mm",), ("idf",)], writes=kps(pb + 1))
            MU = G_MU.rearrange("q (w d k r) -> q w d k r", w=4, d=2, k=7)
            psm = PS[pb + 1][:, :].rearrange("q (m w s h) -> q m w s h", m=4, w=4, s=2)[:, :, :, :, 0]
            P.op("dve", lambda e: e.tensor_reduce(out=MU[:, :, :, 0, :].rearrange("q w d r -> q d r w"),
                                                  in_=psm.rearrange("q (d r) w s -> q d r w s", d=2),
                                                  axis=AX.X, op=ALU.add), reads=kps(pb + 1), writes=[("g_mu",)])
            MT = G_MT[:, 32:56].rearrange("q (i w d) -> q i w d", i=3, w=4)
            ME = "pool"
            for k in range(6):
                a = MU[:, :, :, k, 0]
                bq = MU[:, :, :, k, 1]
                P.op(ME, lambda e, a=a: e.tensor_tensor(out=MT[:, 0], in0=a, in1=a, op=ALU.mult),
                     reads=[("g_mu",)], writes=[("g_mt",)])
                P.op(ME, lambda e, bq=bq: e.tensor_tensor(out=MT[:, 1], in0=bq, in1=bq, op=ALU.mult),
                     reads=[("g_mu",), ("g_mt",)], writes=[("g_mt",)])
                P.op(ME, lambda e, k=k: e.tensor_tensor(out=MU[:, :, :, k + 1, 0], in0=MT[:, 0], in1=MT[:, 1],
                                                        op=ALU.subtract), reads=[("g_mt",)], writes=[("g_mu",)])
                P.op(ME, lambda e, a=a, bq=bq: e.tensor_tensor(out=MT[:, 2], in0=a, in1=bq, op=ALU.mult),
                     reads=[("g_mu",), ("g_mt",)], writes=[("g_mt",)])
                P.op(ME, lambda e, k=k: e.tensor_tensor(out=MU[:, :, :, k + 1, 1], in0=MT[:, 2], in1=MT[:, 2],
                                                        op=ALU.add), reads=[("g_mt",)], writes=[("g_mu",)])
            P.op("sp", lambda e: e.dma_start(out=swm_d[gt], in_=G_MU), reads=[("g_mu",)], writes=[("swm", gt)],
                 dma=True, slot="gen3")

        ssm_param_loads()

        def mod_block(j):
            wb = w_acquire(wb_ada[j])
            wv = wview(wb, "k16")
            b = big_bank() if j < 8 else 6 + (j % 2)
            for cc in range(4):
                for k in range(16):
                    P.op("pe", lambda e, wv=wv, cc=cc, k=k, b=b: e.matmul(
                        PS[b][:, cc * 2:cc * 2 + 2], lhsT=wv[:, k, cc * 128:(cc + 1) * 128], rhs=SCT[:, k, :],
                        start=(k == 0), stop=(k == 15)),
                        reads=[("wb", wb), ("sct", 0), ("sct", 1)], writes=kps(b))
            if j < 8:
                P.op("dve", lambda e, j=j, b=b: e.tensor_tensor(
                    out=MODT[:, j * 4:(j + 1) * 4, :], in0=PS[b][:, 0:8].rearrange("p (c v) -> p c v", v=2),
                    in1=VT2[:, j * 4:(j + 1) * 4].unsqueeze(2).to_broadcast([128, 4, 2]), op=ALU.add),
                    reads=kps(b) + [("vt2",)], writes=[("modt", j)])
            else:
                for cc in range(4):
                    P.op("act", lambda e, j=j, b=b, cc=cc: e.activation(
                        out=MODT[:, j * 4 + cc, :], in_=PS[b][:, cc * 2:cc * 2 + 2], func=AF.Identity,
                        bias=VT2[:, j * 4 + cc:j * 4 + cc + 1], scale=1.0),
                        reads=kps(b) + [("vt2",)], writes=[("modt", j)])

        def modk(j0, n=16):
            return [("modt", j) for j in range(j0 // 4, (j0 + n + 3) // 4)]

        def mod_finish_f():
            P.op("dve", lambda e: e.scalar_tensor_tensor(
                out=SCF[:, :, :], in0=MODT[:, 64:80, :], scalar=1.0, in1=GPF.unsqueeze(2).to_broadcast([128, 16, 2]),
                op0=ALU.add, op1=ALU.mult), reads=modk(64) + [("vt1",)], writes=[("scf",)])

        gen_on = not (debug or "").startswith(("mod", "ph1", "ph2", "ph3"))
        if gen_on:
            genA(0)
            genB(0)
        for j in range(8):
            mod_block(j)
            if gen_on:
                if j < 7:
                    genA(j + 1)
                    genB(j + 1)
                genC(j)
        P.op("dve", lambda e: e.scalar_tensor_tensor(
            out=SCM[:, :, :], in0=MODT[:, 16:32, :], scalar=1.0, in1=GPM.unsqueeze(2).to_broadcast([128, 16, 2]),
            op0=ALU.add, op1=ALU.mult), reads=modk(16) + [("vt1",)], writes=[("scm",)])
        modall = modk(0, 96)

        if debug == "gen":
            P.barrier()
            dbg_outs["swb"] = dout("dbg_swb", [8, 128, NSW], BF16)
            dbg_outs["swm"] = dout("dbg_swm", [8, 128, NMU])
            P.op("sp", lambda e: e.dma_start(out=dbg_outs["swb"], in_=swb_d), dma=True, slot="dbg")
            P.op("sp", lambda e: e.dma_start(out=dbg_outs["swm"], in_=swm_d), dma=True, slot="dbg2")
            P.emit()
            return nc
        if debug == "mod":
            dbg_outs["modt"] = dout("dbg_modt", [128, 192])
            P.op("sp", lambda e: e.dma_start(out=dbg_outs["modt"], in_=MODT[:, :, :].rearrange("p j v -> p (j v)")),
                 reads=modall, dma=True, slot="dbg")
            P.emit()
            return nc


        UA = BIG[:, 0:8192].rearrange("p (t n) -> p t n", t=8)
        VT = BIG[:, 8192:16384].rearrange("p (t n) -> p t n", t=8)
        VH = VHR[:, :].rearrange("p (t n) -> p t n", t=8)
        GV = XT[1][:, 0:1024]
        UTF = UT[:, :, :, :].rearrange("p g j c -> p (g j c)").bitcast(F32)
        GMROW = UTF[:, 0:2048]
        GFROW = UTF[:, 2048:4096]

        FREQ = sb("FREQ", [128, 512], F32)
        PIDX = sb("PIDX", [128, 4], F32)
        PIDI = sb("PIDI", [128, 4], I32)

        def build_freq():
            FI = XT[1][:, 0:512].bitcast(I32)
            P.op("pool", lambda e: e.iota(FI, pattern=[[1, 512]], base=0, channel_multiplier=0), writes=[("xt", 1)])
            P.op("dve", lambda e: e.tensor_copy(out=FREQ[:, :], in_=FI), reads=[("xt", 1)], writes=[("freq",)])
            P.op("act", lambda e: e.activation(out=FREQ[:, :], in_=FREQ[:, :], func=AF.Exp,
                                               scale=-math.log(10000.0) / 512.0), reads=[("freq",)], writes=[("freq",)])
            P.op("pool", lambda e: e.iota(PIDI[:, 0:1], pattern=[[0, 1]], base=0, channel_multiplier=1),
                 writes=[("pidi",)])
            P.op("dve", lambda e: e.tensor_single_scalar(out=PIDI[:, 1:2], in_=PIDI[:, 0:1], scalar=63,
                                                         op=ALU.bitwise_and), reads=[("pidi",)], writes=[("pidi1",)])
            P.op("dve", lambda e: e.tensor_single_scalar(out=PIDI[:, 2:3], in_=PIDI[:, 0:1], scalar=6,
                                                         op=ALU.arith_shift_right), reads=[("pidi",)], writes=[("pidi2",)])
            P.op("dve", lambda e: e.tensor_copy(out=PIDX[:, 0:2], in_=PIDI[:, 1:3]),
                 reads=[("pidi1",), ("pidi2",)], writes=[("pidx",)])

        def sincos_table(dst, dst_key, idx_col, add_const, tmpA, tmpB, tmp_key):
            TI = tmpB.bitcast(I32)
            P.op("dve", lambda e: e.tensor_scalar(out=tmpA, in0=FREQ[:, :], scalar1=idx_col, scalar2=None, op0=ALU.mult),
                 reads=[("freq",), ("pidx",)], writes=[tmp_key])
            if add_const != 0:
                P.op("dve", lambda e: e.scalar_tensor_tensor(out=tmpA, in0=FREQ[:, :], scalar=float(add_const), in1=tmpA,
                                                             op0=ALU.mult, op1=ALU.add),
                     reads=[("freq",), tmp_key], writes=[tmp_key])
            P.op("dve", lambda e: e.tensor_scalar(out=TI, in0=tmpA, scalar1=1.0 / TWO_PI, scalar2=None, op0=ALU.mult),
                 reads=[tmp_key], writes=[tmp_key])
            P.op("dve", lambda e: e.tensor_copy(out=tmpB, in_=TI), reads=[tmp_key], writes=[tmp_key])
            P.op("dve", lambda e: e.scalar_tensor_tensor(out=tmpA, in0=tmpB, scalar=-TWO_PI, in1=tmpA, op0=ALU.mult,
                                                         op1=ALU.add), reads=[tmp_key], writes=[tmp_key])
            P.op("dve", lambda e: e.tensor_scalar(out=tmpB, in0=tmpA, scalar1=math.pi, scalar2=-TWO_PI, op0=ALU.is_gt,
                                                  op1=ALU.mult), reads=[tmp_key], writes=[tmp_key])
            P.op("dve", lambda e: e.tensor_tensor(out=tmpA, in0=tmpA, in1=tmpB, op=ALU.add), reads=[tmp_key],
                 writes=[tmp_key])
            P.op("act", lambda e: e.activation(out=dst[:, 0:512], in_=tmpA, func=AF.Sin), reads=[tmp_key],
                 writes=[dst_key])
            P.op("dve", lambda e: e.tensor_scalar(out=tmpA, in0=tmpA, scalar1=math.pi / 2, scalar2=None, op0=ALU.add),
                 reads=[tmp_key, dst_key], writes=[tmp_key])
            P.op("dve", lambda e: e.tensor_scalar(out=tmpB, in0=tmpA, scalar1=math.pi, scalar2=-TWO_PI, op0=ALU.is_gt,
                                                  op1=ALU.mult), reads=[tmp_key], writes=[tmp_key])
            P.op("dve", lambda e: e.tensor_tensor(out=tmpA, in0=tmpA, in1=tmpB, op=ALU.add), reads=[tmp_key],
                 writes=[tmp_key])
            P.op("act", lambda e: e.activation(out=dst[:, 512:1024], in_=tmpA, func=AF.Sin), reads=[tmp_key],
                 writes=[dst_key])

        VHF = VHR[:, :].bitcast(F32)
        POSC = VHF[:, 0:1024]
        ROWT = VHF[:, 1024:2048]
        PTMPA = VHF[:, 2048:2560]
        PTMPB = VHF[:, 2560:3072]

        def load_x_tile(u, t, xb_i, src):
            xb = XT[xb_i]
            P.op("sp", lambda e: e.dma_start(out=xb[:, :], in_=src[t * 128:(t + 1) * 128, :]),
                 writes=[("xt", xb_i)], dma=True, slot=("xt", xb_i))
            if u == 0:
                sincos_table(ROWT, ("rowt",), PIDX[:, 1:2], 2 * t, PTMPA, PTMPB, ("ptmp",))
                P.op("dve", lambda e: e.tensor_tensor(out=xb[:, 0:1024], in0=xb[:, 0:1024], in1=ROWT, op=ALU.add),
                     reads=[("rowt",), ("xt", xb_i)], writes=[("xt", xb_i)])
                P.op("dve", lambda e: e.tensor_tensor(out=xb[:, 1024:2048], in0=xb[:, 1024:2048], in1=POSC, op=ALU.add),
                     reads=[("posc",), ("xt", xb_i)], writes=[("xt", xb_i)])

        def norm_and_transpose(src_ap, src_keys, t, sc_ap, sh_ap, v, bank_base, mkeys=(), stage=0):
            mkeys = list(mkeys)
            if stage in (0, 1):
                _norm_part(src_ap, src_keys, t)
            if stage in (0, 2):
                _tr_part(src_ap, src_keys, t, sc_ap, sh_ap, v, bank_base, mkeys)

        def _norm_part(src_ap, src_keys, t):
            for c4 in range(4):
                P.op("dve", lambda e, c4=c4: e.bn_stats(out=BNS[:, t, c4, :], in_=src_ap[:, c4 * 512:(c4 + 1) * 512]),
                     reads=src_keys, writes=[("bns", t, c4)])
            col = 16 + t * 8
            mv = smcol(col, 2)
            P.op("dve", lambda e: e.bn_aggr(out=mv, in_=BNS[:, t, :, :].rearrange("p a b -> p (a b)")),
                 reads=[("bns", t, c4) for c4 in range(4)], writes=[("sm", col)])
            P.op("dve", lambda e: e.scalar_tensor_tensor(out=smcol(col + 2), in0=mv[:, 0:1], scalar=mv[:, 0:1],
                                                         in1=mv[:, 1:2], op0=ALU.mult, op1=ALU.add),
                 reads=[("sm", col)], writes=[("sm", col + 2)])
            P.op("act", lambda e: e.activation(out=smcol(col + 3), in_=smcol(col + 2), func=AF.Sqrt,
                                               bias=EPSC[:, 0:1], scale=1.0),
                 reads=[("sm", col + 2), ("epsc",)], writes=[("sm", col + 3)])
            P.op("dve", lambda e: e.reciprocal(out=smcol(col + 4), in_=smcol(col + 3)),
                 reads=[("sm", col + 3)], writes=[("sm", col + 4)])
            P.op("act", lambda e: e.activation(out=src_ap, in_=src_ap, func=AF.Copy, scale=smcol(col + 4)),
                 reads=list(src_keys) + [("sm", col + 4)], writes=src_keys)

        def _tr_part(src_ap, src_keys, t, sc_ap, sh_ap, v, bank_base, mkeys):
            for kg in range(4):
                b = bank_base + kg
                for j in range(4):
                    k = kg * 4 + j
                    P.op("pe", lambda e, k=k, b=b, j=j: e.transpose(out=PS[b][:, j * 128:(j + 1) * 128],
                                                                    in_=src_ap[:, k * 128:(k + 1) * 128], identity=IDF[:, :]),
                         reads=list(src_keys) + [("idf",)], writes=[("ps", b)])
                for j in range(4):
                    k = kg * 4 + j
                    eng = ev_eng()
                    if eng == "act":
                        P.op("act", lambda e, k=k, b=b, j=j: e.activation(
                            out=HT[:, k, t * 128:(t + 1) * 128], in_=PS[b][:, j * 128:(j + 1) * 128], func=AF.Identity,
                            scale=sc_ap[:, k, v:v + 1], bias=sh_ap[:, k, v:v + 1]),
                            reads=[("ps", b), ("scm",), ("scf",)] + mkeys, writes=[("ht", t)])
                    else:
                        P.op("dve", lambda e, k=k, b=b, j=j: e.tensor_scalar(
                            out=HT[:, k, t * 128:(t + 1) * 128], in0=PS[b][:, j * 128:(j + 1) * 128],
                            scalar1=sc_ap[:, k, v:v + 1], scalar2=sh_ap[:, k, v:v + 1], op0=ALU.mult, op1=ALU.add),
                            reads=[("ps", b), ("scm",), ("scf",)] + mkeys, writes=[("ht", t)])

        P.barrier()
        build_freq()

        def unit(u):
            v = u
            W = wb_unit[u]
            xsrc = x_in[u]
            if u == 0:
                sincos_table(POSC, ("posc",), PIDX[:, 0:1], 0, PTMPA, PTMPB, ("ptmp",))
            def ph1(t, stage):
                if stage == 1:
                    load_x_tile(u, t, t % 2, xsrc)
                norm_and_transpose(XT[t % 2][:, :], [("xt", t % 2)], t, SCM, MODT[:, 0:16, :], v, (t % 2) * 4, modk(0),
                                   stage=stage)
            for t in range(8):
                ph1(t, 1)
                ph1(t, 2)
            P.barrier()
            if debug == "ph1" and u == int(debug_unit):
                return "stop"
            if "gv" not in _SKIP:
                P.op("sp", lambda e: e.dma_start(out=GV, in_=chunk_g_v.partition_broadcast(128)),
                     writes=[("gv",), ("xt", 1)], dma=True, slot="c0")
            for j in range(6):
                if "ub" in _SKIP and j >= 4:
                    continue
                wb = w_acquire(W["win"][j])
                wv = wview(wb, "k16")
                if j < 4:
                    for t in range(8):
                        b = big_bank()
                        for k in range(16):
                            P.op("pe", lambda e, k=k, t=t, b=b, wv=wv: e.matmul(
                                PS[b][:, :], lhsT=HT[:, k, t * 128:(t + 1) * 128], rhs=wv[:, k, :],
                                start=(k == 0), stop=(k == 15)), reads=[("wb", wb), ("ht", t)], writes=kps(b))
                        if j < 2:
                            P.op("act", lambda e, t=t, b=b, j=j: e.activation(
                                out=UA[:, t, j * 512:(j + 1) * 512], in_=PS[b][:, :], func=AF.Copy),
                                reads=kps(b), writes=[("ua", t, j)])
                        else:
                            jj = j - 2
                            P.op("act", lambda e, t=t, b=b, jj=jj: e.activation(
                                out=VT[:, t, jj * 512:(jj + 1) * 512], in_=PS[b][:, :], func=AF.Copy),
                                reads=kps(b), writes=[("vt", t, jj)])
                            P.op("dve", lambda e, t=t, jj=jj: e.bn_stats(out=BNS[:, t, jj, :],
                                                                         in_=VT[:, t, jj * 512:(jj + 1) * 512]),
                                 reads=[("vt", t, jj)], writes=[("bns", t, jj)])
                else:
                    jj = j - 4
                    for cc in range(4):
                        gt = jj * 4 + cc
                        for th in range(2):
                            b = big_bank()
                            for k in range(16):
                                P.op("pe", lambda e, k=k, th=th, b=b, wv=wv, cc=cc: e.matmul(
                                    PS[b][:, :], lhsT=wv[:, k, cc * 128:(cc + 1) * 128],
                                    rhs=HT[:, k, th * 512:(th + 1) * 512], start=(k == 0), stop=(k == 15)),
                                    reads=[("wb", wb)] + [("ht", th * 4 + i) for i in range(4)], writes=kps(b))
                            eng = ev_eng()
                            outap = UT[:, gt, :, th * 64:(th + 1) * 64].rearrange("p j c -> p c j")
                            inap = PS[b][:, :].rearrange("p (c j) -> p c j", j=TC)
                            if eng == "act":
                                P.op("act", lambda e, outap=outap, inap=inap: e.activation(out=outap, in_=inap, func=AF.Copy),
                                     reads=kps(b), writes=[("ut", gt, th)])
                            else:
                                P.op("dve", lambda e, outap=outap, inap=inap: e.tensor_copy(out=outap, in_=inap),
                                     reads=kps(b), writes=[("ut", gt, th)])
            if debug == "ph2" and u == int(debug_unit):
                return "stop"
            def cmix(t, stage):
                col = 16 + t * 8
                c0 = 128 + t * 8
                if stage == 1:
                    col = 16 + t * 8
                    mv = smcol(col, 2)
                    P.op("dve", lambda e, t=t, mv=mv: e.bn_aggr(out=mv, in_=BNS[:, t, 0:2, :].rearrange("p a b -> p (a b)")),
                         reads=[("bns", t, 0), ("bns", t, 1)], writes=[("sm", col)])
                    P.op("dve", lambda e, mv=mv, col=col: e.scalar_tensor_tensor(
                        out=smcol(col + 2), in0=mv[:, 0:1], scalar=mv[:, 0:1], in1=mv[:, 1:2], op0=ALU.mult, op1=ALU.add),
                        reads=[("sm", col)], writes=[("sm", col + 2)])
                    P.op("act", lambda e, col=col: e.activation(out=smcol(col + 3), in_=smcol(col + 2), func=AF.Sqrt,
                                                                bias=EPSC[:, 0:1], scale=1.0),
                         reads=[("sm", col + 2), ("epsc",)], writes=[("sm", col + 3)])
                    P.op("dve", lambda e, col=col: e.reciprocal(out=smcol(col + 4), in_=smcol(col + 3)),
                         reads=[("sm", col + 3)], writes=[("sm", col + 4)])
                    P.op("dve", lambda e, t=t, col=col: e.scalar_tensor_tensor(
                        out=VH[:, t, :], in0=VT[:, t, :], scalar=smcol(col + 4), in1=GV, op0=ALU.mult, op1=ALU.mult),
                        reads=[("vt", t, 0), ("vt", t, 1), ("sm", col + 4), ("gv",)], writes=[("vh", t)])
                    zb = 4 + (t % 2) * 2
                    for h in range(8):
                        b = zb + h // 4
                        P.op("pe", lambda e, t=t, h=h, b=b: e.matmul(
                            PS[b][:, (h % 4) * 128:(h % 4 + 1) * 128], lhsT=WST[:, h, :], rhs=VH[:, t, h * 128:(h + 1) * 128],
                            start=True, stop=True), reads=[("vh", t), ("wst", h)], writes=[("ps", b)])
                    for h in range(8):
                        b = zb + h // 4
                        P.op("dve", lambda e, t=t, h=h, b=b: e.scalar_tensor_tensor(
                            out=UA[:, t, h * 128:(h + 1) * 128], in0=PS[b][:, (h % 4) * 128:(h % 4 + 1) * 128],
                            scalar=BSP[:, h:h + 1], in1=UA[:, t, h * 128:(h + 1) * 128], op0=ALU.add, op1=ALU.mult),
                            reads=[("ps", b), ("bsp",), ("ua", t, h // 4)], writes=[("ua", t, h // 4)])
                    for c2i in range(2):
                        P.op("dve", lambda e, t=t, c2i=c2i: e.bn_stats(out=BNS[:, t, 2 + c2i, :],
                                                                       in_=UA[:, t, c2i * 512:(c2i + 1) * 512]),
                             reads=[("ua", t, c2i)], writes=[("bns", t, 2 + c2i)])
                    col2 = col + 5
                    P.op("dve", lambda e, t=t, col2=col2: e.bn_aggr(out=SMALL[:, 128 + t * 8:130 + t * 8],
                                                                   in_=BNS[:, t, 2:4, :].rearrange("p a b -> p (a b)")),
                         reads=[("bns", t, 2), ("bns", t, 3)], writes=[("sm", 128 + t * 8)])
                    c0 = 128 + t * 8
                    P.op("dve", lambda e, c0=c0: e.scalar_tensor_tensor(
                        out=smcol(c0 + 2), in0=smcol(c0), scalar=smcol(c0), in1=smcol(c0 + 1), op0=ALU.mult, op1=ALU.add),
                        reads=[("sm", c0)], writes=[("sm", c0 + 2)])
                    P.op("act", lambda e, c0=c0: e.activation(out=smcol(c0 + 3), in_=smcol(c0 + 2), func=AF.Sqrt,
                                                              bias=EPSC[:, 0:1], scale=1.0),
                         reads=[("sm", c0 + 2), ("epsc",)], writes=[("sm", c0 + 3)])
                    P.op("dve", lambda e, c0=c0: e.reciprocal(out=smcol(c0 + 4), in_=smcol(c0 + 3)),
                         reads=[("sm", c0 + 3)], writes=[("sm", c0 + 4)])
                    P.op("act", lambda e, t=t, c0=c0: e.activation(out=UA[:, t, :], in_=UA[:, t, :], func=AF.Copy,
                                                                   scale=smcol(c0 + 4)),
                         reads=[("ua", t, 0), ("ua", t, 1), ("sm", c0 + 4)], writes=[("ua", t, 0), ("ua", t, 1)])
                if stage == 2:
                    tb = (t % 2) * 2
                    for kg in range(2):
                        b = tb + kg
                        for j in range(4):
                            k = kg * 4 + j
                            P.op("pe", lambda e, t=t, k=k, b=b, j=j: e.transpose(
                                out=PS[b][:, j * 128:(j + 1) * 128], in_=UA[:, t, k * 128:(k + 1) * 128], identity=IDF[:, :]),
                                reads=[("ua", t, k // 4), ("idf",)], writes=[("ps", b)])
                        for j in range(4):
                            k = kg * 4 + j
                            eng = ev_eng()
                            if eng == "act":
                                P.op("act", lambda e, t=t, k=k, b=b, j=j: e.activation(
                                    out=HT[:, k, t * 128:(t + 1) * 128], in_=PS[b][:, j * 128:(j + 1) * 128], func=AF.Copy,
                                    scale=GOA[:, k:k + 1]), reads=[("ps", b), ("vt1",)], writes=[("ht", t)])
                            else:
                                P.op("dve", lambda e, t=t, k=k, b=b, j=j: e.tensor_scalar(
                                    out=HT[:, k, t * 128:(t + 1) * 128], in0=PS[b][:, j * 128:(j + 1) * 128],
                                    scalar1=GOA[:, k:k + 1], scalar2=None, op0=ALU.mult),
                                    reads=[("ps", b), ("vt1",)], writes=[("ht", t)])
            cmix(0, 1)
            for t in range(8):
                if t < 7:
                    cmix(t + 1, 1)
                cmix(t, 2)
            P.barrier()
            if debug == "ph3" and u == int(debug_unit):
                return "stop"
            SWBt = BIG[:, 0:2048].bitcast(BF16)
            SWCt = BIG[:, 2048:5312].bitcast(BF16)
            Bst = SWBt.rearrange("q (n d r m) -> q n d r m", n=8, d=2, r=2)
            Cst = SWCt[:, 0:4608].rearrange("q (n d r c) -> q n d r c", n=9, d=2, r=2)
            KTw = SWCt[:, 4608:6528].rearrange("q (l c) -> q l c", l=15)
            SBF = BIG[:, 5312:6368].bitcast(BF16).rearrange("q (d i c) -> q d i c", d=2, i=8)
            SMUS = [BIG[:, 6368 + i * 168:6480 + i * 168].rearrange("q (w d k r) -> q w d k r", w=4, d=2, k=7) for i in range(2)]
            NMUS = [BIG[:, 6480 + i * 168:6536 + i * 168].rearrange("q (w d k) -> q w d k", w=4, d=2) for i in range(2)]
            SMUF = [BIG[:, 6368 + i * 168:6480 + i * 168] for i in range(2)]
            HM = BIG[:, 6704:6728]
            T1 = BIG[:, 6728:7240]
            FSt = BIG[:, 7240:7752]
            FS = FSt.rearrange("q (s d r w) -> q s d r w", s=4, d=2, r=2)
            YG = BIG[:, 8192:16384].rearrange("q (g n) -> q g n", g=8)
            YGB = VHR[:, :].rearrange("q (g n) -> q g n", g=8)
            HT2 = HT[:, 8:16, :].rearrange("p k n -> p (k n)").bitcast(F32)
            SBUFS = [[XT[0][:, 0:1536], XT[1][:, 0:1536]], [HT2[:, 0:1536], HT2[:, 1536:3072]]]
            PTMP = [XT[0][:, 1536 + i * 128:1664 + i * 128] for i in range(4)] + \
                   [XT[1][:, 1536 + i * 128:1664 + i * 128] for i in range(4)] + \
                   [HT2[:, 3072 + i * 128:3200 + i * 128] for i in range(8)]
            nlev = 7 if u == 0 else 5

            def dat(buf, i, d, shift=0):
                if u == 0:
                    o = 64 if d == 0 else 0
                    return buf.rearrange("q (i c) -> q i c", i=8)[:, i, o + shift:o + 128 + shift]
                o = 16 if d == 0 else 0
                return buf.rearrange("q (i s c) -> q i s c", i=8, s=4)[:, i, :, o + shift:o + 32 + shift]

            def dat2(buf, w, d, shift=0):
                if u == 0:
                    o = 64 if d == 0 else 0
                    return buf.rearrange("q (i c) -> q i c", i=8)[:, 2 * w:2 * w + 2, o + shift:o + 128 + shift]
                o = 16 if d == 0 else 0
                return buf.rearrange("q (i c) -> q i c", i=32)[:, 8 * w:8 * w + 8, o + shift:o + 32 + shift]

            def dat_rows(buf, i0, i1, d):
                if u == 0:
                    o = 64 if d == 0 else 0
                    return buf.rearrange("q (i c) -> q i c", i=8)[:, i0:i1, o:o + 128]
                o = 16 if d == 0 else 0
                return buf.rearrange("q (i s c) -> q i s c", i=8, s=4)[:, i0:i1, :, o:o + 32]

            def ptv(i):
                return PTMP[i] if u == 0 else PTMP[i].rearrange("q (s c) -> q s c", s=4)

            def skey(d, si, w):
                return ("scan", d, si, w)

            P.op("pool", lambda e: e.memset(XT[0][:, :], 0.0), writes=[skey(0, 0, w) for w in range(4)] + [("xt", 0)])
            P.op("pool", lambda e: e.memset(XT[1][:, :], 0.0), writes=[skey(0, 1, w) for w in range(4)] + [("xt", 1)])
            P.op("pool", lambda e: e.memset(HT2[:, :], 0.0), writes=[skey(1, si_, w) for w in range(4) for si_ in range(2)])
            P.op("pool", lambda e: e.memset(BIG[:, 5312:6368], 0.0), writes=[("sbf", 0), ("sbf", 1)])

            def emit_V(gt):
                P.op("sp", lambda e: e.dma_start(out=SWBt[:, :], in_=swb_d[gt, :, OFF_B:OFF_B + 4096]), reads=[("swb", gt)],
                     writes=[("swB",)], dma=True, slot="swB")
                for d in range(2):
                    for w in range(4):
                        for ri in range(2):
                            col = (d * 2 + ri) * 128
                            for j in range(8):
                                n = 7 - j if d == 0 else j
                                P.op("pe", lambda e, w=w, ri=ri, j=j, n=n, d=d, col=col: e.matmul(
                                    PS[w][:, col:col + 128], lhsT=Bst[32 * w:32 * w + 32, n, d, ri, :],
                                    rhs=UT[32 * w:32 * w + 32, gt, j, :], start=(j == 0), stop=(j == 7),
                                    tile_position=(32 * w, 0)),
                                    reads=[("swB",), ("ut", gt, 0), ("ut", gt, 1)], writes=kps(w))

            def load_mu(gt):
                i = gt % 2
                P.op("sp", lambda e: e.dma_start(out=SMUF[i], in_=swm_d[gt]), reads=[("swm", gt)],
                     writes=[("smu", i)], dma=True, slot=("smu", i))
                P.op("act", lambda e: e.activation(out=NMUS[i], in_=SMUS[i][:, :, :, :, 1], func=AF.Copy, scale=-1.0),
                     reads=[("smu", i)], writes=[("nmu", i)])

            def evac_V(gt):
                for d in range(2):
                    for w in range(4):
                        if u == 0:
                            inap = PS[w][:, d * 256:(d + 1) * 256].rearrange("q (i c) -> q i c", i=2)
                        else:
                            inap = PS[w][:, d * 256:(d + 1) * 256].rearrange("q (i s c) -> q i s c", i=2, s=4)
                        outap = dat_rows(SBUFS[d][0], 2 * w, 2 * w + 2, d)
                        P.op("act", lambda e, outap=outap, inap=inap: e.activation(out=outap, in_=inap, func=AF.Copy),
                             reads=kps(w), writes=[skey(d, 0, w)])

            def scan(gt):
                SMU, NMU = SMUS[gt % 2], NMUS[gt % 2]
                mk = [("smu", gt % 2), ("nmu", gt % 2)]
                if u == 0:
                    for d in range(2):
                        h0r = H0T[:, (d * 2 + 0) * 32 + gt * 4:(d * 2 + 0) * 32 + gt * 4 + 4]
                        h0i = H0T[:, (d * 2 + 1) * 32 + gt * 4:(d * 2 + 1) * 32 + gt * 4 + 4]
                        mur, mui = SMU[:, :, d, 0, 0], SMU[:, :, d, 0, 1]
                        HMr, HMi, HMt = HM[:, d * 12:d * 12 + 4], HM[:, d * 12 + 4:d * 12 + 8], HM[:, d * 12 + 8:d * 12 + 12]
                        hk = [("hm", d)]
                        rk = [("hm", d), ("h0t",)] + mk
                        P.op("dve", lambda e, mui=mui, h0i=h0i, HMt=HMt: e.tensor_tensor(out=HMt, in0=mui, in1=h0i, op=ALU.mult), reads=rk, writes=hk)
                        P.op("dve", lambda e, mur=mur, h0r=h0r, HMr=HMr: e.tensor_tensor(out=HMr, in0=mur, in1=h0r, op=ALU.mult), reads=rk, writes=hk)
                        P.op("dve", lambda e, HMr=HMr, HMt=HMt: e.tensor_tensor(out=HMr, in0=HMr, in1=HMt, op=ALU.subtract), reads=rk, writes=hk)
                        P.op("dve", lambda e, mui=mui, h0r=h0r, HMt=HMt: e.tensor_tensor(out=HMt, in0=mui, in1=h0r, op=ALU.mult), reads=rk, writes=hk)
                        P.op("dve", lambda e, mur=mur, h0i=h0i, HMi=HMi: e.tensor_tensor(out=HMi, in0=mur, in1=h0i, op=ALU.mult), reads=rk, writes=hk)
                        P.op("dve", lambda e, HMi=HMi, HMt=HMt: e.tensor_tensor(out=HMi, in0=HMi, in1=HMt, op=ALU.add), reads=rk, writes=hk)
                        pos = 64 if d == 0 else 127
                        SA5 = SBUFS[d][0].rearrange("q (w r c) -> q w r c", w=4, r=2)
                        allsa = [skey(d, 0, w) for w in range(4)]
                        P.op("dve", lambda e, pos=pos, SA5=SA5, HMr=HMr: e.tensor_tensor(
                            out=SA5[:, :, 0, pos], in0=SA5[:, :, 0, pos], in1=HMr, op=ALU.add), reads=allsa + hk, writes=allsa)
                        P.op("dve", lambda e, pos=pos, SA5=SA5, HMi=HMi: e.tensor_tensor(
                            out=SA5[:, :, 1, pos], in0=SA5[:, :, 1, pos], in1=HMi, op=ALU.add), reads=allsa + hk, writes=allsa)
                si = 0
                for k in range(nlev):
                    for term in (0, 1, 3):
                        for d in range(2):
                            sft = (1 << k) * (-1 if d == 0 else 1)
                            src, dst = SBUFS[d][si], SBUFS[d][1 - si]
                            for w in range(4):
                                mur = SMU[:, w, d, k, 0:1]
                                mui = SMU[:, w, d, k, 1:2]
                                nmui = NMU[:, w, d, k:k + 1]
                                re, im = 2 * w, 2 * w + 1
                                rk = [skey(d, si, w), skey(d, 1 - si, w)] + mk
                                wk = [skey(d, 1 - si, w)]
                                if term == 0:
                                    o_, i0_, sc_, i1_ = dat2(dst, w, d), dat2(src, w, d, sft), mur, dat2(src, w, d)
                                elif term == 1:
                                    o_, i0_, sc_, i1_ = dat(dst, re, d), dat(src, im, d, sft), nmui, dat(dst, re, d)
                                else:
                                    o_, i0_, sc_, i1_ = dat(dst, im, d), dat(src, re, d, sft), mui, dat(dst, im, d)
                                P.op("dve", lambda e, o_=o_, i0_=i0_, sc_=sc_, i1_=i1_: e.scalar_tensor_tensor(
                                    out=o_, in0=i0_, scalar=sc_, in1=i1_, op0=ALU.mult, op1=ALU.add), reads=rk, writes=wk)
                    si = 1 - si
                for d in range(2):
                    fin = SBUFS[d][si]
                    fkeys = [skey(d, si, w) for w in range(4)]
                    c0 = 1 if d == 0 else 0
                    if u == 0:
                        P.op("act", lambda e, fin=fin, d=d, c0=c0: e.activation(
                            out=SBF[:, d, :, c0:c0 + 128], in_=dat_rows(fin, 0, 8, d), func=AF.Copy),
                            reads=fkeys, writes=[("sbf", d)])
                        hs = 0 if d == 0 else 128
                        P.op("act", lambda e, d=d, hs=hs, gt=gt: e.activation(
                            out=SBF[:, d, :, hs].rearrange("q (w r) -> q w r", w=4),
                            in_=H0T[:, d * 64:d * 64 + 64].rearrange("q (r w) -> q w r", r=2)[:, gt * 4:gt * 4 + 4, :],
                            func=AF.Copy), reads=[("h0t",)], writes=[("sbf", d)])
                    else:
                        P.op("act", lambda e, fin=fin, d=d, c0=c0: e.activation(
                            out=SBF[:, d, :, :].rearrange("q i (s c) -> q i s c", s=4)[:, :, :, c0:c0 + 32],
                            in_=dat_rows(fin, 0, 8, d), func=AF.Copy), reads=fkeys, writes=[("sbf", d)])
                        pos = 47 if d == 0 else 0
                        P.op("act", lambda e, fin=fin, d=d, pos=pos, gt=gt: e.activation(
                            out=FS[:, :, d, :, gt * 4:gt * 4 + 4].rearrange("q s r w -> q w r s"),
                            in_=fin.rearrange("q (w r s c) -> q w r s c", w=4, r=2, s=4)[:, :, :, :, pos], func=AF.Copy),
                            reads=fkeys, writes=[("fs",)])

            def firc(gt):
                yb = 4 + 2 * (gt % 2)
                P.op("sp", lambda e: e.dma_start(out=SWCt[:, :], in_=swb_d[gt, :, OFF_C:OFF_C + 6528]), reads=[("swb", gt)],
                     writes=[("swC",)], dma=True, slot="swC")
                for j in range(8):
                    bank = yb + j // 4
                    col = (j % 4) * 128
                    for i in range(8):
                        L = 0 if i == j else (j - i if i < j else 7 + (i - j))
                        P.op("pe", lambda e, bank=bank, col=col, L=L, i=i: e.matmul(
                            PS[bank][:, col:col + 128], lhsT=KTw[:, L, :], rhs=UT[:, gt, i, :], start=(i == 0), stop=False),
                            reads=[("swC",), ("ut", gt, 0), ("ut", gt, 1)], writes=kps(bank))
                    for w in range(4):
                        for d in range(2):
                            for ri in range(2):
                                n = j + 1 if d == 0 else 8 - j
                                idx = 2 * w + ri
                                c0 = 0 if d == 0 else 1
                                if u == 0:
                                    rhs = SBF[:, d, idx, c0:c0 + 128]
                                else:
                                    rhs = SBF[:, d, idx, :].rearrange("q (s c) -> q s c", s=4)[:, :, c0:c0 + 32]
                                last = (d == 1 and ri == 1)
                                P.op("pe", lambda e, bank=bank, col=col, w=w, d=d, ri=ri, n=n, rhs=rhs, last=last: e.matmul(
                                    PS[bank][32 * w:32 * w + 32, col:col + 128], lhsT=Cst[:, n, d, ri, 32 * w:32 * w + 32],
                                    rhs=rhs, start=False, stop=last, tile_position=(0, 32 * w)),
                                    reads=[("swC",), ("sbf", 0), ("sbf", 1)], writes=kps(bank))

            def yevac(gt):
                yb = 4 + 2 * (gt % 2)
                for jb in range(2):
                    bank = yb + jb
                    inap = PS[bank][:, :].rearrange("q (j c) -> q j c", j=4)
                    P.op("act", lambda e, jb=jb, inap=inap: e.activation(
                        out=YG[:, gt, :].rearrange("q (c j) -> q j c", j=TC)[:, jb * 4:jb * 4 + 4, :], in_=inap,
                        func=AF.Gelu_apprx_tanh), reads=kps(bank), writes=[("yg", gt)])
                    P.op("act", lambda e, jb=jb, inap=inap: e.activation(
                        out=YGB[:, gt, :].rearrange("q (c j) -> q j c", j=TC)[:, jb * 4:jb * 4 + 4, :], in_=inap,
                        func=AF.Gelu_apprx_tanh), reads=kps(bank), writes=[("ygb", gt)])

            emit_V(0)
            load_mu(0)
            evac_V(0)
            emit_V(1)
            for gt in range(8):
                scan(gt)
                if gt < 7:
                    load_mu(gt + 1)
                    evac_V(gt + 1)
                if u == 0:
                    for jb in (8 + 2 * gt, 9 + 2 * gt):
                        mod_block(jb)
                firc(gt)
                if gt < 6:
                    emit_V(gt + 2)
                yevac(gt)
            if u == 0:
                mod_finish_f()
            if u == 1:
                for sq in range(4):
                    b = hi_bank()
                    P.op("pe", lambda e, sq=sq, b=b: e.transpose(out=PS[b][:, 0:128], in_=FSt[:, sq * 128:(sq + 1) * 128],
                                                                 identity=IDF[:, :]), reads=[("fs",), ("idf",)], writes=kps(b))
                    P.op("dve", lambda e, sq=sq, b=b: e.tensor_copy(out=T1[:, sq * 128:(sq + 1) * 128], in_=PS[b][:, 0:128]),
                         reads=kps(b), writes=[("t1",)])
                    P.op("sp", lambda e, sq=sq: e.dma_start(
                        out=ns_out[sq].rearrange("d r (w t) p -> (d r w) (t p)", t=2), in_=T1[:, sq * 128:(sq + 1) * 128]),
                        reads=[("t1",)], dma=True, slot=("ns", sq))
            P.barrier()
            if debug == "ph4" and u == int(debug_unit):
                return "stop"
            SIGT = [XT[0][:, 0:512], XT[0][:, 512:1024]]
            SQT = [XT[0][:, 1024:1536], XT[0][:, 1536:2048], XT[1][:, 1024:1536], XT[1][:, 1536:2048]]
            pend = []

            def flush_pend(keep):
                while len(pend) > keep:
                    ti_, th_, o_ = pend.pop(0)
                    P.op("pe", lambda e, ti_=ti_, th_=th_, o_=o_: e.matmul(
                        PS[6 + th_][:, :], lhsT=ONESF[:, :], rhs=SQT[ti_], start=(o_ == 0), stop=(o_ == 7)),
                        reads=[("sqt", ti_), ("onesf",)], writes=kps(6 + th_))
            RB = XT[1][:, 0:1024]
            tcount = 0
            for j in range(2):
                wb = w_acquire(W["glu"][j])
                wv = wview(wb, "k8")
                for cc in range(4):
                    o = j * 4 + cc
                    for th in range(2):
                        b = big_bank()
                        for k in range(8):
                            P.op("pe", lambda e, k=k, b=b, wv=wv, cc=cc, th=th: e.matmul(
                                PS[b][:, :], lhsT=wv[:, k, cc * 128:(cc + 1) * 128], rhs=YGB[:, k, th * 512:(th + 1) * 512],
                                start=(k == 0), stop=(k == 7)), reads=[("wb", wb)] + [("ygb", g) for g in range(8)],
                                writes=kps(b))
                        ti = tcount % 2
                        qi = tcount % 4
                        tcount += 1
                        P.op("act", lambda e, b=b, ti=ti, o=o: e.activation(out=SIGT[ti], in_=PS[b][:, :], func=AF.Sigmoid,
                                                                          bias=BGL[:, o:o + 1], scale=1.0),
                             reads=kps(b) + [("vt1",)], writes=[("sigt", ti)])
                        ysl = YG[:, o, th * 512:(th + 1) * 512]
                        P.op("dve", lambda e, ysl=ysl, ti=ti: e.tensor_tensor(out=ysl, in0=ysl, in1=SIGT[ti], op=ALU.mult),
                             reads=[("sigt", ti), ("yg", o)], writes=[("yg", o)])
                        P.op("pool", lambda e, ysl=ysl, qi=qi: e.tensor_tensor(out=SQT[qi], in0=ysl, in1=ysl, op=ALU.mult),
                             reads=[("yg", o)], writes=[("sqt", qi)])
                        flush_pend(2)
                        pend.append((qi, th, o))
            flush_pend(0)
            for th in range(2):
                P.op("act", lambda e, th=th: e.activation(out=RB[:, th * 512:(th + 1) * 512], in_=PS[6 + th][:, :],
                                                          func=AF.Sqrt, bias=EPSC[:, 0:1], scale=1.0 / 1024.0),
                     reads=kps(6 + th) + [("epsc",)], writes=[("rb", th)])
                P.op("dve", lambda e, th=th: e.reciprocal(out=RB[:, th * 512:(th + 1) * 512], in_=RB[:, th * 512:(th + 1) * 512]),
                     reads=[("rb", th)], writes=[("rb", th)])
            for o in range(8):
                P.op("dve", lambda e, o=o: e.scalar_tensor_tensor(
                    out=HT[:, 8 + o, :], in0=YG[:, o, :], scalar=GOB[:, o:o + 1], in1=RB, op0=ALU.mult, op1=ALU.mult),
                    reads=[("yg", o), ("rb", 0), ("rb", 1), ("vt1",)], writes=[("ht", t) for t in range(8)])
            P.barrier()
            if debug == "ph5" and u == int(debug_unit):
                return "stop"
            MO = BIG[:, :].rearrange("p (t n) -> p t n", t=8)
            for j in range(4):
                wb = w_acquire(W["wout"][j])
                wv = wview(wb, "k16")
                for t in range(8):
                    b = big_bank()
                    for k in range(16):
                        P.op("pe", lambda e, k=k, t=t, b=b, wv=wv: e.matmul(
                            PS[b][:, :], lhsT=HT[:, k, t * 128:(t + 1) * 128], rhs=wv[:, k, :],
                            start=(k == 0), stop=(k == 15)), reads=[("wb", wb), ("ht", t)], writes=kps(b))
                    P.op("act", lambda e, t=t, b=b, j=j: e.activation(out=MO[:, t, j * 512:(j + 1) * 512], in_=PS[b][:, :],
                                                                      func=AF.Copy), reads=kps(b), writes=[("mo", t, j)])
                    P.op("dve", lambda e, t=t, j=j: e.bn_stats(out=BNS[:, t, j, :], in_=MO[:, t, j * 512:(j + 1) * 512]),
                         reads=[("mo", t, j)], writes=[("bns", t, j)])

            def gate_row(dst, gvec, j0, key):
                P.op("sp", lambda e: e.dma_start(out=dst, in_=gvec.partition_broadcast(128)), writes=[key], dma=True, slot="c0")
                GBT = [XT[1][:, 1024:1152], XT[1][:, 1152:1280]]
                for k in range(16):
                    gi = k % 2
                    b = 4 + (k // 4) % 4
                    P.op("dve", lambda e, gi=gi, k=k: e.tensor_copy(out=GBT[gi], in_=MODT[:, j0 + k, v:v + 1].to_broadcast([128, 128])),
                         reads=modk(j0), writes=[("gbt", gi), ("xt", 1)])
                    P.op("pe", lambda e, gi=gi, k=k, b=b: e.matmul(PS[b][:, (k % 4) * 128:(k % 4 + 1) * 128], lhsT=GBT[gi],
                                                                   rhs=IDF[:, :], start=True, stop=True),
                         reads=[("gbt", gi), ("idf",), ("xt", 1)], writes=kps(b))
                    if k % 4 == 3:
                        k4 = k // 4
                        P.op("dve", lambda e, b=b, k4=k4: e.tensor_tensor(
                            out=dst[:, k4 * 512:(k4 + 1) * 512], in0=PS[b][:, :], in1=dst[:, k4 * 512:(k4 + 1) * 512], op=ALU.mult),
                            reads=kps(b) + [key], writes=[key])

            gate_row(GMROW, g_post_mix, 32, ("gmrow",))
            if u == 0:
                sincos_table(POSC, ("posc",), PIDX[:, 0:1], 0, PTMPA, PTMPB, ("ptmp",))
            for t in range(8):
                mok = [("mo", t, j) for j in range(4)]
                P.op("dve", lambda e, t=t: e.tensor_tensor(out=MO[:, t, :], in0=MO[:, t, :], in1=GMROW, op=ALU.mult),
                     reads=mok + [("gmrow",)] + [("bns", t, c4) for c4 in range(4)], writes=mok)

            def res_s1(t):
                c0 = 80 + t * 8
                P.op("dve", lambda e: e.bn_aggr(out=SMALL[:, c0:c0 + 2], in_=BNS[:, t, :, :].rearrange("p a b -> p (a b)")),
                     reads=[("bns", t, c4) for c4 in range(4)], writes=[("sm", c0)])
                P.op("dve", lambda e: e.scalar_tensor_tensor(
                    out=smcol(c0 + 2), in0=smcol(c0), scalar=smcol(c0), in1=smcol(c0 + 1), op0=ALU.mult, op1=ALU.add),
                    reads=[("sm", c0)], writes=[("sm", c0 + 2)])
                P.op("act", lambda e: e.activation(out=smcol(c0 + 3), in_=smcol(c0 + 2), func=AF.Sqrt,
                                                   bias=EPSC[:, 0:1], scale=1.0),
                     reads=[("sm", c0 + 2), ("epsc",)], writes=[("sm", c0 + 3)])
                P.op("dve", lambda e: e.reciprocal(out=smcol(c0 + 4), in_=smcol(c0 + 3)),
                     reads=[("sm", c0 + 3)], writes=[("sm", c0 + 4)])
                xb_i = t % 2
                load_x_tile(u, t, xb_i, xsrc)
                mok = [("mo", t, j) for j in range(4)]
                P.op("dve", lambda e: e.scalar_tensor_tensor(
                    out=XT[xb_i][:, :], in0=MO[:, t, :], scalar=smcol(c0 + 4), in1=XT[xb_i][:, :], op0=ALU.mult, op1=ALU.add),
                    reads=mok + [("sm", c0 + 4), ("xt", xb_i)], writes=[("xt", xb_i)])
                P.op("pool", lambda e: e.dma_start(out=y_out[u][t * 128:(t + 1) * 128, :], in_=XT[xb_i][:, :]),
                     reads=[("xt", xb_i)], writes=[("ydram", t)], dma=True, slot=("yo", xb_i))
                for c4 in range(4):
                    P.op("dve", lambda e, c4=c4: e.bn_stats(out=BNS[:, t, c4, :], in_=XT[xb_i][:, c4 * 512:(c4 + 1) * 512]),
                         reads=[("xt", xb_i)], writes=[("bns", t, c4)])
                c1 = 144 + t * 8
                P.op("dve", lambda e: e.bn_aggr(out=SMALL[:, c1:c1 + 2], in_=BNS[:, t, :, :].rearrange("p a b -> p (a b)")),
                     reads=[("bns", t, c4) for c4 in range(4)], writes=[("sm", c1)])
                P.op("dve", lambda e: e.scalar_tensor_tensor(
                    out=smcol(c1 + 2), in0=smcol(c1), scalar=smcol(c1), in1=smcol(c1 + 1), op0=ALU.mult, op1=ALU.add),
                    reads=[("sm", c1)], writes=[("sm", c1 + 2)])
                P.op("act", lambda e: e.activation(out=smcol(c1 + 3), in_=smcol(c1 + 2), func=AF.Sqrt,
                                                   bias=EPSC[:, 0:1], scale=1.0),
                     reads=[("sm", c1 + 2), ("epsc",)], writes=[("sm", c1 + 3)])
                P.op("dve", lambda e: e.reciprocal(out=smcol(c1 + 4), in_=smcol(c1 + 3)),
                     reads=[("sm", c1 + 3)], writes=[("sm", c1 + 4)])
                P.op("act", lambda e: e.activation(out=MO[:, t, :], in_=XT[xb_i][:, :], func=AF.Copy, scale=smcol(c1 + 4)),
                     reads=[("xt", xb_i), ("sm", c1 + 4)], writes=mok)

            def res_s2(t):
                mok = [("mo", t, j) for j in range(4)]
                tb = (t % 2) * 2
                for kg in range(4):
                    b = tb + kg % 2
                    for jq in range(4):
                        k = kg * 4 + jq
                        P.op("pe", lambda e, k=k, b=b, jq=jq: e.transpose(
                            out=PS[b][:, jq * 128:(jq + 1) * 128], in_=MO[:, t, k * 128:(k + 1) * 128], identity=IDF[:, :]),
                            reads=mok + [("idf",)], writes=kps(b))
                    for jq in range(4):
                        k = kg * 4 + jq
                        eng = ev_eng()
                        if eng == "act":
                            P.op("act", lambda e, k=k, b=b, jq=jq: e.activation(
                                out=HT[:, k, t * 128:(t + 1) * 128], in_=PS[b][:, jq * 128:(jq + 1) * 128], func=AF.Identity,
                                scale=SCF[:, k, v:v + 1], bias=MODT[:, 48 + k, v:v + 1]),
                                reads=kps(b) + [("scf",)] + modk(48), writes=[("ht", t)])
                        else:
                            P.op("dve", lambda e, k=k, b=b, jq=jq: e.tensor_scalar(
                                out=HT[:, k, t * 128:(t + 1) * 128], in0=PS[b][:, jq * 128:(jq + 1) * 128],
                                scalar1=SCF[:, k, v:v + 1], scalar2=MODT[:, 48 + k, v:v + 1], op0=ALU.mult, op1=ALU.add),
                                reads=kps(b) + [("scf",)] + modk(48), writes=[("ht", t)])

            res_s1(0)
            for t in range(8):
                if t < 7:
                    res_s1(t + 1)
                res_s2(t)
            P.barrier()
            if debug == "ph6" and u == int(debug_unit):
                return "stop"
            Fv = BIG[:, :].rearrange("p (t n) -> p t n", t=8)
            AT = VHR[:, :].rearrange("p (a c n) -> p a c n", a=2, c=4)
            RT = [XT[0][:, 0:512], XT[0][:, 512:1024], XT[0][:, 1024:1536], XT[0][:, 1536:2048]]
            rcount = 0
            gate_row(GFROW, g_post_ffn, 80, ("gfrow",))
            for g in range(16):
                i1, i2 = W["ffn"][g]
                wb1 = w_acquire(i1)
                wv1 = wview(wb1, "k16")
                ab = g % 2
                for cc in range(4):
                    for th in range(2):
                        b = big_bank()
                        for k in range(16):
                            P.op("pe", lambda e, k=k, b=b, wv1=wv1, cc=cc, th=th: e.matmul(
                                PS[b][:, :], lhsT=wv1[:, k, cc * 128:(cc + 1) * 128], rhs=HT[:, k, th * 512:(th + 1) * 512],
                                start=(k == 0), stop=(k == 15)),
                                reads=[("wb", wb1)] + [("ht", th * 4 + i) for i in range(4)], writes=kps(b))
                        ri_ = rcount % 4
                        rcount += 1
                        P.op("act", lambda e, b=b, ri_=ri_: e.activation(out=RT[ri_], in_=PS[b][:, :], func=AF.Relu),
                             reads=kps(b), writes=[("rt", ri_)])
                        P.op("pool" if rcount % 2 else "dve", lambda e, ri_=ri_, ab=ab, cc=cc, th=th: e.tensor_tensor(
                            out=AT[:, ab, cc, th * 512:(th + 1) * 512], in0=RT[ri_], in1=RT[ri_], op=ALU.mult),
                            reads=[("rt", ri_)], writes=[("at", ab, cc, th)])
                wb2 = w_acquire(i2)
                wv2 = wview(wb2, "c4")
                for t in range(8):
                    for nh in range(4):
                        b = hi_bank()
                        for cc in range(4):
                            P.op("pe", lambda e, t=t, nh=nh, b=b, cc=cc, wv2=wv2, ab=ab: e.matmul(
                                PS[b][:, :], lhsT=AT[:, ab, cc, t * 128:(t + 1) * 128], rhs=wv2[:, cc, nh * 512:(nh + 1) * 512],
                                start=(cc == 0), stop=(cc == 3)),
                                reads=[("wb", wb2)] + [("at", ab, c_, t // 4) for c_ in range(4)], writes=kps(b))
                        fsl = Fv[:, t, nh * 512:(nh + 1) * 512]
                        if g == 0:
                            P.op("act", lambda e, b=b, fsl=fsl: e.activation(out=fsl, in_=PS[b][:, :], func=AF.Copy),
                                 reads=kps(b), writes=[("f", t, nh)])
                        else:
                            P.op("dve", lambda e, b=b, fsl=fsl: e.tensor_tensor(out=fsl, in0=PS[b][:, :], in1=fsl, op=ALU.add),
                                 reads=kps(b) + [("f", t, nh)], writes=[("f", t, nh)])
            P.barrier()
            for t in range(8):
                fk = [("f", t, nh) for nh in range(4)]
                for c4 in range(4):
                    P.op("dve", lambda e, t=t, c4=c4: e.bn_stats(out=BNS[:, t, c4, :], in_=Fv[:, t, c4 * 512:(c4 + 1) * 512]),
                         reads=fk, writes=[("bns", t, c4)])
                c2_ = 208 + t * 5
                P.op("dve", lambda e, t=t, c2_=c2_: e.bn_aggr(out=SMALL[:, c2_:c2_ + 2], in_=BNS[:, t, :, :].rearrange("p a b -> p (a b)")),
                     reads=[("bns", t, c4) for c4 in range(4)], writes=[("sm", c2_)])
                P.op("dve", lambda e, c2_=c2_: e.scalar_tensor_tensor(
                    out=smcol(c2_ + 2), in0=smcol(c2_), scalar=smcol(c2_), in1=smcol(c2_ + 1), op0=ALU.mult, op1=ALU.add),
                    reads=[("sm", c2_)], writes=[("sm", c2_ + 2)])
                P.op("act", lambda e, c2_=c2_: e.activation(out=smcol(c2_ + 3), in_=smcol(c2_ + 2), func=AF.Sqrt,
                                                            bias=EPSC[:, 0:1], scale=1.0),
                     reads=[("sm", c2_ + 2), ("epsc",)], writes=[("sm", c2_ + 3)])
                P.op("dve", lambda e, c2_=c2_: e.reciprocal(out=smcol(c2_ + 4), in_=smcol(c2_ + 3)),
                     reads=[("sm", c2_ + 3)], writes=[("sm", c2_ + 4)])
                xb_i = t % 2
                P.op("sp", lambda e, t=t, xb_i=xb_i: e.dma_start(out=XT[xb_i][:, :], in_=y_out[u][t * 128:(t + 1) * 128, :]),
                     reads=[("ydram", t)], writes=[("xt", xb_i)], dma=True, slot=("xt", xb_i))
                P.op("dve", lambda e, t=t, c2_=c2_: e.scalar_tensor_tensor(
                    out=Fv[:, t, :], in0=Fv[:, t, :], scalar=smcol(c2_ + 4), in1=GFROW, op0=ALU.mult, op1=ALU.mult),
                    reads=fk + [("sm", c2_ + 4), ("gfrow",)], writes=fk)
                P.op("pool", lambda e, t=t, xb_i=xb_i: e.tensor_tensor(out=XT[xb_i][:, :], in0=XT[xb_i][:, :], in1=Fv[:, t, :],
                                                                        op=ALU.add), reads=fk + [("xt", xb_i)], writes=[("xt", xb_i)])
                P.op("pool", lambda e, t=t, xb_i=xb_i: e.dma_start(out=y_out[u][t * 128:(t + 1) * 128, :], in_=XT[xb_i][:, :]),
                     reads=[("xt", xb_i)], writes=[("ydram", t)], dma=True, slot=("yo", xb_i))
            P.barrier()
            return None

        debug_unit = 0
        if debug is not None and ":" in debug:
            debug, debug_unit = debug.split(":")
        stop = None
        for u in range(2):
            stop = unit(u)
            if stop:
                break
        if debug == "ph4":
            dbg_outs["big"] = dout("dbg_big", [128, 8192])
            P.op("sp", lambda e: e.dma_start(out=dbg_outs["big"], in_=BIG[:, 8192:16384]), dma=True, slot="dbg2")
            P.emit()
            return nc
        if debug in ("ph1", "ph2", "ph3"):
            dbg_outs["ht"] = dout("dbg_ht", [128, 16 * 1024], BF16)
            P.barrier()
            P.op("sp", lambda e: e.dma_start(out=dbg_outs["ht"], in_=HT[:, :, :].rearrange("p k n -> p (k n)")),
                 dma=True, slot="dbg")
            if debug != "ph1":
                dbg_outs["big"] = dout("dbg_big", [128, 16384])
                P.op("sp", lambda e: e.dma_start(out=dbg_outs["big"], in_=BIG[:, :]), dma=True, slot="dbg2")
                dbg_outs["ut"] = dout("dbg_ut", [128, 8192], BF16)
                P.op("sp", lambda e: e.dma_start(out=dbg_outs["ut"], in_=UT[:, :, :, :].rearrange("p g j c -> p (g j c)")),
                     dma=True, slot="dbg3")
            P.emit()
            return nc
        if debug in ("ph5", "ph6"):
            dbg_outs["ht"] = dout("dbg_ht", [128, 16 * 1024], BF16)
            P.op("sp", lambda e: e.dma_start(out=dbg_outs["ht"], in_=HT[:, :, :].rearrange("p k n -> p (k n)")),
                 dma=True, slot="dbg")
        P.emit()
    return nc


def _consts():
    ident = np.eye(128, dtype=np.float32)
    q = np.arange(128)
    pm = np.stack([((q // 16) % 2 == 0), ((q // 16) % 2 == 1)], axis=1).astype(np.float32)
    pairm = (q[:, None] // 32 == np.arange(4)[None, :]).astype(np.float32)
    sel = np.zeros((64, 8, 128), np.float32)
    for gt in range(8):
        for qq in range(128):
            sel[gt * 8 + qq // 16, gt, qq] = 1.0
    return dict(k_ident=ident, k_pm=pm, k_pairm=pairm, k_sel=sel)


def make_in_maps(inp):
    f = lambda a: np.ascontiguousarray(np.asarray(a, dtype=np.float32))
    shared = {}
    for k in ["w_ada", "b_ada", "g_pre_mix", "w_in", "chunk_w_s", "chunk_b_s", "chunk_g_v", "ssm_lam_re", "ssm_lam_im",
              "ssm_log_dt", "ssm_b_re", "ssm_b_im", "ssm_c_re", "ssm_c_im", "ssm_d", "w_glu", "b_glu", "g_out_a",
              "g_out_b", "w_out", "g_post_mix", "g_pre_ffn", "w_ff1", "w_ff2", "g_post_ffn"]:
        shared[k] = f(inp[k])[0]
    shared.update(_consts())
    xs = f(inp["x_sample"]); xp = f(inp["x_prompt"]); stt = f(inp["state_ssm"]); c = f(inp["c"]); cc = f(inp["c_ctx"])
    maps = []
    for i in range(NCORES):
        m = dict(shared)
        m["xs"] = xs[i]
        m["xp"] = np.ascontiguousarray(xp[4 * i:4 * i + 4].reshape(1024, 2048))
        m["st0"] = np.ascontiguousarray(stt[i, 0])
        m["c2"] = np.ascontiguousarray(np.stack([c[i], cc]))
        maps.append(m)
    return maps


_NC_CACHE = {}


def kernel(**inputs):
    if "nc" not in _NC_CACHE:
        _NC_CACHE["nc"] = build()
    nc = _NC_CACHE["nc"]
    maps = make_in_maps(inputs)
    res = run_bass_kernel_spmd(nc, maps, core_ids=list(range(NCORES)))
    r = res.results
    y_sample = np.stack([r[i]["ys"] for i in range(NCORES)]).astype(np.float32)
    y_prompt = np.concatenate([r[i]["yp"].reshape(4, 256, 2048) for i in range(NCORES)]).astype(np.float32)
    ns = np.concatenate([r[i]["ns"].reshape(4, 1, 2, 2, 64, 64) for i in range(NCORES)]).astype(np.float32)
    return (y_prompt, y_sample, ns)
```

```python
import contextlib
import math
import os
_SKIP = os.environ.get('KSKIP', '').split(',')
import numpy as np
import concourse.bass as bass
import concourse.mybir as mybir
from concourse.bass_utils import run_bass_kernel_spmd

F32 = mybir.dt.float32
BF16 = mybir.dt.bfloat16
I32 = mybir.dt.int32
AF = mybir.ActivationFunctionType
ALU = mybir.AluOpType
AX = mybir.AxisListType

EPS = 1e-6
NCORES = 8
TC = 8
NCH = 128
PADL = 64
TWO_PI = 2.0 * math.pi


class Prog:
    ENG = ("pe", "act", "dve", "pool", "sp")

    def __init__(self, nc):
        self.nc = nc
        self.ops = []
        self.last_w = {}
        self.readers = {}
        self.slot_last = {}
        self.last_on_eng = {}
        self.pending_barrier = {}

    def op(self, eng, fn, reads=(), writes=(), dma=False, slot=None, extra_deps=()):
        i = len(self.ops)
        deps = set(extra_deps)
        for k in reads:
            j = self.last_w.get(k)
            if j is not None:
                deps.add(j)
        for k in writes:
            j = self.last_w.get(k)
            if j is not None:
                deps.add(j)
            for r in self.readers.get(k, ()):
                deps.add(r)
        if dma:
            assert slot is not None
            j = self.slot_last.get(slot)
            if j is not None:
                deps.add(j)
            self.slot_last[slot] = i
        if eng in self.pending_barrier:
            deps |= self.pending_barrier.pop(eng)
        deps.discard(i)
        if eng == "pe":
            deps = {j for j in deps if not (self.ops[j]["eng"] == "pe" and not self.ops[j]["dma"])}
        best = {}
        keep = set()
        for j in deps:
            oj = self.ops[j]
            if oj["dma"]:
                keep.add(j)
            else:
                best[oj["eng"]] = max(best.get(oj["eng"], -1), j)
        deps = keep | set(best.values())
        self.ops.append(dict(eng=eng, fn=fn, deps=deps, dma=dma, slot=slot))
        for k in writes:
            self.last_w[k] = i
            self.readers[k] = []
        for k in reads:
            if k not in writes:
                lst = self.readers.setdefault(k, [])
                if not dma:
                    lst[:] = [r for r in lst if self.ops[r]["dma"] or self.ops[r]["eng"] != eng]
                lst.append(i)
        self.last_on_eng[eng] = i
        return i

    def barrier(self):
        deps = set(self.last_on_eng.values()) | set(self.slot_last.values())
        for e in self.ENG:
            self.pending_barrier[e] = set(deps) | self.pending_barrier.get(e, set())

    def emit(self, final_wait_eng="sp"):
        nc = self.nc
        ops = self.ops
        n = len(ops)
        needed = [False] * n
        for o in ops:
            for j in o["deps"]:
                needed[j] = True
        for i, o in enumerate(ops):
            if o["dma"]:
                needed[i] = True
        slots = sorted({o["slot"] for o in ops if o["dma"]}, key=str)
        LIMIT = 1000
        with contextlib.ExitStack() as st:
            tok = [None] * n
            ecount = {e: 0 for e in self.ENG}
            scount = {s: 0 for s in slots}
            for i, o in enumerate(ops):
                if o["dma"]:
                    scount[o["slot"]] += 16
                    tok[i] = (("d", o["slot"]), scount[o["slot"]])
                elif needed[i]:
                    c = ecount[o["eng"]]
                    ecount[o["eng"]] += 1
                    tok[i] = (("e", o["eng"]), (c // LIMIT) * 1000000 + (c % LIMIT) + 1)
            self.signal_counts = dict(ecount)
            esem = {}
            for e in self.ENG:
                for ep in range(ecount[e] // LIMIT + 1):
                    esem[(e, ep)] = st.enter_context(nc.semaphore("s_%s_%d" % (e, ep)))
            ssem = {s: st.enter_context(nc.semaphore("d_%d" % k)) for k, s in enumerate(slots)}

            def sem_and_val(key, val):
                if key[0] == "e":
                    return esem[(key[1], val // 1000000)], val % 1000000
                return ssem[key[1]], val

            block = st.enter_context(nc.Block())
            handles = {"pe": "tensor", "act": "scalar", "dve": "vector", "pool": "gpsimd", "sp": "sync"}

            def run_engine(ename, eng):
                waited = {}
                for i, o in enumerate(ops):
                    if o["eng"] != ename:
                        continue
                    for j in sorted(o["deps"]):
                        if tok[j] is None:
                            continue
                        key, val = tok[j]
                        if key == ("e", ename) and ename == "pe":
                            continue
                        if waited.get(key, 0) < val:
                            sm, vv = sem_and_val(key, val)
                            eng.wait_ge(sm, vv)
                            waited[key] = val
                    inst = o["fn"](eng)
                    if tok[i] is not None:
                        key, val = tok[i]
                        assert inst is not None
                        sm, vv = sem_and_val(key, val)
                        inst.then_inc(sm, 16 if o["dma"] else 1)
                if ename == final_wait_eng:
                    for s_ in slots:
                        if waited.get(("d", s_), 0) < scount[s_]:
                            eng.wait_ge(ssem[s_], scount[s_])

            for ename in self.ENG:
                dec = getattr(block, handles[ename])

                def _mk(en):
                    def body(eng):
                        run_engine(en, eng)
                    return body
                dec(_mk(ename))
        return nc


def _ap(t, offset, dims):
    return bass.AP(tensor=t, offset=offset, ap=[list(d) for d in dims])


def build(debug=None):
    nc = bass.Bass("TRN2", target_bir_lowering=False)

    def din(name, shape, dt=F32):
        return nc.dram_tensor(name, list(shape), dt, kind="ExternalInput").ap()

    def dout(name, shape, dt=F32):
        return nc.dram_tensor(name, list(shape), dt, kind="ExternalOutput").ap()

    x_in = [din("xs", [1024, 2048]), din("xp", [1024, 2048])]
    st0 = din("st0", [2, 2, 64, 64])
    c2 = din("c2", [2, 2048])
    w_ada = din("w_ada", [2048, 12288])
    b_ada = din("b_ada", [12288])
    g_pre_mix = din("g_pre_mix", [2048])
    w_in = din("w_in", [2048, 3072])
    chunk_w_s = din("chunk_w_s", [8, 128, 128])
    chunk_b_s = din("chunk_b_s", [8, 128])
    chunk_g_v = din("chunk_g_v", [1024])
    lam_re = din("ssm_lam_re", [2, 64, 64])
    lam_im = din("ssm_lam_im", [2, 64, 64])
    log_dt = din("ssm_log_dt", [2, 64])
    b_re = din("ssm_b_re", [2, 64, 64, 16])
    b_im = din("ssm_b_im", [2, 64, 64, 16])
    c_re = din("ssm_c_re", [2, 64, 16, 64])
    c_im = din("ssm_c_im", [2, 64, 16, 64])
    ssm_d = din("ssm_d", [1024])
    w_glu = din("w_glu", [1024, 1024])
    b_glu = din("b_glu", [1024])
    g_out_a = din("g_out_a", [1024])
    g_out_b = din("g_out_b", [1024])
    w_out = din("w_out", [2048, 2048])
    g_post_mix = din("g_post_mix", [2048])
    g_pre_ffn = din("g_pre_ffn", [2048])
    w_ff1 = din("w_ff1", [2048, 8192])
    w_ff2 = din("w_ff2", [8192, 2048])
    g_post_ffn = din("g_post_ffn", [2048])
    ident_d = din("k_ident", [128, 128])
    pm_d = din("k_pm", [128, 2])
    pairm_d = din("k_pairm", [128, 4])
    sel_d = din("k_sel", [64, 8, 128])

    y_out = [dout("ys", [1024, 2048]), dout("yp", [1024, 2048])]
    ns_out = dout("ns", [4, 2, 2, 64, 64])

    NSW = 8 * 2 * 2 * 128 + 9 * 2 * 2 * 128 + 15 * 128
    OFF_B, OFF_C, OFF_K = 0, 4096, 4096 + 4608
    swb_d = nc.dram_tensor("swb", [8, 128, NSW], BF16, kind="Internal").ap()
    NMU = 4 * 2 * 7 * 2
    swm_d = nc.dram_tensor("swm", [8, 128, NMU], F32, kind="Internal").ap()

    dbg_outs = {}

    with contextlib.ExitStack() as st:
        def sb(name, shape, dt=F32):
            return st.enter_context(nc.sbuf_tensor(name, list(shape), dt))

        BIG = sb("BIG", [128, 16384], F32)
        HT = sb("HT", [128, 16, 1024], BF16)
        VHR = sb("VHR", [128, 8192], BF16)
        UT = sb("UT", [128, 8, TC, NCH], BF16)
        WB = [sb("WB%d" % i, [128, 8192], BF16) for i in range(3)]
        XT = [sb("XT%d" % i, [128, 2048], F32) for i in range(2)]
        IDF = sb("IDF", [128, 128], F32)
        IDB = sb("IDB", [128, 128], BF16)
        ONESF = sb("ONESF", [128, 128], F32)
        PMK = sb("PMK", [128, 2], F32)
        PAIRM = sb("PAIRM", [128, 4], F32)
        VT1 = sb("VT1", [128, 96], F32)
        VT2 = sb("VT2", [128, 96], F32)
        MODT = sb("MODT", [128, 96, 2], F32)
        SCM = sb("SCM", [128, 16, 2], F32)
        SCF = sb("SCF", [128, 16, 2], F32)
        SCT = sb("SCT", [128, 16, 2], BF16)
        WST = sb("WST", [128, 8, 128], BF16)
        BSP = sb("BSP", [128, 8], F32)
        EPSC = sb("EPSC", [128, 1], F32)
        SMALL = sb("SMALL", [128, 256], F32)
        BNS = sb("BNS", [128, 8, 4, 6], F32)
        PS = [st.enter_context(nc.psum_tensor("PS%d" % i, [128, 512], F32)) for i in range(8)]

        P = Prog(nc)
        cnt = {"big": 0, "hi": 0, "ev": 0}

        def big_bank():
            b = cnt["big"] % 4
            cnt["big"] += 1
            return b

        def hi_bank():
            b = 4 + cnt["hi"] % 4
            cnt["hi"] += 1
            return b

        def ev_eng():
            cnt["ev"] += 1
            return "act" if cnt["ev"] % 2 else "dve"

        def kps(b):
            return [("ps", b)]

        def smcol(i, n=1):
            return SMALL[:, i:i + n]

        wblocks = []

        def add_wblock(src, kind):
            wblocks.append((src, kind))
            return len(wblocks) - 1

        def wview(b, kind):
            t = WB[b]
            if kind == "k16":
                return t[:, :].rearrange("p (k n) -> p k n", k=16)
            if kind == "k8":
                return t[:, 0:4096].rearrange("p (k n) -> p k n", k=8)
            if kind == "c4":
                return t[:, :].rearrange("p (c n) -> p c n", c=4)
            raise ValueError(kind)

        wstate = {"issued": 0}

        def w_issue_upto(i):
            while wstate["issued"] <= min(i, len(wblocks) - 1):
                j = wstate["issued"]
                src, kind = wblocks[j]
                b = j % 3
                P.op("pool", lambda e, b=b, kind=kind, src=src: e.dma_start(out=wview(b, kind), in_=src),
                     writes=[("wb", b)], dma=True, slot=("wb", b))
                wstate["issued"] += 1

        def w_acquire(i):
            w_issue_upto(i + 2)
            return i % 3

        def ada_block(j):
            return add_wblock(w_ada[:, j * 512:(j + 1) * 512].rearrange("(k p) n -> p k n", p=128), "k16")

        wb_ada = [ada_block(j) for j in range(8)]
        wb_unit = []
        for u in range(2):
            d = {}
            d["win"] = [add_wblock(w_in[:, j * 512:(j + 1) * 512].rearrange("(k p) n -> p k n", p=128), "k16")
                        for j in range(6)]
            if u == 0:
                wb_ada += [ada_block(j) for j in range(8, 24)]
            d["glu"] = [add_wblock(w_glu[:, j * 512:(j + 1) * 512].rearrange("(k p) n -> p k n", p=128), "k8")
                        for j in range(2)]
            d["wout"] = [add_wblock(w_out[:, j * 512:(j + 1) * 512].rearrange("(k p) n -> p k n", p=128), "k16")
                         for j in range(4)]
            d["ffn"] = []
            for g in range(16):
                i1 = add_wblock(w_ff1[:, g * 512:(g + 1) * 512].rearrange("(k p) n -> p k n", p=128), "k16")
                i2 = add_wblock(w_ff2[g * 512:(g + 1) * 512, :].rearrange("(c p) n -> p c n", p=128), "c4")
                d["ffn"].append((i1, i2))
            wb_unit.append(d)

        def rstd_from_bn(slots_ap, nslots, out_col, scratch_col):
            mv = smcol(scratch_col, 2)
            P.op("dve", lambda e: e.bn_aggr(out=mv, in_=slots_ap), reads=[("bns",)], writes=[("sm", scratch_col)])
            P.op("dve", lambda e: e.scalar_tensor_tensor(out=smcol(scratch_col + 2), in0=mv[:, 0:1], scalar=mv[:, 0:1],
                                                         in1=mv[:, 1:2], op0=ALU.mult, op1=ALU.add),
                 reads=[("sm", scratch_col)], writes=[("sm", scratch_col + 2)])
            P.op("act", lambda e: e.activation(out=smcol(scratch_col + 3), in_=smcol(scratch_col + 2), func=AF.Sqrt,
                                               bias=EPSC[:, 0:1], scale=1.0),
                 reads=[("sm", scratch_col + 2)], writes=[("sm", scratch_col + 3)])
            P.op("dve", lambda e: e.reciprocal(out=smcol(out_col), in_=smcol(scratch_col + 3)),
                 reads=[("sm", scratch_col + 3)], writes=[("sm", out_col)])

        P.op("sp", lambda e: e.dma_start(out=IDF[:, :], in_=ident_d), writes=[("idf",)], dma=True, slot="c0")
        P.op("sp", lambda e: e.dma_start(out=PMK[:, :], in_=pm_d), writes=[("pmk",)], dma=True, slot="c1")
        P.op("sp", lambda e: e.dma_start(out=PAIRM[:, :], in_=pairm_d), writes=[("pairm",)], dma=True, slot="c2")
        P.op("dve", lambda e: e.tensor_copy(out=IDB[:, :], in_=IDF[:, :]), reads=[("idf",)], writes=[("idb",)])
        P.op("dve", lambda e: e.memset(ONESF[:, :], 1.0), writes=[("onesf",)])
        P.op("dve", lambda e: e.memset(EPSC[:, :], EPS), writes=[("epsc",)])

        VN1 = XT[0][0:96, 0:128]
        VN2 = XT[0][0:96, 128:256]
        vecs = [(c2.rearrange("v (k q) -> (v k) q", q=128), 0, 32),
                (g_pre_mix.rearrange("(k q) -> k q", q=128), 32, 16),
                (g_pre_ffn.rearrange("(k q) -> k q", q=128), 48, 16),
                (g_out_a.rearrange("(k q) -> k q", q=128), 64, 8),
                (g_out_b.rearrange("(k q) -> k q", q=128), 72, 8),
                (b_glu.rearrange("(k q) -> k q", q=128), 80, 8),
                (ssm_d.rearrange("(k q) -> k q", q=128), 88, 8)]
        for vi, (src, r0, nr) in enumerate(vecs):
            P.op("sp", lambda e, src=src, r0=r0, nr=nr: e.dma_start(out=XT[0][r0:r0 + nr, 0:128], in_=src),
                 writes=[("vn1", vi)], dma=True, slot="c%d" % (vi % 3))
        P.op("sp", lambda e: e.dma_start(out=VN2, in_=b_ada.rearrange("(r q) -> r q", q=128)),
             writes=[("vn2",)], dma=True, slot="c0")
        P.op("pe", lambda e: e.transpose(out=PS[7][:, 0:96], in_=VN1, identity=IDF[0:96, 0:96]),
             reads=[("vn1", i) for i in range(7)] + [("idf",)], writes=kps(7))
        P.op("dve", lambda e: e.tensor_copy(out=VT1[:, :], in_=PS[7][:, 0:96]), reads=kps(7), writes=[("vt1",)])
        P.op("pe", lambda e: e.transpose(out=PS[6][:, 0:96], in_=VN2, identity=IDF[0:96, 0:96]),
             reads=[("vn2",), ("idf",)], writes=kps(6))
        P.op("dve", lambda e: e.tensor_copy(out=VT2[:, :], in_=PS[6][:, 0:96]), reads=kps(6), writes=[("vt2",)])
        for v in range(2):
            P.op("act", lambda e, v=v: e.activation(out=SCT[:, :, v], in_=VT1[:, v * 16:(v + 1) * 16], func=AF.Silu),
                 reads=[("vt1",)], writes=[("sct", v)])
        GPM, GPF, GOA, GOB, BGL, DSK = (VT1[:, 32:48], VT1[:, 48:64], VT1[:, 64:72], VT1[:, 72:80],
                                        VT1[:, 80:88], VT1[:, 88:96])
        WSN = XT[1][:, 0:1024].rearrange("p (h q) -> p h q", h=8)
        P.op("sp", lambda e: e.dma_start(out=WSN, in_=chunk_w_s.rearrange("h p q -> p h q")),
             writes=[("wsn",)], dma=True, slot="c1")
        P.op("sp", lambda e: e.dma_start(out=XT[1][0:8, 1024:1152], in_=chunk_b_s),
             writes=[("bsn",)], dma=True, slot="c2")
        for h in range(8):
            b = hi_bank()
            P.op("pe", lambda e, h=h, b=b: e.transpose(out=PS[b][:, 0:128], in_=WSN[:, h, :], identity=IDF[:, :]),
                 reads=[("wsn",), ("idf",)], writes=kps(b))
            P.op("dve", lambda e, h=h, b=b: e.tensor_copy(out=WST[:, h, :], in_=PS[b][:, 0:128]),
                 reads=kps(b), writes=[("wst", h)])
        b = hi_bank()
        P.op("pe", lambda e, b=b: e.transpose(out=PS[b][:, 0:8], in_=XT[1][0:8, 1024:1152], identity=IDF[0:8, 0:8]),
             reads=[("bsn",), ("idf",)], writes=kps(b))
        P.op("dve", lambda e, b=b: e.tensor_copy(out=BSP[:, :], in_=PS[b][:, 0:8]), reads=kps(b), writes=[("bsp",)])

        H0T = sb("H0T", [128, 128], F32)
        BIGH = HT[:, :, :].rearrange("p k n -> p (k n)").bitcast(F32)
        _go = {"o": 0}

        _regions = [BIG[:, :], BIGH, UT[:, :, :, :].rearrange("p g j c -> p (g j c)").bitcast(F32),
                    VHR[:, :].bitcast(F32)]
        _regions += [XT[0][:, 512:2048], XT[1][:, 1152:2048]]
        _rsz = [16384, 8192, 4096, 4096, 1536, 896]

        _roff = [0] * 6

        def galloc(n):
            for r in range(len(_regions)):
                if _roff[r] + n <= _rsz[r]:
                    o = _roff[r]
                    _roff[r] += n
                    return _regions[r][:, o:o + n]
            raise AssertionError("gen scratch exhausted")

        _pw = [galloc(9 * 256) for _ in range(2)]
        G_TBC = galloc(9 * 256)
        G_CMS = [galloc(2304).bitcast(BF16) for _ in range(2)]
        G_CST = galloc(2304).bitcast(BF16)
        G_TBB = galloc(8 * 256)
        G_BST = galloc(2048).bitcast(BF16)
        G_BN = galloc(2048)
        G_CR = galloc(2048)
        G_TMPC = galloc(9 * 128)
        G_SEL = galloc(1024)
        G_TMPB = galloc(8 * 128)
        G_KT = galloc(960).bitcast(BF16)
        G_KS = galloc(512)
        G_MM = galloc(512)
        G_LN = galloc(258)
        G_BXS = [galloc(256).bitcast(BF16) for _ in range(2)]
        GS = []
        for _sx in range(2):
            GS.append(dict(PR=galloc(258), DT=galloc(2), T=[galloc(128) for _ in range(16)], BRP=galloc(256),
                           BB=galloc(256), PW=_pw[_sx]))
        G_K0 = galloc(128)
        G_MU = galloc(112)
        G_MT = galloc(64)

        def v3(ap, d0, d1):
            return ap.rearrange("q (a b) -> q a b", a=d0, b=d1)

        def ssm_param_loads():
            P.op("sp", lambda e: e.dma_start(out=G_SEL[0:64, :], in_=sel_d.rearrange("g t q -> g (t q)")),
                 writes=[("g_sel",)], dma=True, slot="pl_sel")
            for d in range(2):
                P.op("sp", lambda e, d=d: e.dma_start(out=G_LN[0:64, d * 128:d * 128 + 64], in_=lam_re[d]),
                     writes=[("g_ln", d, 0)], dma=True, slot=("pl_ln0", d))
                P.op("sp", lambda e, d=d: e.dma_start(out=G_LN[0:64, d * 128 + 64:d * 128 + 128], in_=lam_im[d]),
                     writes=[("g_ln", d, 1)], dma=True, slot=("pl_ln1", d))
                P.op("sp", lambda e, d=d: e.dma_start(out=G_LN[0:64, 256 + d:257 + d],
                                                     in_=log_dt[d].rearrange("(g o) -> g o", o=1)),
                     writes=[("g_ln", d, 2)], dma=True, slot=("pl_ln2", d))
                for ri, src in enumerate((b_re, b_im)):
                    dst = G_BN[64 * d:64 * d + 64, ri * 1024:(ri + 1) * 1024].rearrange("p (g h) -> p g h", g=64)
                    P.op("sp", lambda e, dst=dst, src=src, d=d: e.dma_start(out=dst, in_=src[d].rearrange("g p h -> p g h")),
                         writes=[("g_bn", d, ri)], dma=True, slot=("pl", d, ri, "1"))
                for ri, src in enumerate((c_re, c_im)):
                    dst = G_CR[:, (d * 2 + ri) * 512:(d * 2 + ri + 1) * 512].rearrange("q (t p) -> q t p", t=8)
                    P.op("sp", lambda e, dst=dst, src=src, d=d: e.dma_start(
                        out=dst, in_=src[d].rearrange("(t g) h p -> (g h) t p", t=8)),
                        writes=[("g_cr", d, ri)], dma=True, slot=("pl", d, ri, "2"))
            for d in range(2):
                sl = G_CR[:, (d * 2 + 1) * 512:(d * 2 + 2) * 512]
                P.op("dve", lambda e, sl=sl: e.tensor_scalar(out=sl, in0=sl, scalar1=-1.0, scalar2=None, op0=ALU.mult),
                     reads=[("g_cr", d, 1)], writes=[("g_cr", d, 1)])
            P.op("sp", lambda e: e.dma_start(out=XT[0][:, 256:384],
                                             in_=st0.rearrange("d r (w s) p -> (d r w) (s p)", s=2)),
                 writes=[("h0n",)], dma=True, slot="c0")
            P.op("pe", lambda e: e.transpose(out=PS[5][:, 0:128], in_=XT[0][:, 256:384], identity=IDF[:, :]),
                 reads=[("h0n",), ("idf",)], writes=kps(5))
            P.op("dve", lambda e: e.tensor_copy(out=H0T[:, :], in_=PS[5][:, 0:128]), reads=kps(5), writes=[("h0t",)])

        gparam_keys = [("g_sel",)] + [("g_ln", d, i) for d in range(2) for i in range(3)]
        gbn_keys = [("g_bn", d, r) for d in range(2) for r in range(2)]
        gcr_keys = [("g_cr", d, r) for d in range(2) for r in range(2)]

        def cmul(eng, o_r, o_i, a_r, a_i, b_r, b_i, t, rk, wk, neg_im=False):
            rk = list(rk) + list(wk)
            wk = list(wk)

            def tt(out, a, b, op):
                P.op(eng, lambda e: e.tensor_tensor(out=out, in0=a, in1=b, op=op), reads=rk, writes=wk)
            if neg_im:
                tt(t, a_i, b_i, ALU.mult)
                tt(o_r, a_r, b_r, ALU.mult)
                tt(o_r, o_r, t, ALU.add)
                tt(t, a_i, b_r, ALU.mult)
                tt(o_i, a_r, b_i, ALU.mult)
                tt(o_i, o_i, t, ALU.subtract)
                return
            tt(t, a_i, b_i, ALU.mult)
            tt(o_r, a_r, b_r, ALU.mult)
            tt(o_r, o_r, t, ALU.subtract)
            tt(t, a_i, b_r, ALU.mult)
            tt(o_i, a_r, b_i, ALU.mult)
            tt(o_i, o_i, t, ALU.add)

        def genA(gt):
            E = "dve"
            sx = gt % 2
            S = GS[sx]
            KG = ("gen", sx)
            G_PR, G_DT, G_BRP, G_BB, G_PW = S["PR"], S["DT"], S["BRP"], S["BB"], S["PW"]
            pb = 4 + (gt % 2) * 2

            G_BX = G_BXS[sx]
            G_CM = G_CMS[sx]
            kcm = ("g_cm", sx)
            PW = G_PW.rearrange("q (n d r p) -> q n d r p", n=9, d=2, r=2)
            PSB = [PS[b_][:, :].bitcast(BF16) for b_ in range(8)]
            def tt(eng, out, a, b, op):
                P.op(eng, lambda e: e.tensor_tensor(out=out, in0=a, in1=b, op=op), reads=[KG], writes=[KG])
            P.op("pe", lambda e: e.matmul(PS[pb][:, 0:258], lhsT=G_SEL[0:64, gt * 128:(gt + 1) * 128], rhs=G_LN[0:64, :],
                                          start=True, stop=True), reads=gparam_keys, writes=kps(pb))
            P.op("dve", lambda e: e.tensor_copy(out=G_PR, in_=PS[pb][:, 0:258]), reads=kps(pb), writes=[KG])
            PRv = G_PR[:, 0:256].rearrange("q (d r p) -> q d r p", d=2, r=2)
            LR, LI = PRv[:, :, 0, :], PRv[:, :, 1, :]
            P.op("act", lambda e: e.activation(out=G_DT, in_=G_PR[:, 256:258], func=AF.Exp), reads=[KG], writes=[KG])
            T = [v3(x, 2, 64) for x in S["T"]]
            A, Bq, MAG, R, SINB, COSB, LBR, LBI, DEN, INV, NR, CRr, CIi, TM, TM2, TM3 = T
            dtb = G_DT.unsqueeze(2).to_broadcast([128, 2, 64])
            tt(E, A, LR, dtb, ALU.mult)
            tt(E, Bq, LI, dtb, ALU.mult)
            P.op("act", lambda e: e.activation(out=MAG, in_=A, func=AF.Exp), reads=[KG], writes=[KG])
            TMi = TM.bitcast(I32)
            P.op(E, lambda e: e.tensor_scalar(out=TMi, in0=Bq, scalar1=1.0 / TWO_PI, scalar2=None, op0=ALU.mult),
                 reads=[KG], writes=[KG])
            P.op(E, lambda e: e.tensor_copy(out=TM2, in_=TMi), reads=[KG], writes=[KG])
            P.op(E, lambda e: e.scalar_tensor_tensor(out=R, in0=TM2, scalar=-TWO_PI, in1=Bq, op0=ALU.mult, op1=ALU.add),
                 reads=[KG], writes=[KG])
            P.op(E, lambda e: e.tensor_scalar(out=TM2, in0=R, scalar1=math.pi, scalar2=-TWO_PI, op0=ALU.is_gt,
                                              op1=ALU.mult), reads=[KG], writes=[KG])
            tt(E, R, R, TM2, ALU.add)
            P.op(E, lambda e: e.tensor_scalar(out=TM2, in0=R, scalar1=-math.pi, scalar2=TWO_PI, op0=ALU.is_lt,
                                              op1=ALU.mult), reads=[KG], writes=[KG])
            tt(E, R, R, TM2, ALU.add)
            P.op("act", lambda e: e.activation(out=SINB, in_=R, func=AF.Sin), reads=[KG], writes=[KG])
            P.op(E, lambda e: e.tensor_scalar(out=TM3, in0=R, scalar1=math.pi / 2, scalar2=None, op0=ALU.add),
                 reads=[KG], writes=[KG])
            P.op(E, lambda e: e.tensor_scalar(out=TM2, in0=TM3, scalar1=math.pi, scalar2=-TWO_PI, op0=ALU.is_gt,
                                              op1=ALU.mult), reads=[KG], writes=[KG])
            tt(E, TM3, TM3, TM2, ALU.add)
            P.op("act", lambda e: e.activation(out=COSB, in_=TM3, func=AF.Sin), reads=[KG], writes=[KG])
            tt(E, LBR, MAG, COSB, ALU.mult)
            tt(E, LBI, MAG, SINB, ALU.mult)
            tt(E, DEN, LR, LR, ALU.mult)
            tt(E, TM, LI, LI, ALU.mult)
            tt(E, DEN, DEN, TM, ALU.add)
            P.op(E, lambda e: e.reciprocal(out=INV, in_=DEN), reads=[KG], writes=[KG])
            P.op(E, lambda e: e.tensor_scalar(out=NR, in0=LBR, scalar1=-1.0, scalar2=None, op0=ALU.add),
                 reads=[KG], writes=[KG])
            tt(E, CRr, NR, LR, ALU.mult)
            tt(E, TM, LBI, LI, ALU.mult)
            tt(E, CRr, CRr, TM, ALU.add)
            tt(E, CRr, CRr, INV, ALU.mult)
            tt(E, CIi, LBI, LR, ALU.mult)
            tt(E, TM, NR, LI, ALU.mult)
            tt(E, CIi, CIi, TM, ALU.subtract)
            tt(E, CIi, CIi, INV, ALU.mult)
            for d in range(2):
                for ri in range(2):
                    src = G_BN[64 * d:64 * d + 64, ri * 1024 + gt * 128:ri * 1024 + (gt + 1) * 128]
                    bnk = pb + 1 - d
                    P.op("pe", lambda e, src=src, d=d, ri=ri, bnk=bnk: e.transpose(
                        out=PS[bnk][:, ri * 64:(ri + 1) * 64], in_=src,
                        identity=IDF[64 * d:64 * d + 64, 64 * d:64 * d + 64]),
                        reads=gbn_keys + [("idf",)], writes=kps(bnk))
            for d in range(2):
                P.op("dve", lambda e, d=d: e.tensor_copy(out=G_BRP[:, d * 128:(d + 1) * 128], in_=PS[pb + 1 - d][:, 0:128]),
                     reads=kps(pb + 1 - d), writes=[KG])
            BRv = G_BRP.rearrange("q (d r p) -> q d r p", d=2, r=2)
            BBv = G_BB.rearrange("q (d r p) -> q d r p", d=2, r=2)
            cmul(E, BBv[:, :, 0, :], BBv[:, :, 1, :], CRr, CIi, BRv[:, :, 0, :], BRv[:, :, 1, :], TM, [KG], [KG])
            PW = G_PW.rearrange("q (n d r p) -> q n d r p", n=9, d=2, r=2)
            P.op("pool", lambda e: e.memset(PW[:, 0, :, 0, :], 1.0), reads=[KG], writes=[KG])
            P.op("pool", lambda e: e.memset(PW[:, 0, :, 1, :], 0.0), reads=[KG], writes=[KG])
            P.op(E, lambda e: e.tensor_copy(out=PW[:, 1, :, 0, :], in_=LBR), reads=[KG], writes=[KG])
            P.op(E, lambda e: e.tensor_copy(out=PW[:, 1, :, 1, :], in_=LBI), reads=[KG], writes=[KG])
            TMPb = G_TMPB.rearrange("q (n d p) -> q n d p", n=8, d=2)
            for (lo, cnt_, m) in ((1, 1, 1), (1, 2, 2), (1, 4, 4)):
                br = PW[:, m, :, 0, :].unsqueeze(1).to_broadcast([128, cnt_, 2, 64])
                bi = PW[:, m, :, 1, :].unsqueeze(1).to_broadcast([128, cnt_, 2, 64])
                cmul(E, PW[:, m + lo:m + lo + cnt_, :, 0, :], PW[:, m + lo:m + lo + cnt_, :, 1, :],
                     PW[:, lo:lo + cnt_, :, 0, :], PW[:, lo:lo + cnt_, :, 1, :], br, bi, TMPb[:, 0:cnt_, :, :],
                     [KG, ("g_tbb",)], [KG, ("g_tbb",)])
            TBb = G_TBB.rearrange("q (n d r p) -> q n d r p", n=8, d=2, r=2)
            bbr = BBv[:, :, 0, :].unsqueeze(1).to_broadcast([128, 8, 2, 64])
            bbi = BBv[:, :, 1, :].unsqueeze(1).to_broadcast([128, 8, 2, 64])
            cmul("dve", TBb[:, :, :, 0, :], TBb[:, :, :, 1, :], PW[:, 0:8, :, 0, :], PW[:, 0:8, :, 1, :], bbr, bbi,
                 TMPb, [KG], [("g_tbb",)])
            BST = G_BST.rearrange("q (m s p) -> q m s p", m=32, s=2)
            for s2 in range(2):
                P.op("act", lambda e, s2=s2: e.activation(
                    out=BST[:, :, s2, :], in_=G_TBB.rearrange("q (m p) -> q m p", m=32), func=AF.Copy,
                    scale=PMK[:, s2:s2 + 1]), reads=[("g_tbb",), ("pmk",)], writes=[("g_bst",)])
            P.op("sp", lambda e: e.dma_start(out=swb_d[gt, :, OFF_B:OFF_B + 4096], in_=G_BST),
                 reads=[("g_bst",)], writes=[("swb", gt)], dma=True, slot="gen0")
            PSB = [PS[b][:, :].bitcast(BF16) for b in range(8)]
            BST6 = G_BST.rearrange("q (n d r m) -> q n d r m", n=8, d=2, r=2)
            for d in range(2):
                for ri in range(2):
                    P.op("pe", lambda e, d=d, ri=ri: e.transpose(
                        out=PSB[pb + 1][:, (d * 2 + ri) * 128:(d * 2 + ri + 1) * 128], in_=BST6[:, 0, d, ri, :],
                        identity=IDB[:, :]), reads=[("g_bst",), ("idb",)], writes=kps(pb + 1))
            P.op("act", lambda e: e.activation(out=G_BX, in_=PSB[pb + 1][:, 0:512], func=AF.Copy), reads=kps(pb + 1),
                 writes=[("g_bx",)])
        def genB(gt):
            E = "dve"
            sx = gt % 2
            S = GS[sx]
            KG = ("gen", sx)
            G_PR, G_DT, G_BRP, G_BB, G_PW = S["PR"], S["DT"], S["BRP"], S["BB"], S["PW"]
            pb = 4 + (gt % 2) * 2

            G_BX = G_BXS[sx]
            G_CM = G_CMS[sx]
            kcm = ("g_cm", sx)
            PW = G_PW.rearrange("q (n d r p) -> q n d r p", n=9, d=2, r=2)
            PSB = [PS[b_][:, :].bitcast(BF16) for b_ in range(8)]
            TBc = G_TBC.rearrange("q (n d r p) -> q n d r p", n=9, d=2, r=2)
            TMPc = G_TMPC.rearrange("q (n d p) -> q n d p", n=9, d=2)
            CRv = G_CR.rearrange("q (d r t p) -> q d r t p", d=2, r=2, t=8)
            cr = CRv[:, :, 0, gt, :].unsqueeze(1).to_broadcast([128, 9, 2, 64])
            ci = CRv[:, :, 1, gt, :].unsqueeze(1).to_broadcast([128, 9, 2, 64])
            cmul("dve", TBc[:, :, :, 0, :], TBc[:, :, :, 1, :], PW[:, :, :, 0, :], PW[:, :, :, 1, :], cr, ci, TMPc,
                 [KG] + gcr_keys, [("g_tbc",)], neg_im=True)
            CM = G_CM.rearrange("q (m s p) -> q m s p", m=36, s=2)
            for s2 in range(2):
                P.op("act", lambda e, s2=s2: e.activation(
                    out=CM[:, :, s2, :], in_=G_TBC.rearrange("q (m p) -> q m p", m=36), func=AF.Copy,
                    scale=PMK[:, s2:s2 + 1]), reads=[("g_tbc",), ("pmk",)], writes=[kcm])
        def genC(gt):
            E = "dve"
            sx = gt % 2
            S = GS[sx]
            KG = ("gen", sx)
            G_PR, G_DT, G_BRP, G_BB, G_PW = S["PR"], S["DT"], S["BRP"], S["BB"], S["PW"]
            pb = 4 + (gt % 2) * 2

            G_BX = G_BXS[sx]
            G_CM = G_CMS[sx]
            kcm = ("g_cm", sx)
            PW = G_PW.rearrange("q (n d r p) -> q n d r p", n=9, d=2, r=2)
            PSB = [PS[b_][:, :].bitcast(BF16) for b_ in range(8)]
            CM3 = G_CM.rearrange("q (m c) -> q m c", m=36)
            CST3 = G_CST.rearrange("q (m c) -> q m c", m=36)
            for grp in range(5):
                m0 = grp * 8
                mcnt = min(8, 36 - m0)
                bb = pb + (grp % 2)
                for i in range(mcnt):
                    P.op("pe", lambda e, m=m0 + i, i=i, bb=bb: e.transpose(
                        out=PSB[bb][:, i * 128:(i + 1) * 128], in_=CM3[:, m, :], identity=IDB[:, :]),
                        reads=[kcm, ("idb",)], writes=kps(bb))
                P.op("act" if grp % 2 else "dve",
                     (lambda e, m0=m0, mcnt=mcnt, bb=bb: e.activation(
                         out=CST3[:, m0:m0 + mcnt, :], in_=PSB[bb][:, 0:mcnt * 128].rearrange("q (m c) -> q m c", m=mcnt),
                         func=AF.Copy)) if grp % 2 else
                     (lambda e, m0=m0, mcnt=mcnt, bb=bb: e.tensor_copy(
                         out=CST3[:, m0:m0 + mcnt, :], in_=PSB[bb][:, 0:mcnt * 128].rearrange("q (m c) -> q m c", m=mcnt))),
                     reads=kps(bb), writes=[("g_cst", grp)])
            cst_keys = [("g_cst", g) for g in range(5)]
            P.op("sp", lambda e: e.dma_start(out=swb_d[gt, :, OFF_C:OFF_C + 4608], in_=G_CST),
                 reads=cst_keys, writes=[("swb", gt)], dma=True, slot="gen1")
            CST5 = G_CST.rearrange("q (n d r c) -> q n d r c", n=9, d=2, r=2)
            BX4 = G_BX.rearrange("q (d r c) -> q d r c", d=2, r=2)
            for d in range(2):
                for w in range(4):
                    for ri in range(2):
                        P.op("pe", lambda e, d=d, w=w, ri=ri: e.matmul(
                            PS[pb][32 * w:32 * w + 32, d * 256:(d + 1) * 256],
                            lhsT=BX4[:, d, ri, 32 * w:32 * w + 32], rhs=CST5[:, 0:8, d, ri, 32 * w:32 * w + 32],
                            start=(ri == 0), stop=(ri == 1), tile_position=(0, 32 * w)),
                            reads=cst_keys + [("g_bx",)], writes=kps(pb))
            P.op("dve", lambda e: e.tensor_copy(out=G_KS, in_=PS[pb][:, 0:512]), reads=kps(pb), writes=[("g_ks",)])
            KS = G_KS.rearrange("q (d n c) -> q d n c", d=2, n=8)
            KT = G_KT.rearrange("q (l w c) -> q l w c", l=15, w=4)
            pmb7 = PAIRM[:, :].unsqueeze(1).unsqueeze(3).to_broadcast([128, 7, 4, 32])
            for d in range(2):
                P.op("dve", lambda e, d=d: e.tensor_tensor(
                    out=KT[:, 1 + 7 * d:8 + 7 * d, :, :], in0=KS[:, d, 1:8, :].unsqueeze(2).to_broadcast([128, 7, 4, 32]),
                    in1=pmb7, op=ALU.mult), reads=[("g_ks",), ("pairm",)], writes=[("g_kt",)])
            K0 = G_K0.rearrange("q (w c) -> q w c", w=4)
            P.op("dve", lambda e: e.tensor_tensor(out=G_MT[:, 0:32], in0=KS[:, 0, 0, :], in1=KS[:, 1, 0, :], op=ALU.add),
                 reads=[("g_ks",)], writes=[("g_k0",)])
            P.op("dve", lambda e: e.tensor_tensor(
                out=K0, in0=G_MT[:, 0:32].unsqueeze(1).to_broadcast([128, 4, 32]),
                in1=PAIRM[:, :].unsqueeze(2).to_broadcast([128, 4, 32]), op=ALU.mult),
                reads=[("g_k0",), ("pairm",)], writes=[("g_k0",)])
            P.op("dve", lambda e: e.scalar_tensor_tensor(
                out=G_KT[:, 0:128], in0=IDF[:, :], scalar=DSK[:, gt:gt + 1], in1=G_K0, op0=ALU.mult, op1=ALU.add),
                reads=[("g_k0",), ("idf",), ("vt1",)], writes=[("g_kt",)])
            P.op("sp", lambda e: e.dma_start(out=swb_d[gt, :, OFF_K:OFF_K + 1920], in_=G_KT),
                 reads=[("g_kt",)], writes=[("swb", gt)], dma=True, slot="gen2")
            MM = G_MM.rearrange("q (m s p) -> q m s p", m=4, s=2)
            for s2 in range(2):
                P.op("act", lambda e, s2=s2: e.activation(
                    out=MM[:, :, s2, :], in_=G_PW[:, 8 * 256:9 * 256].rearrange("q (m p) -> q m p", m=4), func=AF.Copy,
                    scale=PMK[:, s2:s2 + 1]), reads=[KG, ("pmk",)], writes=[("g_mm",)])
            for m in range(4):
                P.op("pe", lambda e, m=m: e.transpose(out=PS[pb + 1][:, m * 128:(m + 1) * 128],
                                                      in_=G_MM[:, m * 128:(m + 1) * 128], identity=IDF[:, :]),
                     reads=[("g_mm",), ("idf",)], writes=kps(pb + 1))
            MU = G_MU.rearrange("q (w d k r) -> q w d k r", w=4, d=2, k=7)
            psm = PS[pb + 1][:, :].rearrange("q (m w s h) -> q m w s h", m=4, w=4, s=2)[:, :, :, :, 0]
            P.op("dve", lambda e: e.tensor_reduce(out=MU[:, :, :, 0, :].rearrange("q w d r -> q d r w"),
                                                  in_=psm.rearrange("q (d r) w s -> q d r w s", d=2),
                                                  axis=AX.X, op=ALU.add), reads=kps(pb + 1), writes=[("g_mu",)])
            MT = G_MT[:, 32:56].rearrange("q (i w d) -> q i w d", i=3, w=4)
            ME = "pool"
            for k in range(6):
                a = MU[:, :, :, k, 0]
                bq = MU[:, :, :, k, 1]
                P.op(ME, lambda e, a=a: e.tensor_tensor(out=MT[:, 0], in0=a, in1=a, op=ALU.mult),
                     reads=[("g_mu",)], writes=[("g_mt",)])
                P.op(ME, lambda e, bq=bq: e.tensor_tensor(out=MT[:, 1], in0=bq, in1=bq, op=ALU.mult),
                     reads=[("g_mu",), ("g_mt",)], writes=[("g_mt",)])
                P.op(ME, lambda e, k=k: e.tensor_tensor(out=MU[:, :, :, k + 1, 0], in0=MT[:, 0], in1=MT[:, 1],
                                                        op=ALU.subtract), reads=[("g_mt",)], writes=[("g_mu",)])
                P.op(ME, lambda e, a=a, bq=bq: e.tensor_tensor(out=MT[:, 2], in0=a, in1=bq, op=ALU.mult),
                     reads=[("g_mu",), ("g_mt",)], writes=[("g_mt",)])
                P.op(ME, lambda e, k=k: e.tensor_tensor(out=MU[:, :, :, k + 1, 1], in0=MT[:, 2], in1=MT[:, 2],
                                                        op=ALU.add), reads=[("g_mt",)], writes=[("g_mu",)])
            P.op("sp", lambda e: e.dma_start(out=swm_d[gt], in_=G_MU), reads=[("g_mu",)], writes=[("swm", gt)],
                 dma=True, slot="gen3")

        ssm_param_loads()

        def mod_block(j):
            wb = w_acquire(wb_ada[j])
            wv = wview(wb, "k16")
            b = big_bank() if j < 8 else 6 + (j % 2)
            for cc in range(4):
                for k in range(16):
                    P.op("pe", lambda e, wv=wv, cc=cc, k=k, b=b: e.matmul(
                        PS[b][:, cc * 2:cc * 2 + 2], lhsT=wv[:, k, cc * 128:(cc + 1) * 128], rhs=SCT[:, k, :],
                        start=(k == 0), stop=(k == 15)),
                        reads=[("wb", wb), ("sct", 0), ("sct", 1)], writes=kps(b))
            if j < 8:
                P.op("dve", lambda e, j=j, b=b: e.tensor_tensor(
                    out=MODT[:, j * 4:(j + 1) * 4, :], in0=PS[b][:, 0:8].rearrange("p (c v) -> p c v", v=2),
                    in1=VT2[:, j * 4:(j + 1) * 4].unsqueeze(2).to_broadcast([128, 4, 2]), op=ALU.add),
                    reads=kps(b) + [("vt2",)], writes=[("modt", j)])
            else:
                for cc in range(4):
                    P.op("act", lambda e, j=j, b=b, cc=cc: e.activation(
                        out=MODT[:, j * 4 + cc, :], in_=PS[b][:, cc * 2:cc * 2 + 2], func=AF.Identity,
                        bias=VT2[:, j * 4 + cc:j * 4 + cc + 1], scale=1.0),
                        reads=kps(b) + [("vt2",)], writes=[("modt", j)])

        def modk(j0, n=16):
            return [("modt", j) for j in range(j0 // 4, (j0 + n + 3) // 4)]

        def mod_finish_f():
            P.op("dve", lambda e: e.scalar_tensor_tensor(
                out=SCF[:, :, :], in0=MODT[:, 64:80, :], scalar=1.0, in1=GPF.unsqueeze(2).to_broadcast([128, 16, 2]),
                op0=ALU.add, op1=ALU.mult), reads=modk(64) + [("vt1",)], writes=[("scf",)])

        gen_on = not (debug or "").startswith(("mod", "ph1", "ph2", "ph3"))
        if gen_on:
            genA(0)
            genB(0)
        for j in range(8):
            mod_block(j)
            if gen_on:
                if j < 7:
                    genA(j + 1)
                    genB(j + 1)
                genC(j)
        P.op("dve", lambda e: e.scalar_tensor_tensor(
            out=SCM[:, :, :], in0=MODT[:, 16:32, :], scalar=1.0, in1=GPM.unsqueeze(2).to_broadcast([128, 16, 2]),
            op0=ALU.add, op1=ALU.mult), reads=modk(16) + [("vt1",)], writes=[("scm",)])
        modall = modk(0, 96)

        if debug == "gen":
            P.barrier()
            dbg_outs["swb"] = dout("dbg_swb", [8, 128, NSW], BF16)
            dbg_outs["swm"] = dout("dbg_swm", [8, 128, NMU])
            P.op("sp", lambda e: e.dma_start(out=dbg_outs["swb"], in_=swb_d), dma=True, slot="dbg")
            P.op("sp", lambda e: e.dma_start(out=dbg_outs["swm"], in_=swm_d), dma=True, slot="dbg2")
            P.emit()
            return nc
        if debug == "mod":
            dbg_outs["modt"] = dout("dbg_modt", [128, 192])
            P.op("sp", lambda e: e.dma_start(out=dbg_outs["modt"], in_=MODT[:, :, :].rearrange("p j v -> p (j v)")),
                 reads=modall, dma=True, slot="dbg")
            P.emit()
            return nc


        UA = BIG[:, 0:8192].rearrange("p (t n) -> p t n", t=8)
        VT = BIG[:, 8192:16384].rearrange("p (t n) -> p t n", t=8)
        VH = VHR[:, :].rearrange("p (t n) -> p t n", t=8)
        GV = XT[1][:, 0:1024]
        UTF = UT[:, :, :, :].rearrange("p g j c -> p (g j c)").bitcast(F32)
        GMROW = UTF[:, 0:2048]
        GFROW = UTF[:, 2048:4096]

        FREQ = sb("FREQ", [128, 512], F32)
        PIDX = sb("PIDX", [128, 4], F32)
        PIDI = sb("PIDI", [128, 4], I32)

        def build_freq():
            FI = XT[1][:, 0:512].bitcast(I32)
            P.op("pool", lambda e: e.iota(FI, pattern=[[1, 512]], base=0, channel_multiplier=0), writes=[("xt", 1)])
            P.op("dve", lambda e: e.tensor_copy(out=FREQ[:, :], in_=FI), reads=[("xt", 1)], writes=[("freq",)])
            P.op("act", lambda e: e.activation(out=FREQ[:, :], in_=FREQ[:, :], func=AF.Exp,
                                               scale=-math.log(10000.0) / 512.0), reads=[("freq",)], writes=[("freq",)])
            P.op("pool", lambda e: e.iota(PIDI[:, 0:1], pattern=[[0, 1]], base=0, channel_multiplier=1),
                 writes=[("pidi",)])
            P.op("dve", lambda e: e.tensor_single_scalar(out=PIDI[:, 1:2], in_=PIDI[:, 0:1], scalar=63,
                                                         op=ALU.bitwise_and), reads=[("pidi",)], writes=[("pidi1",)])
            P.op("dve", lambda e: e.tensor_single_scalar(out=PIDI[:, 2:3], in_=PIDI[:, 0:1], scalar=6,
                                                         op=ALU.arith_shift_right), reads=[("pidi",)], writes=[("pidi2",)])
            P.op("dve", lambda e: e.tensor_copy(out=PIDX[:, 0:2], in_=PIDI[:, 1:3]),
                 reads=[("pidi1",), ("pidi2",)], writes=[("pidx",)])

        def sincos_table(dst, dst_key, idx_col, add_const, tmpA, tmpB, tmp_key):
            TI = tmpB.bitcast(I32)
            P.op("dve", lambda e: e.tensor_scalar(out=tmpA, in0=FREQ[:, :], scalar1=idx_col, scalar2=None, op0=ALU.mult),
                 reads=[("freq",), ("pidx",)], writes=[tmp_key])
            if add_const != 0:
                P.op("dve", lambda e: e.scalar_tensor_tensor(out=tmpA, in0=FREQ[:, :], scalar=float(add_const), in1=tmpA,
                                                             op0=ALU.mult, op1=ALU.add),
                     reads=[("freq",), tmp_key], writes=[tmp_key])
            P.op("dve", lambda e: e.tensor_scalar(out=TI, in0=tmpA, scalar1=1.0 / TWO_PI, scalar2=None, op0=ALU.mult),
                 reads=[tmp_key], writes=[tmp_key])
            P.op("dve", lambda e: e.tensor_copy(out=tmpB, in_=TI), reads=[tmp_key], writes=[tmp_key])
            P.op("dve", lambda e: e.scalar_tensor_tensor(out=tmpA, in0=tmpB, scalar=-TWO_PI, in1=tmpA, op0=ALU.mult,
                                                         op1=ALU.add), reads=[tmp_key], writes=[tmp_key])
            P.op("dve", lambda e: e.tensor_scalar(out=tmpB, in0=tmpA, scalar1=math.pi, scalar2=-TWO_PI, op0=ALU.is_gt,
                                                  op1=ALU.mult), reads=[tmp_key], writes=[tmp_key])
            P.op("dve", lambda e: e.tensor_tensor(out=tmpA, in0=tmpA, in1=tmpB, op=ALU.add), reads=[tmp_key],
                 writes=[tmp_key])
            P.op("act", lambda e: e.activation(out=dst[:, 0:512], in_=tmpA, func=AF.Sin), reads=[tmp_key],
                 writes=[dst_key])
            P.op("dve", lambda e: e.tensor_scalar(out=tmpA, in0=tmpA, scalar1=math.pi / 2, scalar2=None, op0=ALU.add),
                 reads=[tmp_key, dst_key], writes=[tmp_key])
            P.op("dve", lambda e: e.tensor_scalar(out=tmpB, in0=tmpA, scalar1=math.pi, scalar2=-TWO_PI, op0=ALU.is_gt,
                                                  op1=ALU.mult), reads=[tmp_key], writes=[tmp_key])
            P.op("dve", lambda e: e.tensor_tensor(out=tmpA, in0=tmpA, in1=tmpB, op=ALU.add), reads=[tmp_key],
                 writes=[tmp_key])
            P.op("act", lambda e: e.activation(out=dst[:, 512:1024], in_=tmpA, func=AF.Sin), reads=[tmp_key],
                 writes=[dst_key])

        VHF = VHR[:, :].bitcast(F32)
        POSC = VHF[:, 0:1024]
        ROWT = VHF[:, 1024:2048]
        PTMPA = VHF[:, 2048:2560]
        PTMPB = VHF[:, 2560:3072]

        def load_x_tile(u, t, xb_i, src):
            xb = XT[xb_i]
            P.op("sp", lambda e: e.dma_start(out=xb[:, :], in_=src[t * 128:(t + 1) * 128, :]),
                 writes=[("xt", xb_i)], dma=True, slot=("xt", xb_i))
            if u == 0:
                sincos_table(ROWT, ("rowt",), PIDX[:, 1:2], 2 * t, PTMPA, PTMPB, ("ptmp",))
                P.op("dve", lambda e: e.tensor_tensor(out=xb[:, 0:1024], in0=xb[:, 0:1024], in1=ROWT, op=ALU.add),
                     reads=[("rowt",), ("xt", xb_i)], writes=[("xt", xb_i)])
                P.op("dve", lambda e: e.tensor_tensor(out=xb[:, 1024:2048], in0=xb[:, 1024:2048], in1=POSC, op=ALU.add),
                     reads=[("posc",), ("xt", xb_i)], writes=[("xt", xb_i)])

        def norm_and_transpose(src_ap, src_keys, t, sc_ap, sh_ap, v, bank_base, mkeys=(), stage=0):
            mkeys = list(mkeys)
            if stage in (0, 1):
                _norm_part(src_ap, src_keys, t)
            if stage in (0, 2):
                _tr_part(src_ap, src_keys, t, sc_ap, sh_ap, v, bank_base, mkeys)

        def _norm_part(src_ap, src_keys, t):
            for c4 in range(4):
                P.op("dve", lambda e, c4=c4: e.bn_stats(out=BNS[:, t, c4, :], in_=src_ap[:, c4 * 512:(c4 + 1) * 512]),
                     reads=src_keys, writes=[("bns", t, c4)])
            col = 16 + t * 8
            mv = smcol(col, 2)
            P.op("dve", lambda e: e.bn_aggr(out=mv, in_=BNS[:, t, :, :].rearrange("p a b -> p (a b)")),
                 reads=[("bns", t, c4) for c4 in range(4)], writes=[("sm", col)])
            P.op("dve", lambda e: e.scalar_tensor_tensor(out=smcol(col + 2), in0=mv[:, 0:1], scalar=mv[:, 0:1],
                                                         in1=mv[:, 1:2], op0=ALU.mult, op1=ALU.add),
                 reads=[("sm", col)], writes=[("sm", col + 2)])
            P.op("act", lambda e: e.activation(out=smcol(col + 3), in_=smcol(col + 2), func=AF.Sqrt,
                                               bias=EPSC[:, 0:1], scale=1.0),
                 reads=[("sm", col + 2), ("epsc",)], writes=[("sm", col + 3)])
            P.op("dve", lambda e: e.reciprocal(out=smcol(col + 4), in_=smcol(col + 3)),
                 reads=[("sm", col + 3)], writes=[("sm", col + 4)])
            P.op("act", lambda e: e.activation(out=src_ap, in_=src_ap, func=AF.Copy, scale=smcol(col + 4)),
                 reads=list(src_keys) + [("sm", col + 4)], writes=src_keys)

        def _tr_part(src_ap, src_keys, t, sc_ap, sh_ap, v, bank_base, mkeys):
            for kg in range(4):
                b = bank_base + kg
                for j in range(4):
                    k = kg * 4 + j
                    P.op("pe", lambda e, k=k, b=b, j=j: e.transpose(out=PS[b][:, j * 128:(j + 1) * 128],
                                                                    in_=src_ap[:, k * 128:(k + 1) * 128], identity=IDF[:, :]),
                         reads=list(src_keys) + [("idf",)], writes=[("ps", b)])
                for j in range(4):
                    k = kg * 4 + j
                    eng = ev_eng()
                    if eng == "act":
                        P.op("act", lambda e, k=k, b=b, j=j: e.activation(
                            out=HT[:, k, t * 128:(t + 1) * 128], in_=PS[b][:, j * 128:(j + 1) * 128], func=AF.Identity,
                            scale=sc_ap[:, k, v:v + 1], bias=sh_ap[:, k, v:v + 1]),
                            reads=[("ps", b), ("scm",), ("scf",)] + mkeys, writes=[("ht", t)])
                    else:
                        P.op("dve", lambda e, k=k, b=b, j=j: e.tensor_scalar(
                            out=HT[:, k, t * 128:(t + 1) * 128], in0=PS[b][:, j * 128:(j + 1) * 128],
                            scalar1=sc_ap[:, k, v:v + 1], scalar2=sh_ap[:, k, v:v + 1], op0=ALU.mult, op1=ALU.add),
                            reads=[("ps", b), ("scm",), ("scf",)] + mkeys, writes=[("ht", t)])

        P.barrier()
        build_freq()

        def unit(u):
            v = u
            W = wb_unit[u]
            xsrc = x_in[u]
            if u == 0:
                sincos_table(POSC, ("posc",), PIDX[:, 0:1], 0, PTMPA, PTMPB, ("ptmp",))
            def ph1(t, stage):
                if stage == 1:
                    load_x_tile(u, t, t % 2, xsrc)
                norm_and_transpose(XT[t % 2][:, :], [("xt", t % 2)], t, SCM, MODT[:, 0:16, :], v, (t % 2) * 4, modk(0),
                                   stage=stage)
            for t in range(8):
                ph1(t, 1)
                ph1(t, 2)
            P.barrier()
            if debug == "ph1" and u == int(debug_unit):
                return "stop"
            if "gv" not in _SKIP:
                P.op("sp", lambda e: e.dma_start(out=GV, in_=chunk_g_v.partition_broadcast(128)),
                     writes=[("gv",), ("xt", 1)], dma=True, slot="c0")
            for j in range(6):
                if "ub" in _SKIP and j >= 4:
                    continue
                wb = w_acquire(W["win"][j])
                wv = wview(wb, "k16")
                if j < 4:
                    for t in range(8):
                        b = big_bank()
                        for k in range(16):
                            P.op("pe", lambda e, k=k, t=t, b=b, wv=wv: e.matmul(
                                PS[b][:, :], lhsT=HT[:, k, t * 128:(t + 1) * 128], rhs=wv[:, k, :],
                                start=(k == 0), stop=(k == 15)), reads=[("wb", wb), ("ht", t)], writes=kps(b))
                        if j < 2:
                            P.op("act", lambda e, t=t, b=b, j=j: e.activation(
                                out=UA[:, t, j * 512:(j + 1) * 512], in_=PS[b][:, :], func=AF.Copy),
                                reads=kps(b), writes=[("ua", t, j)])
                        else:
                            jj = j - 2
                            P.op("act", lambda e, t=t, b=b, jj=jj: e.activation(
                                out=VT[:, t, jj * 512:(jj + 1) * 512], in_=PS[b][:, :], func=AF.Copy),
                                reads=kps(b), writes=[("vt", t, jj)])
                            P.op("dve", lambda e, t=t, jj=jj: e.bn_stats(out=BNS[:, t, jj, :],
                                                                         in_=VT[:, t, jj * 512:(jj + 1) * 512]),
                                 reads=[("vt", t, jj)], writes=[("bns", t, jj)])
                else:
                    jj = j - 4
                    for cc in range(4):
                        gt = jj * 4 + cc
                        for th in range(2):
                            b = big_bank()
                            for k in range(16):
                                P.op("pe", lambda e, k=k, th=th, b=b, wv=wv, cc=cc: e.matmul(
                                    PS[b][:, :], lhsT=wv[:, k, cc * 128:(cc + 1) * 128],
                                    rhs=HT[:, k, th * 512:(th + 1) * 512], start=(k == 0), stop=(k == 15)),
                                    reads=[("wb", wb)] + [("ht", th * 4 + i) for i in range(4)], writes=kps(b))
                            eng = ev_eng()
                            outap = UT[:, gt, :, th * 64:(th + 1) * 64].rearrange("p j c -> p c j")
                            inap = PS[b][:, :].rearrange("p (c j) -> p c j", j=TC)
                            if eng == "act":
                                P.op("act", lambda e, outap=outap, inap=inap: e.activation(out=outap, in_=inap, func=AF.Copy),
                                     reads=kps(b), writes=[("ut", gt, th)])
                            else:
                                P.op("dve", lambda e, outap=outap, inap=inap: e.tensor_copy(out=outap, in_=inap),
                                     reads=kps(b), writes=[("ut", gt, th)])
            if debug == "ph2" and u == int(debug_unit):
                return "stop"
            def cmix(t, stage):
                col = 16 + t * 8
                c0 = 128 + t * 8
                if stage == 1:
                    col = 16 + t * 8
                    mv = smcol(col, 2)
                    P.op("dve", lambda e, t=t, mv=mv: e.bn_aggr(out=mv, in_=BNS[:, t, 0:2, :].rearrange("p a b -> p (a b)")),
                         reads=[("bns", t, 0), ("bns", t, 1)], writes=[("sm", col)])
                    P.op("dve", lambda e, mv=mv, col=col: e.scalar_tensor_tensor(
                        out=smcol(col + 2), in0=mv[:, 0:1], scalar=mv[:, 0:1], in1=mv[:, 1:2], op0=ALU.mult, op1=ALU.add),
                        reads=[("sm", col)], writes=[("sm", col + 2)])
                    P.op("act", lambda e, col=col: e.activation(out=smcol(col + 3), in_=smcol(col + 2), func=AF.Sqrt,
                                                                bias=EPSC[:, 0:1], scale=1.0),
                         reads=[("sm", col + 2), ("epsc",)], writes=[("sm", col + 3)])
                    P.op("dve", lambda e, col=col: e.reciprocal(out=smcol(col + 4), in_=smcol(col + 3)),
                         reads=[("sm", col + 3)], writes=[("sm", col + 4)])
                    P.op("dve", lambda e, t=t, col=col: e.scalar_tensor_tensor(
                        out=VH[:, t, :], in0=VT[:, t, :], scalar=smcol(col + 4), in1=GV, op0=ALU.mult, op1=ALU.mult),
                        reads=[("vt", t, 0), ("vt", t, 1), ("sm", col + 4), ("gv",)], writes=[("vh", t)])
                    zb = 4 + (t % 2) * 2
                    for h in range(8):
                        b = zb + h // 4
                        P.op("pe", lambda e, t=t, h=h, b=b: e.matmul(
                            PS[b][:, (h % 4) * 128:(h % 4 + 1) * 128], lhsT=WST[:, h, :], rhs=VH[:, t, h * 128:(h + 1) * 128],
                            start=True, stop=True), reads=[("vh", t), ("wst", h)], writes=[("ps", b)])
                    for h in range(8):
                        b = zb + h // 4
                        P.op("dve", lambda e, t=t, h=h, b=b: e.scalar_tensor_tensor(
                            out=UA[:, t, h * 128:(h + 1) * 128], in0=PS[b][:, (h % 4) * 128:(h % 4 + 1) * 128],
                            scalar=BSP[:, h:h + 1], in1=UA[:, t, h * 128:(h + 1) * 128], op0=ALU.add, op1=ALU.mult),
                            reads=[("ps", b), ("bsp",), ("ua", t, h // 4)], writes=[("ua", t, h // 4)])
                    for c2i in range(2):
                        P.op("dve", lambda e, t=t, c2i=c2i: e.bn_stats(out=BNS[:, t, 2 + c2i, :],
                                                                       in_=UA[:, t, c2i * 512:(c2i + 1) * 512]),
                             reads=[("ua", t, c2i)], writes=[("bns", t, 2 + c2i)])
                    col2 = col + 5
                    P.op("dve", lambda e, t=t, col2=col2: e.bn_aggr(out=SMALL[:, 128 + t * 8:130 + t * 8],
                                                                   in_=BNS[:, t, 2:4, :].rearrange("p a b -> p (a b)")),
                         reads=[("bns", t, 2), ("bns", t, 3)], writes=[("sm", 128 + t * 8)])
                    c0 = 128 + t * 8
                    P.op("dve", lambda e, c0=c0: e.scalar_tensor_tensor(
                        out=smcol(c0 + 2), in0=smcol(c0), scalar=smcol(c0), in1=smcol(c0 + 1), op0=ALU.mult, op1=ALU.add),
                        reads=[("sm", c0)], writes=[("sm", c0 + 2)])
                    P.op("act", lambda e, c0=c0: e.activation(out=smcol(c0 + 3), in_=smcol(c0 + 2), func=AF.Sqrt,
                                                              bias=EPSC[:, 0:1], scale=1.0),
                         reads=[("sm", c0 + 2), ("epsc",)], writes=[("sm", c0 + 3)])
                    P.op("dve", lambda e, c0=c0: e.reciprocal(out=smcol(c0 + 4), in_=smcol(c0 + 3)),
                         reads=[("sm", c0 + 3)], writes=[("sm", c0 + 4)])
                    P.op("act", lambda e, t=t, c0=c0: e.activation(out=UA[:, t, :], in_=UA[:, t, :], func=AF.Copy,
                                                                   scale=smcol(c0 + 4)),
                         reads=[("ua", t, 0), ("ua", t, 1), ("sm", c0 + 4)], writes=[("ua", t, 0), ("ua", t, 1)])
                if stage == 2:
                    tb = (t % 2) * 2
                    for kg in range(2):
                        b = tb + kg
                        for j in range(4):
                            k = kg * 4 + j
                            P.op("pe", lambda e, t=t, k=k, b=b, j=j: e.transpose(
                                out=PS[b][:, j * 128:(j + 1) * 128], in_=UA[:, t, k * 128:(k + 1) * 128], identity=IDF[:, :]),
                                reads=[("ua", t, k // 4), ("idf",)], writes=[("ps", b)])
                        for j in range(4):
                            k = kg * 4 + j
                            eng = ev_eng()
                            if eng == "act":
                                P.op("act", lambda e, t=t, k=k, b=b, j=j: e.activation(
                                    out=HT[:, k, t * 128:(t + 1) * 128], in_=PS[b][:, j * 128:(j + 1) * 128], func=AF.Copy,
                                    scale=GOA[:, k:k + 1]), reads=[("ps", b), ("vt1",)], writes=[("ht", t)])
                            else:
                                P.op("dve", lambda e, t=t, k=k, b=b, j=j: e.tensor_scalar(
                                    out=HT[:, k, t * 128:(t + 1) * 128], in0=PS[b][:, j * 128:(j + 1) * 128],
                                    scalar1=GOA[:, k:k + 1], scalar2=None, op0=ALU.mult),
                                    reads=[("ps", b), ("vt1",)], writes=[("ht", t)])
            cmix(0, 1)
            for t in range(8):
                if t < 7:
                    cmix(t + 1, 1)
                cmix(t, 2)
            P.barrier()
            if debug == "ph3" and u == int(debug_unit):
                return "stop"
            SWBt = BIG[:, 0:2048].bitcast(BF16)
            SWCt = BIG[:, 2048:5312].bitcast(BF16)
            Bst = SWBt.rearrange("q (n d r m) -> q n d r m", n=8, d=2, r=2)
            Cst = SWCt[:, 0:4608].rearrange("q (n d r c) -> q n d r c", n=9, d=2, r=2)
            KTw = SWCt[:, 4608:6528].rearrange("q (l c) -> q l c", l=15)
            SBF = BIG[:, 5312:6368].bitcast(BF16).rearrange("q (d i c) -> q d i c", d=2, i=8)
            SMUS = [BIG[:, 6368 + i * 168:6480 + i * 168].rearrange("q (w d k r) -> q w d k r", w=4, d=2, k=7) for i in range(2)]
            NMUS = [BIG[:, 6480 + i * 168:6536 + i * 168].rearrange("q (w d k) -> q w d k", w=4, d=2) for i in range(2)]
            SMUF = [BIG[:, 6368 + i * 168:6480 + i * 168] for i in range(2)]
            HM = BIG[:, 6704:6728]
            T1 = BIG[:, 6728:7240]
            FSt = BIG[:, 7240:7752]
            FS = FSt.rearrange("q (s d r w) -> q s d r w", s=4, d=2, r=2)
            YG = BIG[:, 8192:16384].rearrange("q (g n) -> q g n", g=8)
            YGB = VHR[:, :].rearrange("q (g n) -> q g n", g=8)
            HT2 = HT[:, 8:16, :].rearrange("p k n -> p (k n)").bitcast(F32)
            SBUFS = [[XT[0][:, 0:1536], XT[1][:, 0:1536]], [HT2[:, 0:1536], HT2[:, 1536:3072]]]
            PTMP = [XT[0][:, 1536 + i * 128:1664 + i * 128] for i in range(4)] + \
                   [XT[1][:, 1536 + i * 128:1664 + i * 128] for i in range(4)] + \
                   [HT2[:, 3072 + i * 128:3200 + i * 128] for i in range(8)]
            nlev = 7 if u == 0 else 5

            def dat(buf, i, d, shift=0):
                if u == 0:
                    o = 64 if d == 0 else 0
                    return buf.rearrange("q (i c) -> q i c", i=8)[:, i, o + shift:o + 128 + shift]
                o = 16 if d == 0 else 0
                return buf.rearrange("q (i s c) -> q i s c", i=8, s=4)[:, i, :, o + shift:o + 32 + shift]

            def dat2(buf, w, d, shift=0):
                if u == 0:
                    o = 64 if d == 0 else 0
                    return buf.rearrange("q (i c) -> q i c", i=8)[:, 2 * w:2 * w + 2, o + shift:o + 128 + shift]
                o = 16 if d == 0 else 0
                return buf.rearrange("q (i c) -> q i c", i=32)[:, 8 * w:8 * w + 8, o + shift:o + 32 + shift]

            def dat_rows(buf, i0, i1, d):
                if u == 0:
                    o = 64 if d == 0 else 0
                    return buf.rearrange("q (i c) -> q i c", i=8)[:, i0:i1, o:o + 128]
                o = 16 if d == 0 else 0
                return buf.rearrange("q (i s c) -> q i s c", i=8, s=4)[:, i0:i1, :, o:o + 32]

            def ptv(i):
                return PTMP[i] if u == 0 else PTMP[i].rearrange("q (s c) -> q s c", s=4)

            def skey(d, si, w):
                return ("scan", d, si, w)

            P.op("pool", lambda e: e.memset(XT[0][:, :], 0.0), writes=[skey(0, 0, w) for w in range(4)] + [("xt", 0)])
            P.op("pool", lambda e: e.memset(XT[1][:, :], 0.0), writes=[skey(0, 1, w) for w in range(4)] + [("xt", 1)])
            P.op("pool", lambda e: e.memset(HT2[:, :], 0.0), writes=[skey(1, si_, w) for w in range(4) for si_ in range(2)])
            P.op("pool", lambda e: e.memset(BIG[:, 5312:6368], 0.0), writes=[("sbf", 0), ("sbf", 1)])

            def emit_V(gt):
                P.op("sp", lambda e: e.dma_start(out=SWBt[:, :], in_=swb_d[gt, :, OFF_B:OFF_B + 4096]), reads=[("swb", gt)],
                     writes=[("swB",)], dma=True, slot="swB")
                for d in range(2):
                    for w in range(4):
                        for ri in range(2):
                            col = (d * 2 + ri) * 128
                            for j in range(8):
                                n = 7 - j if d == 0 else j
                                P.op("pe", lambda e, w=w, ri=ri, j=j, n=n, d=d, col=col: e.matmul(
                                    PS[w][:, col:col + 128], lhsT=Bst[32 * w:32 * w + 32, n, d, ri, :],
                                    rhs=UT[32 * w:32 * w + 32, gt, j, :], start=(j == 0), stop=(j == 7),
                                    tile_position=(32 * w, 0)),
                                    reads=[("swB",), ("ut", gt, 0), ("ut", gt, 1)], writes=kps(w))

            def load_mu(gt):
                i = gt % 2
                P.op("sp", lambda e: e.dma_start(out=SMUF[i], in_=swm_d[gt]), reads=[("swm", gt)],
                     writes=[("smu", i)], dma=True, slot=("smu", i))
                P.op("act", lambda e: e.activation(out=NMUS[i], in_=SMUS[i][:, :, :, :, 1], func=AF.Copy, scale=-1.0),
                     reads=[("smu", i)], writes=[("nmu", i)])

            def evac_V(gt):
                for d in range(2):
                    for w in range(4):
                        if u == 0:
                            inap = PS[w][:, d * 256:(d + 1) * 256].rearrange("q (i c) -> q i c", i=2)
                        else:
                            inap = PS[w][:, d * 256:(d + 1) * 256].rearrange("q (i s c) -> q i s c", i=2, s=4)
                        outap = dat_rows(SBUFS[d][0], 2 * w, 2 * w + 2, d)
                        P.op("act", lambda e, outap=outap, inap=inap: e.activation(out=outap, in_=inap, func=AF.Copy),
                             reads=kps(w), writes=[skey(d, 0, w)])

            def scan(gt):
                SMU, NMU = SMUS[gt % 2], NMUS[gt % 2]
                mk = [("smu", gt % 2), ("nmu", gt % 2)]
                if u == 0:
                    for d in range(2):
                        h0r = H0T[:, (d * 2 + 0) * 32 + gt * 4:(d * 2 + 0) * 32 + gt * 4 + 4]
                        h0i = H0T[:, (d * 2 + 1) * 32 + gt * 4:(d * 2 + 1) * 32 + gt * 4 + 4]
                        mur, mui = SMU[:, :, d, 0, 0], SMU[:, :, d, 0, 1]
                        HMr, HMi, HMt = HM[:, d * 12:d * 12 + 4], HM[:, d * 12 + 4:d * 12 + 8], HM[:, d * 12 + 8:d * 12 + 12]
                        hk = [("hm", d)]
                        rk = [("hm", d), ("h0t",)] + mk
                        P.op("dve", lambda e, mui=mui, h0i=h0i, HMt=HMt: e.tensor_tensor(out=HMt, in0=mui, in1=h0i, op=ALU.mult), reads=rk, writes=hk)
                        P.op("dve", lambda e, mur=mur, h0r=h0r, HMr=HMr: e.tensor_tensor(out=HMr, in0=mur, in1=h0r, op=ALU.mult), reads=rk, writes=hk)
                        P.op("dve", lambda e, HMr=HMr, HMt=HMt: e.tensor_tensor(out=HMr, in0=HMr, in1=HMt, op=ALU.subtract), reads=rk, writes=hk)
                        P.op("dve", lambda e, mui=mui, h0r=h0r, HMt=HMt: e.tensor_tensor(out=HMt, in0=mui, in1=h0r, op=ALU.mult), reads=rk, writes=hk)
                        P.op("dve", lambda e, mur=mur, h0i=h0i, HMi=HMi: e.tensor_tensor(out=HMi, in0=mur, in1=h0i, op=ALU.mult), reads=rk, writes=hk)
                        P.op("dve", lambda e, HMi=HMi, HMt=HMt: e.tensor_tensor(out=HMi, in0=HMi, in1=HMt, op=ALU.add), reads=rk, writes=hk)
                        pos = 64 if d == 0 else 127
                        SA5 = SBUFS[d][0].rearrange("q (w r c) -> q w r c", w=4, r=2)
                        allsa = [skey(d, 0, w) for w in range(4)]
                        P.op("dve", lambda e, pos=pos, SA5=SA5, HMr=HMr: e.tensor_tensor(
                            out=SA5[:, :, 0, pos], in0=SA5[:, :, 0, pos], in1=HMr, op=ALU.add), reads=allsa + hk, writes=allsa)
                        P.op("dve", lambda e, pos=pos, SA5=SA5, HMi=HMi: e.tensor_tensor(
                            out=SA5[:, :, 1, pos], in0=SA5[:, :, 1, pos], in1=HMi, op=ALU.add), reads=allsa + hk, writes=allsa)
                si = 0
                for k in range(nlev):
                    for term in (0, 1, 3):
                        for d in range(2):
                            sft = (1 << k) * (-1 if d == 0 else 1)
                            src, dst = SBUFS[d][si], SBUFS[d][1 - si]
                            for w in range(4):
                                mur = SMU[:, w, d, k, 0:1]
                                mui = SMU[:, w, d, k, 1:2]
                                nmui = NMU[:, w, d, k:k + 1]
                                re, im = 2 * w, 2 * w + 1
                                rk = [skey(d, si, w), skey(d, 1 - si, w)] + mk
                                wk = [skey(d, 1 - si, w)]
                                if term == 0:
                                    o_, i0_, sc_, i1_ = dat2(dst, w, d), dat2(src, w, d, sft), mur, dat2(src, w, d)
                                elif term == 1:
                                    o_, i0_, sc_, i1_ = dat(dst, re, d), dat(src, im, d, sft), nmui, dat(dst, re, d)
                                else:
                                    o_, i0_, sc_, i1_ = dat(dst, im, d), dat(src, re, d, sft), mui, dat(dst, im, d)
                                P.op("dve", lambda e, o_=o_, i0_=i0_, sc_=sc_, i1_=i1_: e.scalar_tensor_tensor(
                                    out=o_, in0=i0_, scalar=sc_, in1=i1_, op0=ALU.mult, op1=ALU.add), reads=rk, writes=wk)
                    si = 1 - si
                for d in range(2):
                    fin = SBUFS[d][si]
                    fkeys = [skey(d, si, w) for w in range(4)]
                    c0 = 1 if d == 0 else 0
                    if u == 0:
                        P.op("act", lambda e, fin=fin, d=d, c0=c0: e.activation(
                            out=SBF[:, d, :, c0:c0 + 128], in_=dat_rows(fin, 0, 8, d), func=AF.Copy),
                            reads=fkeys, writes=[("sbf", d)])
                        hs = 0 if d == 0 else 128
                        P.op("act", lambda e, d=d, hs=hs, gt=gt: e.activation(
                            out=SBF[:, d, :, hs].rearrange("q (w r) -> q w r", w=4),
                            in_=H0T[:, d * 64:d * 64 + 64].rearrange("q (r w) -> q w r", r=2)[:, gt * 4:gt * 4 + 4, :],
                            func=AF.Copy), reads=[("h0t",)], writes=[("sbf", d)])
                    else:
                        P.op("act", lambda e, fin=fin, d=d, c0=c0: e.activation(
                            out=SBF[:, d, :, :].rearrange("q i (s c) -> q i s c", s=4)[:, :, :, c0:c0 + 32],
                            in_=dat_rows(fin, 0, 8, d), func=AF.Copy), reads=fkeys, writes=[("sbf", d)])
                        pos = 47 if d == 0 else 0
                        P.op("act", lambda e, fin=fin, d=d, pos=pos, gt=gt: e.activation(
                            out=FS[:, :, d, :, gt * 4:gt * 4 + 4].rearrange("q s r w -> q w r s"),
                            in_=fin.rearrange("q (w r s c) -> q w r s c", w=4, r=2, s=4)[:, :, :, :, pos], func=AF.Copy),
                            reads=fkeys, writes=[("fs",)])

            def firc(gt):
                yb = 4 + 2 * (gt % 2)
                P.op("sp", lambda e: e.dma_start(out=SWCt[:, :], in_=swb_d[gt, :, OFF_C:OFF_C + 6528]), reads=[("swb", gt)],
                     writes=[("swC",)], dma=True, slot="swC")
                for j in range(8):
                    bank = yb + j // 4
                    col = (j % 4) * 128
                    for i in range(8):
                        L = 0 if i == j else (j - i if i < j else 7 + (i - j))
                        P.op("pe", lambda e, bank=bank, col=col, L=L, i=i: e.matmul(
                            PS[bank][:, col:col + 128], lhsT=KTw[:, L, :], rhs=UT[:, gt, i, :], start=(i == 0), stop=False),
                            reads=[("swC",), ("ut", gt, 0), ("ut", gt, 1)], writes=kps(bank))
                    for w in range(4):
                        for d in range(2):
                            for ri in range(2):
                                n = j + 1 if d == 0 else 8 - j
                                idx = 2 * w + ri
                                c0 = 0 if d == 0 else 1
                                if u == 0:
                                    rhs = SBF[:, d, idx, c0:c0 + 128]
                                else:
                                    rhs = SBF[:, d, idx, :].rearrange("q (s c) -> q s c", s=4)[:, :, c0:c0 + 32]
                                last = (d == 1 and ri == 1)
                                P.op("pe", lambda e, bank=bank, col=col, w=w, d=d, ri=ri, n=n, rhs=rhs, last=last: e.matmul(
                                    PS[bank][32 * w:32 * w + 32, col:col + 128], lhsT=Cst[:, n, d, ri, 32 * w:32 * w + 32],
                                    rhs=rhs, start=False, stop=last, tile_position=(0, 32 * w)),
                                    reads=[("swC",), ("sbf", 0), ("sbf", 1)], writes=kps(bank))

            def yevac(gt):
                yb = 4 + 2 * (gt % 2)
                for jb in range(2):
                    bank = yb + jb
                    inap = PS[bank][:, :].rearrange("q (j c) -> q j c", j=4)
                    P.op("act", lambda e, jb=jb, inap=inap: e.activation(
                        out=YG[:, gt, :].rearrange("q (c j) -> q j c", j=TC)[:, jb * 4:jb * 4 + 4, :], in_=inap,
                        func=AF.Gelu_apprx_tanh), reads=kps(bank), writes=[("yg", gt)])
                    P.op("act", lambda e, jb=jb, inap=inap: e.activation(
                        out=YGB[:, gt, :].rearrange("q (c j) -> q j c", j=TC)[:, jb * 4:jb * 4 + 4, :], in_=inap,
                        func=AF.Gelu_apprx_tanh), reads=kps(bank), writes=[("ygb", gt)])

            emit_V(0)
            load_mu(0)
            evac_V(0)
            emit_V(1)
            for gt in range(8):
                scan(gt)
                if gt < 7:
                    load_mu(gt + 1)
                    evac_V(gt + 1)
                if u == 0:
                    for jb in (8 + 2 * gt, 9 + 2 * gt):
                        mod_block(jb)
                firc(gt)
                if gt < 6:
                    emit_V(gt + 2)
                yevac(gt)
            if u == 0:
                mod_finish_f()
            if u == 1:
                for sq in range(4):
                    b = hi_bank()
                    P.op("pe", lambda e, sq=sq, b=b: e.transpose(out=PS[b][:, 0:128], in_=FSt[:, sq * 128:(sq + 1) * 128],
                                                                 identity=IDF[:, :]), reads=[("fs",), ("idf",)], writes=kps(b))
                    P.op("dve", lambda e, sq=sq, b=b: e.tensor_copy(out=T1[:, sq * 128:(sq + 1) * 128], in_=PS[b][:, 0:128]),
                         reads=kps(b), writes=[("t1",)])
                    P.op("sp", lambda e, sq=sq: e.dma_start(
                        out=ns_out[sq].rearrange("d r (w t) p -> (d r w) (t p)", t=2), in_=T1[:, sq * 128:(sq + 1) * 128]),
                        reads=[("t1",)], dma=True, slot=("ns", sq))
            P.barrier()
            if debug == "ph4" and u == int(debug_unit):
                return "stop"
            SIGT = [XT[0][:, 0:512], XT[0][:, 512:1024]]
            SQT = [XT[0][:, 1024:1536], XT[0][:, 1536:2048], XT[1][:, 1024:1536], XT[1][:, 1536:2048]]
            pend = []

            def flush_pend(keep):
                while len(pend) > keep:
                    ti_, th_, o_ = pend.pop(0)
                    P.op("pe", lambda e, ti_=ti_, th_=th_, o_=o_: e.matmul(
                        PS[6 + th_][:, :], lhsT=ONESF[:, :], rhs=SQT[ti_], start=(o_ == 0), stop=(o_ == 7)),
                        reads=[("sqt", ti_), ("onesf",)], writes=kps(6 + th_))
            RB = XT[1][:, 0:1024]
            tcount = 0
            for j in range(2):
                wb = w_acquire(W["glu"][j])
                wv = wview(wb, "k8")
                for cc in range(4):
                    o = j * 4 + cc
                    for th in range(2):
                        b = big_bank()
                        for k in range(8):
                            P.op("pe", lambda e, k=k, b=b, wv=wv, cc=cc, th=th: e.matmul(
                                PS[b][:, :], lhsT=wv[:, k, cc * 128:(cc + 1) * 128], rhs=YGB[:, k, th * 512:(th + 1) * 512],
                                start=(k == 0), stop=(k == 7)), reads=[("wb", wb)] + [("ygb", g) for g in range(8)],
                                writes=kps(b))
                        ti = tcount % 2
                        qi = tcount % 4
                        tcount += 1
                        P.op("act", lambda e, b=b, ti=ti, o=o: e.activation(out=SIGT[ti], in_=PS[b][:, :], func=AF.Sigmoid,
                                                                          bias=BGL[:, o:o + 1], scale=1.0),
                             reads=kps(b) + [("vt1",)], writes=[("sigt", ti)])
                        ysl = YG[:, o, th * 512:(th + 1) * 512]
                        P.op("dve", lambda e, ysl=ysl, ti=ti: e.tensor_tensor(out=ysl, in0=ysl, in1=SIGT[ti], op=ALU.mult),
                             reads=[("sigt", ti), ("yg", o)], writes=[("yg", o)])
                        P.op("pool", lambda e, ysl=ysl, qi=qi: e.tensor_tensor(out=SQT[qi], in0=ysl, in1=ysl, op=ALU.mult),
                             reads=[("yg", o)], writes=[("sqt", qi)])
                        flush_pend(2)
                        pend.append((qi, th, o))
            flush_pend(0)
            for th in range(2):
                P.op("act", lambda e, th=th: e.activation(out=RB[:, th * 512:(th + 1) * 512], in_=PS[6 + th][:, :],
                                                          func=AF.Sqrt, bias=EPSC[:, 0:1], scale=1.0 / 1024.0),
                     reads=kps(6 + th) + [("epsc",)], writes=[("rb", th)])
                P.op("dve", lambda e, th=th: e.reciprocal(out=RB[:, th * 512:(th + 1) * 512], in_=RB[:, th * 512:(th + 1) * 512]),
                     reads=[("rb", th)], writes=[("rb", th)])
            for o in range(8):
                P.op("dve", lambda e, o=o: e.scalar_tensor_tensor(
                    out=HT[:, 8 + o, :], in0=YG[:, o, :], scalar=GOB[:, o:o + 1], in1=RB, op0=ALU.mult, op1=ALU.mult),
                    reads=[("yg", o), ("rb", 0), ("rb", 1), ("vt1",)], writes=[("ht", t) for t in range(8)])
            P.barrier()
            if debug == "ph5" and u == int(debug_unit):
                return "stop"
            MO = BIG[:, :].rearrange("p (t n) -> p t n", t=8)
            for j in range(4):
                wb = w_acquire(W["wout"][j])
                wv = wview(wb, "k16")
                for t in range(8):
                    b = big_bank()
                    for k in range(16):
                        P.op("pe", lambda e, k=k, t=t, b=b, wv=wv: e.matmul(
                            PS[b][:, :], lhsT=HT[:, k, t * 128:(t + 1) * 128], rhs=wv[:, k, :],
                            start=(k == 0), stop=(k == 15)), reads=[("wb", wb), ("ht", t)], writes=kps(b))
                    P.op("act", lambda e, t=t, b=b, j=j: e.activation(out=MO[:, t, j * 512:(j + 1) * 512], in_=PS[b][:, :],
                                                                      func=AF.Copy), reads=kps(b), writes=[("mo", t, j)])
                    P.op("dve", lambda e, t=t, j=j: e.bn_stats(out=BNS[:, t, j, :], in_=MO[:, t, j * 512:(j + 1) * 512]),
                         reads=[("mo", t, j)], writes=[("bns", t, j)])

            def gate_row(dst, gvec, j0, key):
                P.op("sp", lambda e: e.dma_start(out=dst, in_=gvec.partition_broadcast(128)), writes=[key], dma=True, slot="c0")
                GBT = [XT[1][:, 1024:1152], XT[1][:, 1152:1280]]
                for k in range(16):
                    gi = k % 2
                    b = 4 + (k // 4) % 4
                    P.op("dve", lambda e, gi=gi, k=k: e.tensor_copy(out=GBT[gi], in_=MODT[:, j0 + k, v:v + 1].to_broadcast([128, 128])),
                         reads=modk(j0), writes=[("gbt", gi), ("xt", 1)])
                    P.op("pe", lambda e, gi=gi, k=k, b=b: e.matmul(PS[b][:, (k % 4) * 128:(k % 4 + 1) * 128], lhsT=GBT[gi],
                                                                   rhs=IDF[:, :], start=True, stop=True),
                         reads=[("gbt", gi), ("idf",), ("xt", 1)], writes=kps(b))
                    if k % 4 == 3:
                        k4 = k // 4
                        P.op("dve", lambda e, b=b, k4=k4: e.tensor_tensor(
                            out=dst[:, k4 * 512:(k4 + 1) * 512], in0=PS[b][:, :], in1=dst[:, k4 * 512:(k4 + 1) * 512], op=ALU.mult),
                            reads=kps(b) + [key], writes=[key])

            gate_row(GMROW, g_post_mix, 32, ("gmrow",))
            if u == 0:
                sincos_table(POSC, ("posc",), PIDX[:, 0:1], 0, PTMPA, PTMPB, ("ptmp",))
            for t in range(8):
                mok = [("mo", t, j) for j in range(4)]
                P.op("dve", lambda e, t=t: e.tensor_tensor(out=MO[:, t, :], in0=MO[:, t, :], in1=GMROW, op=ALU.mult),
                     reads=mok + [("gmrow",)] + [("bns", t, c4) for c4 in range(4)], writes=mok)

            def res_s1(t):
                c0 = 80 + t * 8
                P.op("dve", lambda e: e.bn_aggr(out=SMALL[:, c0:c0 + 2], in_=BNS[:, t, :, :].rearrange("p a b -> p (a b)")),
                     reads=[("bns", t, c4) for c4 in range(4)], writes=[("sm", c0)])
                P.op("dve", lambda e: e.scalar_tensor_tensor(
                    out=smcol(c0 + 2), in0=smcol(c0), scalar=smcol(c0), in1=smcol(c0 + 1), op0=ALU.mult, op1=ALU.add),
                    reads=[("sm", c0)], writes=[("sm", c0 + 2)])
                P.op("act", lambda e: e.activation(out=smcol(c0 + 3), in_=smcol(c0 + 2), func=AF.Sqrt,
                                                   bias=EPSC[:, 0:1], scale=1.0),
                     reads=[("sm", c0 + 2), ("epsc",)], writes=[("sm", c0 + 3)])
                P.op("dve", lambda e: e.reciprocal(out=smcol(c0 + 4), in_=smcol(c0 + 3)),
                     reads=[("sm", c0 + 3)], writes=[("sm", c0 + 4)])
                xb_i = t % 2
                load_x_tile(u, t, xb_i, xsrc)
                mok = [("mo", t, j) for j in range(4)]
                P.op("dve", lambda e: e.scalar_tensor_tensor(
                    out=XT[xb_i][:, :], in0=MO[:, t, :], scalar=smcol(c0 + 4), in1=XT[xb_i][:, :], op0=ALU.mult, op1=ALU.add),
                    reads=mok + [("sm", c0 + 4), ("xt", xb_i)], writes=[("xt", xb_i)])
                P.op("pool", lambda e: e.dma_start(out=y_out[u][t * 128:(t + 1) * 128, :], in_=XT[xb_i][:, :]),
                     reads=[("xt", xb_i)], writes=[("ydram", t)], dma=True, slot=("yo", xb_i))
                for c4 in range(4):
                    P.op("dve", lambda e, c4=c4: e.bn_stats(out=BNS[:, t, c4, :], in_=XT[xb_i][:, c4 * 512:(c4 + 1) * 512]),
                         reads=[("xt", xb_i)], writes=[("bns", t, c4)])
                c1 = 144 + t * 8
                P.op("dve", lambda e: e.bn_aggr(out=SMALL[:, c1:c1 + 2], in_=BNS[:, t, :, :].rearrange("p a b -> p (a b)")),
                     reads=[("bns", t, c4) for c4 in range(4)], writes=[("sm", c1)])
                P.op("dve", lambda e: e.scalar_tensor_tensor(
                    out=smcol(c1 + 2), in0=smcol(c1), scalar=smcol(c1), in1=smcol(c1 + 1), op0=ALU.mult, op1=ALU.add),
                    reads=[("sm", c1)], writes=[("sm", c1 + 2)])
                P.op("act", lambda e: e.activation(out=smcol(c1 + 3), in_=smcol(c1 + 2), func=AF.Sqrt,
                                                   bias=EPSC[:, 0:1], scale=1.0),
                     reads=[("sm", c1 + 2), ("epsc",)], writes=[("sm", c1 + 3)])
                P.op("dve", lambda e: e.reciprocal(out=smcol(c1 + 4), in_=smcol(c1 + 3)),
                     reads=[("sm", c1 + 3)], writes=[("sm", c1 + 4)])
                P.op("act", lambda e: e.activation(out=MO[:, t, :], in_=XT[xb_i][:, :], func=AF.Copy, scale=smcol(c1 + 4)),
                     reads=[("xt", xb_i), ("sm", c1 + 4)], writes=mok)

            def res_s2(t):
                mok = [("mo", t, j) for j in range(4)]
                tb = (t % 2) * 2
                for kg in range(4):
                    b = tb + kg % 2
                    for jq in range(4):
                        k = kg * 4 + jq
                        P.op("pe", lambda e, k=k, b=b, jq=jq: e.transpose(
                            out=PS[b][:, jq * 128:(jq + 1) * 128], in_=MO[:, t, k * 128:(k + 1) * 128], identity=IDF[:, :]),
                            reads=mok + [("idf",)], writes=kps(b))
                    for jq in range(4):
                        k = kg * 4 + jq
                        eng = ev_eng()
                        if eng == "act":
                            P.op("act", lambda e, k=k, b=b, jq=jq: e.activation(
                                out=HT[:, k, t * 128:(t + 1) * 128], in_=PS[b][:, jq * 128:(jq + 1) * 128], func=AF.Identity,
                                scale=SCF[:, k, v:v + 1], bias=MODT[:, 48 + k, v:v + 1]),
                                reads=kps(b) + [("scf",)] + modk(48), writes=[("ht", t)])
                        else:
                            P.op("dve", lambda e, k=k, b=b, jq=jq: e.tensor_scalar(
                                out=HT[:, k, t * 128:(t + 1) * 128], in0=PS[b][:, jq * 128:(jq + 1) * 128],
                                scalar1=SCF[:, k, v:v + 1], scalar2=MODT[:, 48 + k, v:v + 1], op0=ALU.mult, op1=ALU.add),
                                reads=kps(b) + [("scf",)] + modk(48), writes=[("ht", t)])

            res_s1(0)
            for t in range(8):
                if t < 7:
                    res_s1(t + 1)
                res_s2(t)
            P.barrier()
            if debug == "ph6" and u == int(debug_unit):
                return "stop"
            Fv = BIG[:, :].rearrange("p (t n) -> p t n", t=8)
            AT = VHR[:, :].rearrange("p (a c n) -> p a c n", a=2, c=4)
            RT = [XT[0][:, 0:512], XT[0][:, 512:1024], XT[0][:, 1024:1536], XT[0][:, 1536:2048]]
            rcount = 0
            gate_row(GFROW, g_post_ffn, 80, ("gfrow",))
            for g in range(16):
                i1, i2 = W["ffn"][g]
                wb1 = w_acquire(i1)
                wv1 = wview(wb1, "k16")
                ab = g % 2
                for cc in range(4):
                    for th in range(2):
                        b = big_bank()
                        for k in range(16):
                            P.op("pe", lambda e, k=k, b=b, wv1=wv1, cc=cc, th=th: e.matmul(
                                PS[b][:, :], lhsT=wv1[:, k, cc * 128:(cc + 1) * 128], rhs=HT[:, k, th * 512:(th + 1) * 512],
                                start=(k == 0), stop=(k == 15)),
                                reads=[("wb", wb1)] + [("ht", th * 4 + i) for i in range(4)], writes=kps(b))
                        ri_ = rcount % 4
                        rcount += 1
                        P.op("act", lambda e, b=b, ri_=ri_: e.activation(out=RT[ri_], in_=PS[b][:, :], func=AF.Relu),
                             reads=kps(b), writes=[("rt", ri_)])
                        P.op("pool" if rcount % 2 else "dve", lambda e, ri_=ri_, ab=ab, cc=cc, th=th: e.tensor_tensor(
                            out=AT[:, ab, cc, th * 512:(th + 1) * 512], in0=RT[ri_], in1=RT[ri_], op=ALU.mult),
                            reads=[("rt", ri_)], writes=[("at", ab, cc, th)])
                wb2 = w_acquire(i2)
                wv2 = wview(wb2, "c4")
                for t in range(8):
                    for nh in range(4):
                        b = hi_bank()
                        for cc in range(4):
                            P.op("pe", lambda e, t=t, nh=nh, b=b, cc=cc, wv2=wv2, ab=ab: e.matmul(
                                PS[b][:, :], lhsT=AT[:, ab, cc, t * 128:(t + 1) * 128], rhs=wv2[:, cc, nh * 512:(nh + 1) * 512],
                                start=(cc == 0), stop=(cc == 3)),
                                reads=[("wb", wb2)] + [("at", ab, c_, t // 4) for c_ in range(4)], writes=kps(b))
                        fsl = Fv[:, t, nh * 512:(nh + 1) * 512]
                        if g == 0:
                            P.op("act", lambda e, b=b, fsl=fsl: e.activation(out=fsl, in_=PS[b][:, :], func=AF.Copy),
                                 reads=kps(b), writes=[("f", t, nh)])
                        else:
                            P.op("dve", lambda e, b=b, fsl=fsl: e.tensor_tensor(out=fsl, in0=PS[b][:, :], in1=fsl, op=ALU.add),
                                 reads=kps(b) + [("f", t, nh)], writes=[("f", t, nh)])
            P.barrier()
            for t in range(8):
                fk = [("f", t, nh) for nh in range(4)]
                for c4 in range(4):
                    P.op("dve", lambda e, t=t, c4=c4: e.bn_stats(out=BNS[:, t, c4, :], in_=Fv[:, t, c4 * 512:(c4 + 1) * 512]),
                         reads=fk, writes=[("bns", t, c4)])
                c2_ = 208 + t * 5
                P.op("dve", lambda e, t=t, c2_=c2_: e.bn_aggr(out=SMALL[:, c2_:c2_ + 2], in_=BNS[:, t, :, :].rearrange("p a b -> p (a b)")),
                     reads=[("bns", t, c4) for c4 in range(4)], writes=[("sm", c2_)])
                P.op("dve", lambda e, c2_=c2_: e.scalar_tensor_tensor(
                    out=smcol(c2_ + 2), in0=smcol(c2_), scalar=smcol(c2_), in1=smcol(c2_ + 1), op0=ALU.mult, op1=ALU.add),
                    reads=[("sm", c2_)], writes=[("sm", c2_ + 2)])
                P.op("act", lambda e, c2_=c2_: e.activation(out=smcol(c2_ + 3), in_=smcol(c2_ + 2), func=AF.Sqrt,
                                                            bias=EPSC[:, 0:1], scale=1.0),
                     reads=[("sm", c2_ + 2), ("epsc",)], writes=[("sm", c2_ + 3)])
                P.op("dve", lambda e, c2_=c2_: e.reciprocal(out=smcol(c2_ + 4), in_=smcol(c2_ + 3)),
                     reads=[("sm", c2_ + 3)], writes=[("sm", c2_ + 4)])
                xb_i = t % 2
                FT = VHF[:, xb_i * 2048:(xb_i + 1) * 2048]
                P.op("sp", lambda e, t=t, FT=FT: e.dma_start(out=FT, in_=y_out[u][t * 128:(t + 1) * 128, :]),
                     reads=[("ydram", t)], writes=[("ft", xb_i)], dma=True, slot=("ft", xb_i))
                P.op("dve", lambda e, t=t, c2_=c2_: e.scalar_tensor_tensor(
                    out=Fv[:, t, :], in0=Fv[:, t, :], scalar=smcol(c2_ + 4), in1=GFROW, op0=ALU.mult, op1=ALU.mult),
                    reads=fk + [("sm", c2_ + 4), ("gfrow",)], writes=fk)
                P.op("pool", lambda e, t=t, FT=FT: e.tensor_tensor(out=FT, in0=FT, in1=Fv[:, t, :], op=ALU.add),
                     reads=fk + [("ft", xb_i)], writes=[("ft", xb_i)])
                P.op("pool", lambda e, t=t, FT=FT: e.dma_start(out=y_out[u][t * 128:(t + 1) * 128, :], in_=FT),
                     reads=[("ft", xb_i)], writes=[("ydram", t)], dma=True, slot=("yo2", xb_i))
            if u == 1:
                P.barrier()
            return None

        debug_unit = 0
        if debug is not None and ":" in debug:
            debug, debug_unit = debug.split(":")
        stop = None
        for u in range(2):
            stop = unit(u)
            if stop:
                break
        if debug == "ph4":
            dbg_outs["big"] = dout("dbg_big", [128, 8192])
            P.op("sp", lambda e: e.dma_start(out=dbg_outs["big"], in_=BIG[:, 8192:16384]), dma=True, slot="dbg2")
            P.emit()
            return nc
        if debug in ("ph1", "ph2", "ph3"):
            dbg_outs["ht"] = dout("dbg_ht", [128, 16 * 1024], BF16)
            P.barrier()
            P.op("sp", lambda e: e.dma_start(out=dbg_outs["ht"], in_=HT[:, :, :].rearrange("p k n -> p (k n)")),
                 dma=True, slot="dbg")
            if debug != "ph1":
                dbg_outs["big"] = dout("dbg_big", [128, 16384])
                P.op("sp", lambda e: e.dma_start(out=dbg_outs["big"], in_=BIG[:, :]), dma=True, slot="dbg2")
                dbg_outs["ut"] = dout("dbg_ut", [128, 8192], BF16)
                P.op("sp", lambda e: e.dma_start(out=dbg_outs["ut"], in_=UT[:, :, :, :].rearrange("p g j c -> p (g j c)")),
                     dma=True, slot="dbg3")
            P.emit()
            return nc
        if debug in ("ph5", "ph6"):
            dbg_outs["ht"] = dout("dbg_ht", [128, 16 * 1024], BF16)
            P.op("sp", lambda e: e.dma_start(out=dbg_outs["ht"], in_=HT[:, :, :].rearrange("p k n -> p (k n)")),
                 dma=True, slot="dbg")
        P.emit()
    return nc


def _consts():
    ident = np.eye(128, dtype=np.float32)
    q = np.arange(128)
    pm = np.stack([((q // 16) % 2 == 0), ((q // 16) % 2 == 1)], axis=1).astype(np.float32)
    pairm = (q[:, None] // 32 == np.arange(4)[None, :]).astype(np.float32)
    sel = np.zeros((64, 8, 128), np.float32)
    for gt in range(8):
        for qq in range(128):
            sel[gt * 8 + qq // 16, gt, qq] = 1.0
    return dict(k_ident=ident, k_pm=pm, k_pairm=pairm, k_sel=sel)


def make_in_maps(inp):
    f = lambda a: np.ascontiguousarray(np.asarray(a, dtype=np.float32))
    shared = {}
    for k in ["w_ada", "b_ada", "g_pre_mix", "w_in", "chunk_w_s", "chunk_b_s", "chunk_g_v", "ssm_lam_re", "ssm_lam_im",
              "ssm_log_dt", "ssm_b_re", "ssm_b_im", "ssm_c_re", "ssm_c_im", "ssm_d", "w_glu", "b_glu", "g_out_a",
              "g_out_b", "w_out", "g_post_mix", "g_pre_ffn", "w_ff1", "w_ff2", "g_post_ffn"]:
        shared[k] = f(inp[k])[0]
    shared.update(_consts())
    xs = f(inp["x_sample"]); xp = f(inp["x_prompt"]); stt = f(inp["state_ssm"]); c = f(inp["c"]); cc = f(inp["c_ctx"])
    maps = []
    for i in range(NCORES):
        m = dict(shared)
        m["xs"] = xs[i]
        m["xp"] = np.ascontiguousarray(xp[4 * i:4 * i + 4].reshape(1024, 2048))
        m["st0"] = np.ascontiguousarray(stt[i, 0])
        m["c2"] = np.ascontiguousarray(np.stack([c[i], cc]))
        maps.append(m)
    return maps


_NC_CACHE = {}


def kernel(**inputs):
    if "nc" not in _NC_CACHE:
        _NC_CACHE["nc"] = build()
    nc = _NC_CACHE["nc"]
    maps = make_in_maps(inputs)
    res = run_bass_kernel_spmd(nc, maps, core_ids=list(range(NCORES)))
    r = res.results
    y_sample = np.stack([r[i]["ys"] for i in range(NCORES)]).astype(np.float32)
    y_prompt = np.concatenate([r[i]["yp"].reshape(4, 256, 2048) for i in range(NCORES)]).astype(np.float32)
    ns = np.concatenate([r[i]["ns"].reshape(4, 1, 2, 2, 64, 64) for i in range(NCORES)]).astype(np.float32)
    return (y_prompt, y_sample, ns)
```
